# Optimizing a Trainium2 kernel written in Bass

```python
import math
import jax, jax.numpy as jnp
from jax import lax
import numpy as np

D_MODEL = 1024
BATCH = 4
SEQ = 4096
DEPTH = 4

N_HEADS = 8
HEAD_DIM = 64
ATTN_WIDTH = N_HEADS * HEAD_DIM
ROT_DIM = HEAD_DIM // 4
ROPE_THETA = 500000.0
MOBA_BLOCK = 256
MOBA_TOPK = 3
Q_CHUNK = 32
CONV_WIDTH = D_MODEL // 2
CONV_KERNEL = 31
D_FF = 4 * D_MODEL
IN_COLS = 3 * ATTN_WIDTH + 2 * CONV_WIDTH + 2 * D_MODEL
DN_ALPHA = (2 * DEPTH) ** 0.25
DN_BETA = (8 * DEPTH) ** -0.25
ADA_SCALE = 0.1
LN_EPS = 1e-5
NEG_INF = -1e30

kernel_name = "hybrid_moba_conformer_gated_deepnorm"


def layer_norm(x, g, b):
    xf = x.astype(jnp.float32)
    mu = jnp.mean(xf, -1, keepdims=True)
    var = jnp.mean(jnp.square(xf - mu), -1, keepdims=True)
    return ((xf - mu) * lax.rsqrt(var + LN_EPS)).astype(x.dtype) * g + b


def partial_rope(x, cos, sin):
    half = ROT_DIM // 2
    x1, x2, rest = x[..., :half], x[..., half:ROT_DIM], x[..., ROT_DIM:]
    return jnp.concatenate([x1 * cos - x2 * sin, x2 * cos + x1 * sin, rest], -1)


def moba_attention(q, k, v):
    B, H, S, hd = q.shape
    n_blk = -(-S // MOBA_BLOCK)
    pad = n_blk * MOBA_BLOCK - S
    kp = jnp.pad(k, ((0, 0), (0, 0), (0, pad), (0, 0)))
    vp = jnp.pad(v, ((0, 0), (0, 0), (0, pad), (0, 0)))
    kb = kp.reshape(B, H, n_blk, MOBA_BLOCK, hd)
    vb = vp.reshape(B, H, n_blk, MOBA_BLOCK, hd)
    k_mean = jnp.mean(kb.astype(jnp.float32), axis=3).astype(q.dtype)
    topk = min(MOBA_TOPK, n_blk)
    scale = hd ** -0.5
    n_chunks = S // Q_CHUNK
    q_chunks = jnp.moveaxis(q.reshape(B, H, n_chunks, Q_CHUNK, hd), 2, 0)
    b_idx = jnp.arange(B)[:, None, None, None]
    h_idx = jnp.arange(H)[None, :, None, None]
    blk_ids = jnp.arange(n_blk)

    def attend_chunk(args):
        ci, qc = args
        q_pos = ci * Q_CHUNK + jnp.arange(Q_CHUNK)
        own = (ci * Q_CHUNK) // MOBA_BLOCK
        gate = jnp.einsum('bhqd,bhnd->bhqn', qc, k_mean, preferred_element_type=jnp.float32)
        gate = jnp.where(blk_ids < own, gate, NEG_INF)
        _, sel = lax.top_k(gate, topk)
        sel_ok = jnp.arange(topk) < own
        k_sel = kb[b_idx, h_idx, sel]
        v_sel = vb[b_idx, h_idx, sel]
        s_sel = jnp.einsum('bhqd,bhqjtd->bhqjt', qc, k_sel, preferred_element_type=jnp.float32) * scale
        s_sel = jnp.where(sel_ok[:, None], s_sel, NEG_INF)
        k_own = lax.dynamic_slice_in_dim(kp, own * MOBA_BLOCK, MOBA_BLOCK, axis=2)
        v_own = lax.dynamic_slice_in_dim(vp, own * MOBA_BLOCK, MOBA_BLOCK, axis=2)
        s_own = jnp.einsum('bhqd,bhtd->bhqt', qc, k_own, preferred_element_type=jnp.float32) * scale
        k_pos = own * MOBA_BLOCK + jnp.arange(MOBA_BLOCK)
        s_own = jnp.where(k_pos[None, :] <= q_pos[:, None], s_own, NEG_INF)
        logits = jnp.concatenate([s_sel.reshape(B, H, Q_CHUNK, topk * MOBA_BLOCK), s_own], -1)
        p = jax.nn.softmax(logits, axis=-1).astype(v.dtype)
        p_sel = p[..., :topk * MOBA_BLOCK].reshape(B, H, Q_CHUNK, topk, MOBA_BLOCK)
        p_own = p[..., topk * MOBA_BLOCK:]
        return (jnp.einsum('bhqjt,bhqjtd->bhqd', p_sel, v_sel)
                + jnp.einsum('bhqt,bhtd->bhqd', p_own, v_own))

    out = lax.map(attend_chunk, (jnp.arange(n_chunks), q_chunks))
    return jnp.moveaxis(out, 0, 2).reshape(B, H, S, hd)


def conformer_conv(glu_in, dw_w, dw_b, ln_g, ln_b, w_pw):
    a, g = jnp.split(glu_in, 2, axis=-1)
    h = a * jax.nn.sigmoid(g)
    h = lax.conv_general_dilated(h, dw_w[:, None, :], window_strides=(1,),
                                 padding=[(CONV_KERNEL - 1, 0)],
                                 dimension_numbers=('NWC', 'WIO', 'NWC'),
                                 feature_group_count=CONV_WIDTH) + dw_b
    h = jax.nn.silu(layer_norm(h, ln_g, ln_b))
    return h @ w_pw


def mixer(u, cos, sin, w_in, w_attn_br, dw_w, dw_b, cln_g, cln_b, w_conv_pw, w_out):
    B, S, _ = u.shape
    proj = u @ w_in
    splits = [ATTN_WIDTH, 2 * ATTN_WIDTH, 3 * ATTN_WIDTH,
              3 * ATTN_WIDTH + 2 * CONV_WIDTH, 3 * ATTN_WIDTH + 2 * CONV_WIDTH + D_MODEL]
    q, k, v, glu_in, g_attn, g_conv = jnp.split(proj, splits, axis=-1)
    q = partial_rope(q.reshape(B, S, N_HEADS, HEAD_DIM), cos, sin).transpose(0, 2, 1, 3)
    k = partial_rope(k.reshape(B, S, N_HEADS, HEAD_DIM), cos, sin).transpose(0, 2, 1, 3)
    v = v.reshape(B, S, N_HEADS, HEAD_DIM).transpose(0, 2, 1, 3)
    attn = moba_attention(q, k, v).transpose(0, 2, 1, 3).reshape(B, S, ATTN_WIDTH)
    y_attn = attn @ w_attn_br
    y_conv = conformer_conv(glu_in, dw_w, dw_b, cln_g, cln_b, w_conv_pw)
    merged = jax.nn.sigmoid(g_attn) * y_attn + jax.nn.sigmoid(g_conv) * y_conv
    return merged @ w_out


def setup_inputs(seed: int = 0) -> dict:
    key = jax.random.key(seed)
    ks = jax.random.split(key, 20)
    f32 = jnp.float32
    L, D, A, C = DEPTH, D_MODEL, ATTN_WIDTH, CONV_WIDTH
    nrm = lambda k, shape, s: jax.random.normal(k, shape, f32) * s
    return {
        "x": jax.random.normal(ks[0], (BATCH, SEQ, D), f32),
        "c": jax.random.normal(ks[1], (BATCH, D), f32),
        "positions": jnp.broadcast_to(jnp.arange(SEQ, dtype=jnp.int32), (BATCH, SEQ)),
        "w_in": nrm(ks[2], (L, D, IN_COLS), D ** -0.5),
        "w_attn_br": nrm(ks[3], (L, A, D), A ** -0.5),
        "conv_dw_w": nrm(ks[4], (L, CONV_KERNEL, C), CONV_KERNEL ** -0.5),
        "conv_dw_b": nrm(ks[5], (L, C), 0.02),
        "conv_ln_g": 1.0 + nrm(ks[6], (L, C), 0.02),
        "conv_ln_b": nrm(ks[7], (L, C), 0.02),
        "w_conv_pw": nrm(ks[8], (L, C, D), C ** -0.5),
        "w_out": nrm(ks[9], (L, D, D), DN_BETA * D ** -0.5),
        "w_ada": nrm(ks[10], (L, D, 6 * D), ADA_SCALE * D ** -0.5),
        "b_ada": nrm(ks[11], (L, 6 * D), 0.02),
        "ln_mix_g": 1.0 + nrm(ks[12], (L, D), 0.02),
        "ln_mix_b": nrm(ks[13], (L, D), 0.02),
        "w_up": nrm(ks[14], (L, D, D_FF), D ** -0.5),
        "w_down": nrm(ks[15], (L, D_FF, D), DN_BETA * D_FF ** -0.5),
        "ln_ffn_g": 1.0 + nrm(ks[16], (L, D), 0.02),
        "ln_ffn_b": nrm(ks[17], (L, D), 0.02),
    }


def reference(x, c, positions, w_in, w_attn_br, conv_dw_w, conv_dw_b, conv_ln_g, conv_ln_b,
              w_conv_pw, w_out, w_ada, b_ada, ln_mix_g, ln_mix_b, w_up, w_down,
              ln_ffn_g, ln_ffn_b):
    inv_freq = ROPE_THETA ** (-jnp.arange(0, ROT_DIM, 2, dtype=jnp.float32) / ROT_DIM)
    ang = positions.astype(jnp.float32)[..., None] * inv_freq
    cos = jnp.cos(ang)[:, :, None, :].astype(x.dtype)
    sin = jnp.sin(ang)[:, :, None, :].astype(x.dtype)
    c_act = jax.nn.silu(c)
    for l in range(DEPTH):
        ada = c_act @ w_ada[l] + b_ada[l]
        sh1, sc1, gt1, sh2, sc2, gt2 = [t[:, None, :] for t in jnp.split(ada, 6, axis=-1)]
        u = x * (1.0 + sc1) + sh1
        y = mixer(u, cos, sin, w_in[l], w_attn_br[l], conv_dw_w[l], conv_dw_b[l],
                  conv_ln_g[l], conv_ln_b[l], w_conv_pw[l], w_out[l])
        x = layer_norm(DN_ALPHA * x + (1.0 + gt1) * y, ln_mix_g[l], ln_mix_b[l])
        u = x * (1.0 + sc2) + sh2
        h = jnp.square(jax.nn.relu(u @ w_up[l])) @ w_down[l]
        x = layer_norm(DN_ALPHA * x + (1.0 + gt2) * h, ln_ffn_g[l], ln_ffn_b[l])
    return x
```

```python
import contextlib
import math
import numpy as np
import concourse.bass as bass
import concourse.mybir as mybir
from concourse.bass_utils import run_bass_kernel_spmd

F32 = mybir.dt.float32
BF16 = mybir.dt.bfloat16
I32 = mybir.dt.int32
AF = mybir.ActivationFunctionType
ALU = mybir.AluOpType
AX = mybir.AxisListType

D = 1024
S = 4096
B = 4
L = 4
H = 8
HD = 64
CW = 512
KC = 31
DFF = 4096
HALF = 2048
ALPHA = (2 * L) ** 0.25
EPS = 1e-5
BIG = 30000.0
NPV = 28 + 48 + 12


class _Op:
    __slots__ = ("eng", "fn", "deps", "is_dma", "idx", "sig", "sem", "semval", "prev_slot_val")


class Phase:
    NDMA = 12

    def __init__(self, nc, name):
        self.nc, self.name = nc, name
        self.ops, self.last_w, self.readers = [], {}, {}

    def op(self, eng, fn, reads=(), writes=(), dma=False):
        o = _Op()
        o.eng, o.fn, o.is_dma = eng, fn, dma
        deps = set()
        for k in reads:
            w = self.last_w.get(k)
            if w is not None:
                deps.add(w)
        for k in writes:
            w = self.last_w.get(k)
            if w is not None:
                deps.add(w)
            deps.update(self.readers.get(k, ()))
        o.idx = len(self.ops)
        deps.discard(o.idx)
        o.deps = deps
        self.ops.append(o)
        for k in reads:
            self.readers.setdefault(k, []).append(o.idx)
        for k in writes:
            self.last_w[k] = o.idx
            self.readers[k] = []
        return o.idx

    def dma(self, eng, out, in_, reads=(), writes=(), **kw):
        return self.op(eng, lambda e: e.dma_start(out=out, in_=in_, **kw), reads, writes, dma=True)

    def emit(self):
        for suf, n in TRUNC.items():
            if self.name.endswith(suf):
                print("phase", self.name, "ops", len(self.ops), "-> trunc", n)
                self.ops = self.ops[:n]
        nc, ops = self.nc, self.ops
        engs = ["pe", "act", "dve", "pool", "sp"]
        streams = {e: [o for o in ops if o.eng == e] for e in engs}

        def skip(p, o):
            return p.eng == "pe" and o.eng == "pe" and not p.is_dma and not o.is_dma

        for o in ops:
            o.sig = o.is_dma
        for o in ops:
            for d in o.deps:
                if not skip(ops[d], o):
                    ops[d].sig = True
        if id(nc) not in _SEMS:
            esem = {e: nc.semaphore(f"trk_{e}").__enter__() for e in engs}
            dsem = {e: [nc.semaphore(f"trk_d{e}{i}").__enter__() for i in range(self.NDMA)] for e in ("sp", "pool")}
            _SEMS[id(nc)] = (esem, dsem, {e: [0] * self.NDMA for e in dsem}, {e: 0 for e in dsem})
        esem, dsem, dcount, dnext = _SEMS[id(nc)]
        with nc.Block() as b0:
            def clr(e):
                for s_ in list(esem.values()):
                    e.sem_clear(s_)
            b0.gpsimd(clr)
        if True:
            ecount = {e: 0 for e in engs}
            dstart = {e: list(dcount[e]) for e in dsem}
            for o in ops:
                if o.is_dma:
                    s = dnext[o.eng] % self.NDMA
                    dnext[o.eng] += 1
                    o.prev_slot_val = dcount[o.eng][s]
                    dcount[o.eng][s] += 16
                    o.sem, o.semval = dsem[o.eng][s], dcount[o.eng][s]
                elif o.sig:
                    ecount[o.eng] += 1
                    o.sem, o.semval = esem[o.eng], ecount[o.eng]
                else:
                    o.sem, o.semval = None, 0
            with nc.Block() as blk:
                def run(e_name):
                    def body(e):
                        waited = {}
                        for o in streams[e_name]:
                            need = {}
                            for d in o.deps:
                                p = ops[d]
                                if skip(p, o):
                                    continue
                                key = id(p.sem)
                                if key not in need or need[key][1] < p.semval:
                                    need[key] = (p.sem, p.semval)
                            if o.is_dma and o.prev_slot_val > 0:
                                key = id(o.sem)
                                if key not in need or need[key][1] < o.prev_slot_val:
                                    need[key] = (o.sem, o.prev_slot_val)
                            for key, (s, v) in need.items():
                                if waited.get(key, 0) >= v:
                                    continue
                                e.wait_ge(s, v)
                                waited[key] = v
                            ins = o.fn(e)
                            if o.is_dma:
                                ins.then_inc(o.sem, 16)
                            elif o.sig:
                                ins.then_inc(o.sem, 1)
                        if e_name in dsem:
                            for i, s in enumerate(dsem[e_name]):
                                v = dcount[e_name][i]
                                if v > dstart[e_name][i] and waited.get(id(s), 0) < v:
                                    e.wait_ge(s, v)
                    return body
                for e_name, reg in (("pe", blk.tensor), ("act", blk.scalar), ("dve", blk.vector),
                                    ("pool", blk.gpsimd), ("sp", blk.sync)):
                    if streams[e_name]:
                        reg(run(e_name))
        if STOP[0] is not None and self.name.endswith(STOP[0]):
            raise StopBuild()


_UID = [0]


def _sbt(nc, name, shape, dt):
    _UID[0] += 1
    return nc.sbuf_tensor(f"{name}_u{_UID[0]}", shape, dt)


_SEMS = {}


class StopBuild(Exception):
    pass


STOP = [None]
TRUNC = {}


def wload(ph, dst, src, key, nsplit, npieces=4):
    K, N = dst.shape[1], dst.shape[2]
    step = K // nsplit
    for pc in range(nsplit):
        keys = [key + (q,) for q in range(pc * npieces // nsplit, (pc + 1) * npieces // nsplit)]
        ph.dma("pool", dst[:, pc * step:(pc + 1) * step, :].rearrange("p k n -> p (k n)"),
               src[:, pc * step * N:(pc + 1) * step * N], writes=keys, max_dma_last_dim=8192)


class PsumRot:
    def __init__(self, tiles, tag):
        self.tiles, self.tag, self.i = tiles, tag, 0

    def next(self):
        t = self.tiles[self.i % len(self.tiles)]
        k = (self.tag, self.i % len(self.tiles))
        self.i += 1
        return t, k


def build(depth=L, dbg=None):
    nc = bass.Bass("TRN2", target_bir_lowering=False)
    dt_in = lambda name, shape, dt=F32: nc.dram_tensor(name, shape, dt, kind="ExternalInput").ap()
    x_d = dt_in("x", [S, D])
    cT_d = dt_in("cT", [128, 8])
    pos_d = dt_in("pos", [128, 32], I32)
    pv_d = dt_in("pvec", [L, 128, NPV])
    dw_d = dt_in("dww", [L, 128, 4 * KC])
    wada_d = dt_in("wada", [L, 12, 128, 8 * 512])
    win_d = dt_in("win", [L, 9, 128, 8 * 512])
    wbr_d = dt_in("wbr", [L, 2, 64, 8 * 512])
    wpw_d = dt_in("wpw", [L, 2, 128, 4 * 512])
    wout_d = dt_in("wout", [L, 2, 128, 8 * 512])
    wup_d = dt_in("wup", [L, 8, 128, 8 * 512])
    wdn_d = dt_in("wdn", [L, 8, 128, 32 * 128])
    out_d = nc.dram_tensor("out", [S, D], F32, kind="ExternalOutput").ap()
    dbg_d = {}
    if dbg:
        for name, shape, dt in dbg:
            dbg_d[name] = nc.dram_tensor("dbg_" + name, shape, dt, kind="ExternalOutput").ap()

    xs_d = nc.dram_tensor("xs", [2, 128, 8 * HALF], F32).ap()
    qT_d = nc.dram_tensor("qTd", [H, 64, HALF], BF16).ap()
    kT_d = nc.dram_tensor("kTd", [H, 64, S], BF16).ap()
    v_d = nc.dram_tensor("vd", [S, 512], BF16).ap()
    bT_d = nc.dram_tensor("bTd", [H, 16, HALF], BF16).ap()

    es = contextlib.ExitStack()
    sb = lambda name, shape, dt=F32: es.enter_context(_sbt(nc, name, shape, dt))
    ps = lambda name, shape, dt=F32: es.enter_context(nc.psum_tensor(name, shape, dt))

    try:
      with es:
        _build_body(nc, es, sb, ps, locals(), depth)
    except StopBuild:
        pass
    return nc


def _build_body(nc, es, sb, ps, G, depth):
    x_d, cT_d, pos_d, pv_d, dw_d, wada_d, out_d = G["x_d"], G["cT_d"], G["pos_d"], G["pv_d"], G["dw_d"], G["wada_d"], G["out_d"]
    win_d, wbr_d, wpw_d, wout_d, wup_d, wdn_d = G["win_d"], G["wbr_d"], G["wpw_d"], G["wout_d"], G["wup_d"], G["wdn_d"]
    xs_d, qT_d, kT_d, v_d, bT_d, dbg_d = G["xs_d"], G["qT_d"], G["kT_d"], G["v_d"], G["bT_d"], G["dbg_d"]
    if True:
        ident_b = sb("ident_b", [128, 128], BF16)
        ident_f = sb("ident_f", [128, 128], F32)
        ones_b = sb("ones_b", [128, 128], BF16)
        trimask = sb("trimask", [128, 128], BF16)
        indic = sb("indic", [16, S], BF16)
        cos_t = sb("cos_t", [128, 32, 8], F32)
        sin_t = sb("sin_t", [128, 32, 8], F32)
        pv = sb("pv", [128, L, NPV], F32)
        prm = sb("prm", [128, L, 96], F32)
        dww = sb("dww", [128, L, 4 * KC], F32)
        kmT = sb("kmT", [64, H, 16], BF16)
        halo = sb("halo", [128, 4, 32], BF16)
        pbank = [ps(f"pb{i}", [128, 512], F32) for i in range(8)]

        def P(l, i, k=None):
            if k is None:
                return prm[:, l, i * 8:(i + 1) * 8]
            return prm[:, l, i * 8 + k:i * 8 + k + 1]


        with contextlib.ExitStack() as es0:
            sb0 = lambda name, shape, dt=F32: es0.enter_context(_sbt(nc, name, shape, dt))
            posi = sb0("posi", [128, 32], I32)
            posf = sb0("posf", [128, 32], F32)
            ang = sb0("ang", [128, 32, 8], F32)
            kk = sb0("kk", [128, 32, 8], F32)
            kki = sb0("kki", [128, 32, 8], I32)
            tmpa = sb0("tmpa", [128, 32, 8], F32)
            iot = sb0("iot", [128, 128], F32)
            iop = sb0("iop", [128, 1], F32)
            iot_i = sb0("iot_i", [128, 128], I32)
            iop_i = sb0("iop_i", [128, 1], I32)
            indi = sb0("indi", [16, S], I32)
            cT = sb0("cTs", [128, 8], F32)
            cth = sb0("cth", [128, 8], F32)
            cTb = sb0("cTb", [128, 8], BF16)
            wa = [sb0(f"wa{i}", [128, 8, 512], BF16) for i in range(2)]
            adas = sb0("adas", [128, L, 48], F32)
            indf = sb0("indf", [16, S], F32)
            halfpi = sb0("halfpi", [128, 1], F32)
            ph = Phase(nc, "p0")
            ph.op("pool", lambda e: e.iota(iot_i[:], pattern=[[1, 128]], base=0, channel_multiplier=0), writes=["iot_i"])
            ph.op("pool", lambda e: e.iota(iop_i[:], pattern=[[0, 1]], base=0, channel_multiplier=1), writes=["iop_i"])
            ph.op("dve", lambda e: e.tensor_copy(out=iot[:], in_=iot_i[:]), reads=["iot_i"], writes=["iot"])
            ph.op("dve", lambda e: e.tensor_copy(out=iop[:], in_=iop_i[:]), reads=["iop_i"], writes=["iop"])
            ph.op("dve", lambda e: e.memset(kmT[:], 0.0), writes=["kmT"])
            ph.op("dve", lambda e: e.tensor_scalar(out=ident_f[:], in0=iot[:], scalar1=iop[:, 0:1], scalar2=None,
                                                   op0=ALU.is_equal), reads=["iot", "iop"], writes=["idf"])
            ph.op("dve", lambda e: e.tensor_copy(out=ident_b[:], in_=ident_f[:]), reads=["idf"], writes=["idb"])
            ph.op("dve", lambda e: e.memset(ones_b[:], 1.0), writes=["ones"])
            ph.op("dve", lambda e: e.tensor_scalar(out=trimask[:], in0=iot[:], scalar1=iop[:, 0:1], scalar2=-BIG,
                                                   op0=ALU.is_lt, op1=ALU.mult), reads=["iot", "iop"], writes=["tri"])
            tmpi = sb0("tmpi", [16, S], F32)
            ph.op("pool", lambda e: e.iota(indi[:], pattern=[[1, S]], base=0, channel_multiplier=-256), writes=["indi"])
            ph.op("dve", lambda e: e.tensor_copy(out=indf[:], in_=indi[:]), reads=["indi"], writes=["indf"])
            ph.op("dve", lambda e: e.tensor_scalar(out=tmpi[:], in0=indf[:], scalar1=255.5, scalar2=None, op0=ALU.is_lt),
                  reads=["indf"], writes=["tmpi1"])
            ph.op("dve", lambda e: e.tensor_scalar(out=indf[:], in0=indf[:], scalar1=-0.5, scalar2=None, op0=ALU.is_gt),
                  reads=["indf", "tmpi1"], writes=["indf1"])
            ph.op("dve", lambda e: e.tensor_tensor(out=indic[:], in0=indf[:], in1=tmpi[:], op=ALU.mult),
                  reads=["indf1", "tmpi1"], writes=["indic"])
            ph.dma("sp", posi[:], pos_d, writes=["posi"])
            ph.op("dve", lambda e: e.tensor_copy(out=posf[:], in_=posi[:]), reads=["posi"], writes=["posf"])
            for i in range(8):
                invf = float(np.float32(500000.0) ** np.float32(-(2.0 * i) / 16.0))
                ph.op("dve", lambda e, i=i, invf=invf: e.tensor_scalar(out=ang[:, :, i], in0=posf[:], scalar1=invf,
                                                                       scalar2=None, op0=ALU.mult),
                      reads=["posf"], writes=["ang"])
            ph.op("dve", lambda e: e.tensor_scalar(out=kk[:], in0=ang[:], scalar1=float(1.0 / (2 * math.pi)), scalar2=None,
                                                   op0=ALU.mult), reads=["ang"], writes=["kk"])
            ph.op("dve", lambda e: e.tensor_copy(out=kki[:], in_=kk[:]), reads=["kk"], writes=["kki"])
            ph.op("dve", lambda e: e.tensor_copy(out=kk[:], in_=kki[:]), reads=["kki"], writes=["kk2"])
            C1, C2 = 6.28125, float(2 * math.pi - 6.28125)
            ph.op("dve", lambda e: e.scalar_tensor_tensor(out=ang[:], in0=kk[:], scalar=-C1, in1=ang[:], op0=ALU.mult,
                                                          op1=ALU.add), reads=["kk2", "ang"], writes=["ang"])
            ph.op("dve", lambda e: e.scalar_tensor_tensor(out=ang[:], in0=kk[:], scalar=-C2, in1=ang[:], op0=ALU.mult,
                                                          op1=ALU.add), reads=["kk2", "ang"], writes=["ang"])
            ph.op("dve", lambda e: e.tensor_scalar(out=tmpa[:], in0=ang[:], scalar1=math.pi, scalar2=-2 * math.pi,
                                                   op0=ALU.is_gt, op1=ALU.mult), reads=["ang"], writes=["tmpa"])
            ph.op("dve", lambda e: e.tensor_tensor(out=ang[:], in0=ang[:], in1=tmpa[:], op=ALU.add),
                  reads=["ang", "tmpa"], writes=["ang"])
            ph.op("dve", lambda e: e.tensor_scalar(out=tmpa[:], in0=ang[:], scalar1=-math.pi, scalar2=2 * math.pi,
                                                   op0=ALU.is_lt, op1=ALU.mult), reads=["ang"], writes=["tmpa"])
            ph.op("dve", lambda e: e.tensor_tensor(out=ang[:], in0=ang[:], in1=tmpa[:], op=ALU.add),
                  reads=["ang", "tmpa"], writes=["ang"])
            ph.op("act", lambda e: e.activation(out=sin_t[:], in_=ang[:], func=AF.Sin), reads=["ang"], writes=["sin"])
            ph.op("dve", lambda e: e.tensor_scalar(out=tmpa[:], in0=ang[:], scalar1=-1.0, scalar2=None, op0=ALU.mult),
                  reads=["ang"], writes=["tmpa"])
            ph.op("dve", lambda e: e.tensor_tensor(out=tmpa[:], in0=tmpa[:], in1=ang[:], op=ALU.max),
                  reads=["ang", "tmpa"], writes=["tmpa"])
            ph.op("dve", lambda e: e.memset(halfpi[:], math.pi / 2), writes=["halfpi"])
            ph.op("act", lambda e: e.activation(out=cos_t[:], in_=tmpa[:], func=AF.Sin, scale=-1.0, bias=halfpi[:, 0:1]),
                  reads=["tmpa", "halfpi"], writes=["cos"])
            ph.dma("sp", pv[:], pv_d.rearrange("l p n -> p l n"), writes=["pv"])
            ph.dma("sp", dww[:], dw_d.rearrange("l p n -> p l n"), writes=["dww"])
            ph.dma("sp", cT[:], cT_d, writes=["cT"])
            ph.op("act", lambda e: e.activation(out=cth[:], in_=cT[:], func=AF.Tanh, scale=0.5), reads=["cT"], writes=["cth"])
            ph.op("dve", lambda e: e.scalar_tensor_tensor(out=cth[:], in0=cth[:], scalar=1.0, in1=cT[:], op0=ALU.add,
                                                          op1=ALU.mult), reads=["cth", "cT"], writes=["cth2"])
            ph.op("dve", lambda e: e.tensor_scalar(out=cTb[:], in0=cth[:], scalar1=0.5, scalar2=None, op0=ALU.mult),
                  reads=["cth2"], writes=["cTb"])
            rot = PsumRot(pbank[0:2], "adaps")
            it = 0
            for l in range(depth):
                for cb in range(12):
                    w = wa[it % 2]
                    wk = ("wa", it % 2)
                    it += 1
                    ph.dma("pool", w[:].rearrange("p k n -> p (k n)"), wada_d[l, cb], writes=[wk], max_dma_last_dim=8192)
                    pt, pk = rot.next()
                    for m in range(4):
                        for k in range(8):
                            ph.op("pe", lambda e, w=w, m=m, k=k, pt=pt: e.matmul(
                                pt[:, m:m + 1], lhsT=w[:, k, m * 128:(m + 1) * 128], rhs=cTb[:, k:k + 1],
                                start=(k == 0), stop=(k == 7)), reads=[wk, "cTb"], writes=[pk])
                    ph.op("dve", lambda e, l=l, cb=cb, pt=pt: e.tensor_tensor(
                        out=adas[:, l, cb * 4:(cb + 1) * 4], in0=pt[:, 0:4], in1=pv[:, l, 32 + cb * 4:32 + (cb + 1) * 4],
                        op=ALU.add), reads=[pk, "pv"], writes=["adas"])
            for l in range(depth):
                A = lambda i: adas[:, l, i * 8:(i + 1) * 8]
                def ts(out, in0, s1, s2, op0, op1=ALU.bypass, l=l):
                    ph.op("dve", lambda e: e.tensor_scalar(out=out, in0=in0, scalar1=s1, scalar2=s2, op0=op0, op1=op1),
                          reads=["adas", "pv", "dww"], writes=["prm"])
                ts(P(l, 0), A(1), 1.0, 1.0 / ALPHA, ALU.add, ALU.mult)
                ts(P(l, 1), A(0), 1.0, None, ALU.mult)
                ts(P(l, 2), A(2), 1.0, 0.5, ALU.add, ALU.mult)
                ts(P(l, 3), A(4), 1.0, 1.0 / ALPHA, ALU.add, ALU.mult)
                ts(P(l, 4), A(3), 1.0, None, ALU.mult)
                ts(P(l, 5), A(5), 1.0, None, ALU.add)
                ts(P(l, 6), pv[:, l, 0:8], ALPHA, None, ALU.mult)
                ts(P(l, 7), pv[:, l, 8:16], ALPHA, None, ALU.mult)
                ts(P(l, 8), pv[:, l, 16:24], ALPHA, None, ALU.mult)
                ts(P(l, 9), pv[:, l, 24:32], ALPHA, None, ALU.mult)
                ts(prm[:, l, 80:84], pv[:, l, 80:84], 1.0, None, ALU.mult)
                ts(prm[:, l, 84:88], pv[:, l, 84:88], 0.5, None, ALU.mult)
                ts(prm[:, l, 88:92], pv[:, l, 88:92], 0.5, None, ALU.mult)
                ph.op("dve", lambda e, l=l: e.tensor_scalar(out=dww[:, l, :], in0=dww[:, l, :], scalar1=0.5, scalar2=None,
                                                            op0=ALU.mult), reads=["dww"], writes=["dww"])
            ph.emit()

        for hf in range(2):
            with contextlib.ExitStack() as es1:
                xT = es1.enter_context(_sbt(nc, "xT", [128, 8, HALF], F32))
                xin = [es1.enter_context(_sbt(nc, f"xin{i}", [128, D], F32)) for i in range(2)]
                ph = Phase(nc, f"px{hf}")
                rot = PsumRot(pbank[0:4], "xps")
                for tt in range(16):
                    xi, xk = xin[tt % 2], ("xin", tt % 2)
                    t0 = hf * HALF + tt * 128
                    ph.dma("sp", xi[:], x_d[t0:t0 + 128, :], writes=[xk])
                    for kq in range(2):
                        pt, pk = rot.next()
                        for j in range(4):
                            k = kq * 4 + j
                            ph.op("pe", lambda e, pt=pt, j=j, k=k, xi=xi: e.transpose(
                                pt[:, j * 128:(j + 1) * 128], xi[:, k * 128:(k + 1) * 128], ident_f[:]),
                                reads=[xk], writes=[pk])
                        eng = "act" if kq == 0 else "dve"
                        dst = xT[:, kq * 4:(kq + 1) * 4, tt * 128:(tt + 1) * 128]
                        src = pt[:].rearrange("p (j t) -> p j t", j=4)
                        if eng == "act":
                            ph.op("act", lambda e, dst=dst, src=src: e.activation(out=dst, in_=src, func=AF.Copy, scale=ALPHA),
                                  reads=[pk], writes=[("xT", tt)])
                        else:
                            ph.op("dve", lambda e, dst=dst, src=src: e.tensor_scalar(out=dst, in0=src, scalar1=ALPHA, scalar2=None,
                                                                                     op0=ALU.mult), reads=[pk], writes=[("xT", tt)])
                ph.dma("sp", xs_d[hf], xT[:].rearrange("p k t -> p (k t)"), reads=[("xT", tt) for tt in range(16)], writes=["xs"])
                ph.emit()

        for l in range(depth):
            for hf in range(2):
                layer_half(nc, l, hf, locals())

        for hf in range(2):
            with contextlib.ExitStack() as es1:
                xT = es1.enter_context(_sbt(nc, "xT", [128, 8, HALF], F32))
                xo = [es1.enter_context(_sbt(nc, f"xo{i}", [128, D], F32)) for i in range(2)]
                ph = Phase(nc, f"pf{hf}")
                ph.dma("sp", xT[:].rearrange("p k t -> p (k t)"), xs_d[hf], writes=["xT"])
                rot = PsumRot(pbank[0:4], "ops")
                for tt in range(16):
                    xi, xk = xo[tt % 2], ("xo", tt % 2)
                    for kq in range(2):
                        pt, pk = rot.next()
                        for j in range(4):
                            k = kq * 4 + j
                            ph.op("pe", lambda e, pt=pt, j=j, k=k, tt=tt: e.transpose(
                                pt[:, j * 128:(j + 1) * 128], xT[:, k, tt * 128:(tt + 1) * 128], ident_f[:]),
                                reads=["xT"], writes=[pk])
                        dst = xi[:, kq * 512:(kq + 1) * 512]
                        if kq == 0:
                            ph.op("act", lambda e, dst=dst, pt=pt: e.activation(out=dst, in_=pt[:], func=AF.Copy, scale=1.0 / ALPHA),
                                  reads=[pk], writes=[(xk, kq)])
                        else:
                            ph.op("dve", lambda e, dst=dst, pt=pt: e.tensor_scalar(out=dst, in0=pt[:], scalar1=1.0 / ALPHA, scalar2=None,
                                                                                   op0=ALU.mult), reads=[pk], writes=[(xk, kq)])
                    t0 = hf * HALF + tt * 128
                    ph.dma("sp", out_d[t0:t0 + 128, :], xi[:], reads=[(xk, 0), (xk, 1)], writes=["out"])
                ph.emit()
    return nc


def layer_half(nc, l, hf, env):
    g = env
    prm, dww, pbank = g["prm"], g["dww"], g["pbank"]
    ident_b, ones_b, trimask, indic = g["ident_b"], g["ones_b"], g["trimask"], g["indic"]
    cos_t, sin_t, kmT, halo = g["cos_t"], g["sin_t"], g["kmT"], g["halo"]
    xs_d, qT_d, kT_d, v_d, bT_d = g["xs_d"], g["qT_d"], g["kT_d"], g["v_d"], g["bT_d"]
    win_d, wbr_d, wpw_d, wout_d, wup_d, wdn_d = g["win_d"], g["wbr_d"], g["wpw_d"], g["wout_d"], g["wup_d"], g["wdn_d"]
    dbg_d = g["dbg_d"]
    P = g["P"]
    T0 = hf * HALF
    nm = f"l{l}h{hf}"
    xs3 = xs_d[hf].rearrange("p (k t) -> p k t", k=8)
    es_mg = contextlib.ExitStack()
    mg = es_mg.enter_context(_sbt(nc, "mg", [128, 8, HALF], BF16))
    esh = contextlib.ExitStack()
    uT = esh.enter_context(_sbt(nc, "uT", [128, 8, HALF], BF16))
    es_x = contextlib.ExitStack()
    xT = es_x.enter_context(_sbt(nc, "xT", [128, 8, HALF], F32))
    if True:
        with contextlib.ExitStack() as esa:
            sba = lambda name, shape, dt=F32: esa.enter_context(_sbt(nc, name, shape, dt))
            wqkv = sba("wqkv", [128, 3, 8, 512], BF16)
            qk_sb = [sba(f"qk_sb{i}", [128, 512], BF16) for i in range(4)]
            v_sb = [sba(f"v_sb{i}", [128, 512], BF16) for i in range(2)]
            rtmp = [sba(f"rtmp{i}", [128, H, 8], F32) for i in range(8)]
            qT_st = sba("qT_st", [64, H, 512], BF16)
            kT_st = sba("kT_st", [64, H, 512], BF16)
            kms = sba("kms", [64, 16], F32)
            gsb = sba("gsb", [128, H, 16], F32)
            top8 = sba("top8", [128, H, 8], F32)
            msk = sba("msk", [128, H, 16], F32)
            bias_sb = sba("bias_sb", [128, H, 16], BF16)
            bT_st = sba("bT_st", [16, H, 512], BF16)
            VB = sba("VB", [128, 8, 16], F32)
            NB = sba("NB", [128, 8, 16], F32)
            VS = sba("VS", [128, 8, 16], F32)
            ph = Phase(nc, nm + "a")
            ph.dma("sp", xT[:], xs3, writes=["xT"])
            for i in range(3):
                wload(ph, wqkv[:, i], win_d[l, i], ("wqkv", i), 4)
            own0 = 8 * hf
            ph.op("pool", lambda e: e.memset(VB[:], -1e30), writes=["VB"])
            ph.op("pool", lambda e: e.memset(NB[:], -BIG), writes=["NB"])
            ph.op("pool", lambda e: e.memset(VS[:], -BIG), writes=["VS"])
            for ob in range(8):
                own = own0 + ob
                if own > 0:
                    ph.op("pool", lambda e, ob=ob, own=own: e.memset(VB[:, ob, 0:own], 0.0), reads=["VB"], writes=["VB"])
                ph.op("pool", lambda e, ob=ob, own=own: e.memset(VS[:, ob, 0:own + 1], 0.0), reads=["VS"], writes=["VS"])
                ph.op("pool", lambda e, ob=ob, own=own: e.memset(NB[:, ob, own:own + 1], 0.0), reads=["NB"], writes=["NB"])
            for k in range(8):
                ph.op("dve", lambda e, k=k: e.tensor_scalar(out=uT[:, k, :], in0=xT[:, k, :], scalar1=P(l, 0, k), scalar2=P(l, 1, k),
                                                            op0=ALU.mult, op1=ALU.add), reads=["xT"], writes=[("uT", k)])
            rot = PsumRot(pbank[0:3], "qkv")
            trot = PsumRot(pbank[3:5], "tr")
            grot = PsumRot(pbank[5:7], "gate")
            pendA = []
            for tg in range(4):
                for t4 in range(4):
                    tt = tg * 4 + t4
                    gt = hf * 16 + tt
                    tsl = slice(tt * 128, (tt + 1) * 128)
                    for blk in range(3):
                        pt, pk = rot.next()
                        for k in range(8):
                            ph.op("pe", lambda e, pt=pt, k=k, blk=blk, tsl=tsl: e.matmul(
                                pt[:], lhsT=uT[:, k, tsl], rhs=wqkv[:, blk, k, :], start=(k == 0), stop=(k == 7)),
                                reads=[("wqkv", blk, k // 2), ("uT", k)], writes=[pk])
                        if blk == 0:
                            while pendA:
                                pendA.pop(0)()
                        if blk == 2:
                            vs, vk = v_sb[tt % 2], ("v_sb", tt % 2)
                            ph.op("act", lambda e, vs=vs, pt=pt: e.activation(out=vs[:], in_=pt[:], func=AF.Copy), reads=[pk], writes=[vk])
                            ph.dma("sp", v_d[T0 + tt * 128:T0 + (tt + 1) * 128, :], vs[:], reads=[vk], writes=["v_d"])
                            continue
                        qs, qk = qk_sb[blk * 2 + tt % 2], ("qk_sb", blk * 2 + tt % 2)
                        p3 = pt[:].rearrange("p (h d) -> p h d", h=H)
                        q3 = qs[:].rearrange("p (h d) -> p h d", h=H)
                        ph.op("dve", lambda e, q3=q3, p3=p3: e.tensor_copy(out=q3[:, :, 16:64], in_=p3[:, :, 16:64]),
                              reads=[pk], writes=[(qk, "nr")])
                        cosb = cos_t[:, gt, :].unsqueeze(1).to_broadcast([128, H, 8])
                        sinb = sin_t[:, gt, :].unsqueeze(1).to_broadcast([128, H, 8])
                        x1, x2 = p3[:, :, 0:8], p3[:, :, 8:16]
                        r = rtmp[blk * 4:blk * 4 + 4]
                        ro = blk * 4
                        def tt_(out, a, b, op, rk, wk, extra_r=()):
                            ph.op("dve", lambda e: e.tensor_tensor(out=out, in0=a, in1=b, op=op), reads=list(rk) + list(extra_r), writes=wk)
                        tt_(r[0][:], x1, cosb, ALU.mult, [pk], [("rt", ro + 0)])
                        tt_(r[1][:], x2, sinb, ALU.mult, [pk], [("rt", ro + 1)])
                        tt_(r[2][:], x2, cosb, ALU.mult, [pk], [("rt", ro + 2)])
                        tt_(r[3][:], x1, sinb, ALU.mult, [pk], [("rt", ro + 3)])
                        tt_(q3[:, :, 0:8], r[0][:], r[1][:], ALU.subtract, [("rt", ro + 0), ("rt", ro + 1)], [(qk, "r1")])
                        tt_(q3[:, :, 8:16], r[2][:], r[3][:], ALU.add, [("rt", ro + 2), ("rt", ro + 3)], [(qk, "r2")])
                        def trans(qs=qs, qk=qk, blk=blk, t4=t4):
                            tp, tk = trot.next()
                            tpb = tp[:].bitcast(BF16)
                            for h in range(H):
                                ph.op("pe", lambda e, tpb=tpb, h=h, qs=qs: e.transpose(
                                    tpb[0:64, h * 128:(h + 1) * 128], qs[:, h * 64:(h + 1) * 64], ident_b[:]),
                                    reads=[(qk, "nr"), (qk, "r1"), (qk, "r2")], writes=[tk])
                            st = qT_st if blk == 0 else kT_st
                            stk = ("qT_st" if blk == 0 else "kT_st")
                            ph.op("act", lambda e, st=st, tpb=tpb, t4=t4: e.activation(
                                out=st[:, :, t4 * 128:(t4 + 1) * 128], in_=tpb[0:64, 0:1024].rearrange("p (h t) -> p h t", h=H), func=AF.Copy),
                                reads=[tk], writes=[(stk, t4)])
                        pendA.append(trans)
                while pendA:
                    pendA.pop(0)()
                ph.dma("sp", kT_d[:, :, T0 + tg * 512:T0 + (tg + 1) * 512].rearrange("h d t -> d h t"), kT_st[:],
                       reads=[("kT_st", i) for i in range(4)], writes=["kT_d"])
                ph.dma("sp", qT_d[:, :, tg * 512:(tg + 1) * 512].rearrange("h d t -> d h t"), qT_st[:],
                       reads=[("qT_st", i) for i in range(4)], writes=["qT_d"])
                b0 = own0 + 2 * tg
                ph.op("dve", lambda e: e.tensor_reduce(out=kms[:], in_=kT_st[:].rearrange("p h (b t) -> p (h b) t", b=2),
                                                       axis=AX.X, op=ALU.add), reads=[("kT_st", i) for i in range(4)], writes=["kms"])
                ph.op("dve", lambda e, b0=b0: e.tensor_scalar(out=kmT[:, :, b0:b0 + 2], in0=kms[:].rearrange("p (h b) -> p h b", b=2),
                                                              scalar1=1.0 / 256, scalar2=None, op0=ALU.mult), reads=["kms"], writes=["kmT"])
                for t4 in range(4):
                    tt = tg * 4 + t4
                    ob = tt // 2
                    gp, gk = grot.next()
                    for h in range(H):
                        ph.op("pe", lambda e, gp=gp, h=h, t4=t4: e.matmul(
                            gp[:, h * 16:(h + 1) * 16], lhsT=qT_st[:, h, t4 * 128:(t4 + 1) * 128], rhs=kmT[:, h, :], start=True, stop=True),
                            reads=[("qT_st", t4), "kmT"], writes=[gk])
                    g3 = gp[:, 0:128].rearrange("p (h s) -> p h s", h=H)
                    bc = lambda t: t[:, ob, :].unsqueeze(1).to_broadcast([128, H, 16])
                    ph.op("dve", lambda e, g3=g3, ob=ob: e.tensor_tensor(out=gsb[:], in0=g3, in1=VB[:, ob, :].unsqueeze(1).to_broadcast([128, H, 16]),
                                                                         op=ALU.add), reads=[gk, "VB"], writes=["gsb"])
                    for h in range(H):
                        ph.op("dve", lambda e, h=h: e.max(out=top8[:, h, :], in_=gsb[:, h, :]), reads=["gsb"], writes=[("top8", h)])
                    ph.op("dve", lambda e: e.tensor_tensor(out=msk[:], in0=gsb[:], in1=top8[:, :, 2:3].to_broadcast([128, H, 16]), op=ALU.is_lt),
                          reads=["gsb"] + [("top8", h) for h in range(H)], writes=["msk"])
                    ph.op("dve", lambda e, ob=ob: e.tensor_tensor(out=msk[:], in0=msk[:], in1=NB[:, ob, :].unsqueeze(1).to_broadcast([128, H, 16]),
                                                                  op=ALU.mult), reads=["msk", "NB"], writes=["msk"])
                    ph.op("dve", lambda e, ob=ob: e.tensor_tensor(out=bias_sb[:], in0=msk[:], in1=VS[:, ob, :].unsqueeze(1).to_broadcast([128, H, 16]),
                                                                  op=ALU.add), reads=["msk", "VS"], writes=["bias_sb"])
                    tp, tk = trot.next()
                    tpb = tp[:].bitcast(BF16)
                    for h in range(H):
                        ph.op("pe", lambda e, tpb=tpb, h=h: e.transpose(tpb[0:16, h * 128:(h + 1) * 128], bias_sb[:, h, :], ident_b[:]),
                              reads=["bias_sb"], writes=[tk])
                    ph.op("act", lambda e, tpb=tpb, t4=t4: e.activation(
                        out=bT_st[:, :, t4 * 128:(t4 + 1) * 128], in_=tpb[0:16, 0:1024].rearrange("p (h t) -> p h t", h=H), func=AF.Copy),
                        reads=[tk], writes=[("bT_st", t4)])
                ph.dma("sp", bT_d[:, :, tg * 512:(tg + 1) * 512].rearrange("h s t -> s h t"), bT_st[:],
                       reads=[("bT_st", i) for i in range(4)], writes=["bT_d"])
            ph.emit()
        es_x.close()
        es_cv = contextlib.ExitStack()
        cvT = es_cv.enter_context(_sbt(nc, "cvT", [128, 4, HALF], BF16))

        with contextlib.ExitStack() as esb:
            sbb = lambda name, shape, dt=F32: esb.enter_context(_sbt(nc, name, shape, dt))
            hT = sbb("hT", [128, 4, 32 + HALF], BF16)
            wgl = sbb("wgl", [128, 2, 8, 512], BF16)
            dmat = sbb("dmat", [128, 4 * KC, 128], BF16)
            sg = [sbb(f"sg{i}", [128, 512], F32) for i in range(2)]
            cv = sbb("cv", [128, 4, 512], F32)
            cvb = sbb("cvb", [128, 4, 512], BF16)
            csq = sbb("csq", [128, 4, 512], BF16)
            st1 = sbb("st1", [128, 512], F32)
            st2 = sbb("st2", [128, 512], F32)
            st3 = sbb("st3", [128, 512], F32)
            mhalf = sbb("mhalf", [128, 512], F32)
            zt = [sbb(f"zt{i}", [128, 512], F32) for i in range(2)]
            th = [sbb(f"th{i}", [128, 512], F32) for i in range(2)]
            ph = Phase(nc, nm + "b")
            for i in range(2):
                wload(ph, wgl[:, i], win_d[l, 3 + i], ("wgl", i), 4)
            ph.op("pool", lambda e: e.memset(mhalf[:], -0.5), writes=["mhalf"])
            for j in range(4 * KC):
                ph.op("dve", lambda e, j=j: e.tensor_scalar(out=dmat[:, j, :], in0=ident_b[:], scalar1=dww[:, l, j:j + 1], scalar2=None,
                                                          op0=ALU.mult), writes=[("dmat", j)])
            if hf == 0:
                ph.op("dve", lambda e: e.memset(hT[:, :, 0:32], 0.0), writes=["hT_halo"])
            else:
                ph.op("dve", lambda e: e.tensor_copy(out=hT[:, :, 0:32], in_=halo[:]), writes=["hT_halo"])
            rot = PsumRot(pbank[0:4], "glu")
            for tg in range(4):
                tsl = slice(tg * 512, (tg + 1) * 512)
                for c in range(4):
                    pa, pak = rot.next()
                    pg, pgk = rot.next()
                    for which, pt, pk in ((0, pa, pak), (1, pg, pgk)):
                        for k in range(8):
                            ph.op("pe", lambda e, pt=pt, k=k, which=which, c=c, tsl=tsl: e.matmul(
                                pt[:], lhsT=wgl[:, which, k, c * 128:(c + 1) * 128], rhs=uT[:, k, tsl], start=(k == 0), stop=(k == 7)),
                                reads=[("wgl", which, k // 2)], writes=[pk])
                    s, sk = sg[c % 2], ("sg", c % 2)
                    ph.op("act", lambda e, s=s, pg=pg: e.activation(out=s[:], in_=pg[:], func=AF.Tanh, scale=0.5), reads=[pgk], writes=[sk])
                    ph.op("dve", lambda e, s=s, pa=pa, c=c, tg=tg: e.scalar_tensor_tensor(
                        out=hT[:, c, 32 + tg * 512:32 + (tg + 1) * 512], in0=s[:], scalar=1.0, in1=pa[:], op0=ALU.add, op1=ALU.mult),
                        reads=[sk, pak], writes=[("hT", tg)])
            if hf == 0:
                ph.op("dve", lambda e: e.tensor_copy(out=halo[:], in_=hT[:, :, HALF:HALF + 32]), reads=[("hT", 3)], writes=["halo"])
            crot = PsumRot(pbank[4:6], "conv")
            for tg in range(4):
                hk = ["hT_halo"] + [("hT", i) for i in range(max(0, tg - 1), tg + 1)]
                s1p, s2p = pbank[6], pbank[7]
                for c in range(4):
                    pt, pk = crot.next()
                    for k in range(KC):
                        c0 = 32 + tg * 512 - 30 + k
                        ph.op("pe", lambda e, pt=pt, c=c, k=k, c0=c0: e.matmul(
                            pt[:], lhsT=dmat[:, c * KC + k, :], rhs=hT[:, c, c0:c0 + 512], start=(k == 0), stop=(k == KC - 1)),
                            reads=hk + [("dmat", c * KC + k)], writes=[pk])
                    ph.op("act", lambda e, pt=pt, c=c: e.activation(out=cv[:, c, :], in_=pt[:], func=AF.Identity, bias=prm[:, l, 80 + c:81 + c]),
                          reads=[pk], writes=[("cv", c)])
                    ph.op("act", lambda e, c=c: e.activation(out=csq[:, c, :], in_=cv[:, c, :], func=AF.Square), reads=[("cv", c)], writes=[("csq", c)])
                    ph.op("act", lambda e, c=c: e.activation(out=cvb[:, c, :], in_=cv[:, c, :], func=AF.Copy), reads=[("cv", c)], writes=[("cvb", c)])
                for c in range(4):
                    ph.op("pe", lambda e, c=c: e.matmul(s1p[:], lhsT=ones_b[:], rhs=cvb[:, c, :], start=(c == 0), stop=(c == 3)),
                          reads=[("cvb", c)], writes=["s1p"])
                for c in range(4):
                    ph.op("pe", lambda e, c=c: e.matmul(s2p[:], lhsT=ones_b[:], rhs=csq[:, c, :], start=(c == 0), stop=(c == 3)),
                          reads=[("csq", c)], writes=["s2p"])
                ln_stats(ph, s1p, s2p, st1, st2, st3, mhalf, 1.0 / CW)
                for c in range(4):
                    z, zk = zt[c % 2], ("zt", c % 2)
                    t_, tk_ = th[c % 2], ("th", c % 2)
                    ph.op("dve", lambda e, c=c: e.tensor_tensor(out=cv[:, c, :], in0=cv[:, c, :], in1=st1[:], op=ALU.subtract),
                          reads=[("cv", c), "mean"], writes=[("cv", c)])
                    ph.op("dve", lambda e, c=c: e.tensor_tensor(out=cv[:, c, :], in0=cv[:, c, :], in1=st3[:], op=ALU.mult),
                          reads=[("cv", c), "rstd"], writes=[("cv", c)])
                    ph.op("dve", lambda e, c=c, z=z: e.tensor_scalar(out=z[:], in0=cv[:, c, :], scalar1=prm[:, l, 84 + c:85 + c],
                                                                     scalar2=prm[:, l, 88 + c:89 + c], op0=ALU.mult, op1=ALU.add),
                          reads=[("cv", c)], writes=[zk])
                    ph.op("act", lambda e, z=z, t_=t_: e.activation(out=t_[:], in_=z[:], func=AF.Tanh), reads=[zk], writes=[tk_])
                    ph.op("dve", lambda e, c=c, z=z, t_=t_, tg=tg: e.scalar_tensor_tensor(
                        out=cvT[:, c, tg * 512:(tg + 1) * 512], in0=t_[:], scalar=1.0, in1=z[:], op0=ALU.add, op1=ALU.mult),
                        reads=[zk, tk_], writes=[("cvT", tg)])
            ph.emit()
        es_at = contextlib.ExitStack()
        attnT = es_at.enter_context(_sbt(nc, "attnT", [64, H, HALF], BF16))

        nk = HALF * (hf + 1)
        nkt = nk // 128
        with contextlib.ExitStack() as esc:
            sbc = lambda name, shape, dt=F32: esc.enter_context(_sbt(nc, name, shape, dt))
            Qa = [sbc(f"Qa{i}", [80, HALF], BF16) for i in range(2)]
            Ka = [sbc(f"Ka{i}", [80, S], BF16) for i in range(2)]
            Va = [sbc(f"Va{i}", [128, 32, 128], BF16) for i in range(2)]
            pT = [sbc(f"pT{i}", [128, 512], BF16) for i in range(4)]
            rec = [sbc(f"rec{i}", [64, 512], F32) for i in range(2)]
            ph = Phase(nc, nm + "c")
            for i in range(2):
                ph.op("dve", lambda e, i=i: e.memset(Va[i][:, :, 64:128], 1.0), writes=[("Va1", i)])
                ph.dma("sp", Ka[i][64:80, 0:nk], indic[:, 0:nk] if hf == 1 else indic[:, 0:nk], writes=[("Kai", i)])
            if hf == 0:
                pass
            srot = PsumRot(pbank[0:4], "sT")
            orot = PsumRot(pbank[4:6], "oT")
            pi = 0
            pend = []
            LAG = 2

            def flush(n_keep):
                while len(pend) > n_keep:
                    pend.pop(0)()

            for h in range(H):
                b = h % 2
                ph.dma("sp", Qa[b][0:64, :], qT_d[h], writes=[("Qa", b)])
                ph.dma("sp", Qa[b][64:80, :], bT_d[h], writes=[("Qab", b)])
                ph.dma("sp", Ka[b][0:64, 0:nk], kT_d[h, :, 0:nk], writes=[("Ka", b)])
                ph.dma("sp", Va[b][:, 0:nkt, 0:64], v_d[0:nk, h * 64:(h + 1) * 64].rearrange("(t p) d -> p t d", p=128),
                       writes=[("Va", b)])
                rd = [("Qa", b), ("Qab", b), ("Ka", b), ("Kai", b)]
                for qg in range(4):
                    op_, ok_ = orot.next()
                    ktiles = list(range(hf * 16 + qg * 4 + 4))
                    for kt in ktiles:
                        lt = kt - hf * 16
                        r0 = max(0, lt - qg * 4)
                        q0 = qg * 512 + r0 * 128
                        n = 512 - r0 * 128
                        sp_, sk_ = srot.next()
                        diag = lt >= qg * 4
                        ph.op("pe", lambda e, sp_=sp_, b=b, kt=kt, q0=q0, n=n, diag=diag: e.matmul(
                            sp_[:, 0:n], lhsT=Ka[b][:, kt * 128:(kt + 1) * 128], rhs=Qa[b][:, q0:q0 + n], start=True, stop=not diag),
                            reads=rd, writes=[sk_])
                        if diag:
                            ph.op("pe", lambda e, sp_=sp_: e.matmul(sp_[:, 0:128], lhsT=ident_b[:], rhs=trimask[:], start=False, stop=True),
                                  reads=[], writes=[sk_])
                        p_, pk_ = pT[pi % 4], ("pT", pi % 4)
                        pi += 1
                        ph.op("act", lambda e, p_=p_, sp_=sp_, n=n: e.activation(out=p_[:, 0:n], in_=sp_[:, 0:n], func=AF.Exp, scale=HD ** -0.5),
                              reads=[sk_], writes=[pk_])
                        c0 = r0 * 128

                        def pv(op_=op_, ok_=ok_, p_=p_, pk_=pk_, b=b, kt=kt, c0=c0, n=n, first=(kt == 0), last=(kt == ktiles[-1])):
                            ph.op("pe", lambda e: e.matmul(op_[:, c0:c0 + n], lhsT=Va[b][:, kt, :], rhs=p_[:, 0:n], start=first, stop=last),
                                  reads=[pk_, ("Va", b), ("Va1", b)], writes=[ok_])
                        pend.append(pv)
                        flush(LAG)

                    def norm(op_=op_, ok_=ok_, h=h, qg=qg):
                        rc, rk = rec[qg % 2], ("rec", qg % 2)
                        ph.op("dve", lambda e: e.reciprocal(out=rc[:], in_=op_[64:128, :]), reads=[ok_], writes=[rk])
                        ph.op("dve", lambda e: e.tensor_tensor(out=attnT[:, h, qg * 512:(qg + 1) * 512], in0=op_[0:64, :], in1=rc[:], op=ALU.mult),
                              reads=[ok_, rk], writes=[("attnT", h)])
                    pend.append(norm)
            flush(0)
            ph.emit()

        with contextlib.ExitStack() as esd:
            sbd = lambda name, shape, dt=F32: esd.enter_context(_sbt(nc, name, shape, dt))
            wga = [sbd(f"wga{i}", [128, 8, 512], BF16) for i in range(2)]
            wgc = [sbd(f"wgc{i}", [128, 8, 512], BF16) for i in range(2)]
            wbr = [sbd(f"wbr{i}", [64, 8, 512], BF16) for i in range(2)]
            wpw = [sbd(f"wpw{i}", [128, 4, 512], BF16) for i in range(2)]
            sa = [sbd(f"sa{i}", [128, 512], F32) for i in range(2)]
            sc_ = [sbd(f"sc{i}", [128, 512], F32) for i in range(2)]
            m1 = [sbd(f"m1{i}", [128, 512], F32) for i in range(2)]
            ph = Phase(nc, nm + "d1")
            rot = PsumRot(pbank[0:8], "mrg")
            for cb in range(2):
                wload(ph, wbr[cb][:], wbr_d[l, cb], ("wbr", cb), 4 if cb == 0 else 1)
                wload(ph, wpw[cb][:], wpw_d[l, cb], ("wpw", cb), 2 if cb == 0 else 1, npieces=2)
                wload(ph, wga[cb][:], win_d[l, 5 + cb], ("wga", cb), 4 if cb == 0 else 1)
                wload(ph, wgc[cb][:], win_d[l, 7 + cb], ("wgc", cb), 4 if cb == 0 else 1)
            it = 0
            for m in range(8):
                cb, mc = m // 4, (m % 4) * 128
                for tg in range(4):
                    tsl = slice(tg * 512, (tg + 1) * 512)
                    pya, kya = rot.next()
                    pyc, kyc = rot.next()
                    pga, kga = rot.next()
                    pgc, kgc = rot.next()
                    for hh in range(H):
                        ph.op("pe", lambda e, pya=pya, hh=hh, cb=cb, mc=mc, tsl=tsl: e.matmul(
                            pya[:], lhsT=wbr[cb][:, hh, mc:mc + 128], rhs=attnT[:, hh, tsl], start=(hh == 0), stop=(hh == H - 1)),
                            reads=[("wbr", cb, hh // 2)], writes=[kya])
                    for k in range(4):
                        ph.op("pe", lambda e, pyc=pyc, k=k, cb=cb, mc=mc, tsl=tsl: e.matmul(
                            pyc[:], lhsT=wpw[cb][:, k, mc:mc + 128], rhs=cvT[:, k, tsl], start=(k == 0), stop=(k == 3)),
                            reads=[("wpw", cb, k // 2)], writes=[kyc])
                    for k in range(8):
                        ph.op("pe", lambda e, pga=pga, k=k, cb=cb, mc=mc, tsl=tsl: e.matmul(
                            pga[:], lhsT=wga[cb][:, k, mc:mc + 128], rhs=uT[:, k, tsl], start=(k == 0), stop=(k == 7)),
                            reads=[("wga", cb, k // 2)], writes=[kga])
                    for k in range(8):
                        ph.op("pe", lambda e, pgc=pgc, k=k, cb=cb, mc=mc, tsl=tsl: e.matmul(
                            pgc[:], lhsT=wgc[cb][:, k, mc:mc + 128], rhs=uT[:, k, tsl], start=(k == 0), stop=(k == 7)),
                            reads=[("wgc", cb, k // 2)], writes=[kgc])
                    i2 = it % 2
                    it += 1
                    ph.op("act", lambda e, i2=i2, pga=pga: e.activation(out=sa[i2][:], in_=pga[:], func=AF.Tanh, scale=0.5), reads=[kga], writes=[("sa", i2)])
                    ph.op("act", lambda e, i2=i2, pgc=pgc: e.activation(out=sc_[i2][:], in_=pgc[:], func=AF.Tanh, scale=0.5), reads=[kgc], writes=[("sc", i2)])
                    ph.op("dve", lambda e, i2=i2, pya=pya: e.scalar_tensor_tensor(out=m1[i2][:], in0=sa[i2][:], scalar=1.0, in1=pya[:], op0=ALU.add, op1=ALU.mult),
                          reads=[("sa", i2), kya], writes=[("m1", i2)])
                    ph.op("dve", lambda e, i2=i2, pyc=pyc: e.scalar_tensor_tensor(out=sc_[i2][:], in0=sc_[i2][:], scalar=1.0, in1=pyc[:], op0=ALU.add, op1=ALU.mult),
                          reads=[("sc", i2), kyc], writes=[("sc", i2)])
                    ph.op("dve", lambda e, i2=i2, m=m, tsl=tsl: e.tensor_tensor(out=mg[:, m, tsl], in0=m1[i2][:], in1=sc_[i2][:], op=ALU.add),
                          reads=[("m1", i2), ("sc", i2)], writes=[("mg", m)])
            ph.emit()
        es_at.close()
        es_cv.close()
        esh.close()
        with contextlib.ExitStack() as esd2:
            xT = esd2.enter_context(_sbt(nc, "xT", [128, 8, HALF], F32))
            wo = [esd2.enter_context(_sbt(nc, f"wo{i}", [128, 8, 512], BF16)) for i in range(2)]
            ph = Phase(nc, nm + "d2")
            for tg in range(4):
                ph.dma("sp", xT[:, :, tg * 512:(tg + 1) * 512], xs3[:, :, tg * 512:(tg + 1) * 512], writes=[("xld", tg)])
            for cb in range(2):
                wload(ph, wo[cb][:], wout_d[l, cb], ("wo", cb), 4 if cb == 0 else 1)
            def ymm(ph, pt, pk, mo, tg):
                cb, mc = mo // 4, (mo % 4) * 128
                tsl = slice(tg * 512, (tg + 1) * 512)
                for k in range(8):
                    ph.op("pe", lambda e, k=k: e.matmul(pt[:], lhsT=wo[cb][:, k, mc:mc + 128], rhs=mg[:, k, tsl], start=(k == 0), stop=(k == 7)),
                          reads=[("wo", cb, k // 2)], writes=[pk])
            deepnorm_ln(nc, ph, esd2, ymm, xT, prm, l, 2, 6, 7, pbank, ones_b, ntg=4, conc=2)
            for tg in range(4):
                ph.dma("sp", xs3[:, :, tg * 512:(tg + 1) * 512], xT[:, :, tg * 512:(tg + 1) * 512],
                       reads=[("x", mo, tg) for mo in range(8)], writes=[("xst", tg)])
            if dbg_d.get("x1") is not None and l == 0:
                ph.dma("sp", dbg_d["x1"][hf], xT[:].rearrange("p k t -> p (k t)"), reads=[("x", mo, tg) for mo in range(8) for tg in range(4)])
            ph.emit()
    es_mg.close()

    for grp in range(2):
        g0 = grp * 1024
        with contextlib.ExitStack() as ese:
            sbe = lambda name, shape, dt=F32: ese.enter_context(_sbt(nc, name, shape, dt))
            xg = sbe("xg", [128, 8, 1024], F32)
            hm = sbe("hm", [128, 32, 1024], BF16)
            with contextlib.ExitStack() as ese1:
                sb1 = lambda name, shape, dt=F32: ese1.enter_context(_sbt(nc, name, shape, dt))
                u2 = sb1("u2", [128, 8, 1024], BF16)
                wu = [sb1(f"wu{i}", [128, 8, 512], BF16) for i in range(2)]
                rl = [sb1(f"rl{i}", [128, 512], BF16) for i in range(3)]
                ph = Phase(nc, nm + f"e{grp}")
                ph.dma("sp", xg[:], xs3[:, :, g0:g0 + 1024], writes=["xT"])
                for k in range(8):
                    ph.op("dve", lambda e, k=k: e.tensor_scalar(out=u2[:, k, :], in0=xg[:, k, :], scalar1=P(l, 3, k), scalar2=P(l, 4, k),
                                                                op0=ALU.mult, op1=ALU.add), reads=["xT"], writes=["u2"])
                rot = PsumRot(pbank[0:4], "up")
                it = 0
                for jb in range(8):
                    w, wk = wu[jb % 2], ("wu", jb % 2)
                    wload(ph, w[:], wup_d[l, jb], wk, 4 if jb == 0 else 1)
                    for jj in range(4):
                        j = jb * 4 + jj
                        for sub in range(2):
                            pt, pk = rot.next()
                            for k in range(8):
                                ph.op("pe", lambda e, pt=pt, k=k, w=w, jj=jj, sub=sub: e.matmul(
                                    pt[:], lhsT=w[:, k, jj * 128:(jj + 1) * 128], rhs=u2[:, k, sub * 512:(sub + 1) * 512], start=(k == 0), stop=(k == 7)),
                                    reads=[wk + ((k // 2),), "u2"], writes=[pk])
                            r, rk = rl[it % 3], ("rl", it % 3)
                            ph.op("act", lambda e, r=r, pt=pt: e.activation(out=r[:], in_=pt[:], func=AF.Relu), reads=[pk], writes=[rk])
                            ph.op("dve", lambda e, r=r, j=j, sub=sub: e.tensor_tensor(out=hm[:, j, sub * 512:(sub + 1) * 512], in0=r[:], in1=r[:], op=ALU.mult),
                                  reads=[rk], writes=[("hm", j)])
                            it += 1
                ph.emit()
            with contextlib.ExitStack() as ese2:
                wd = [ese2.enter_context(_sbt(nc, f"wd{i}", [128, 32, 128], BF16)) for i in range(2)]
                ph = Phase(nc, nm + f"f{grp}")
                def ymm2(ph, pt, pk, mo, tg):
                    w, wk = wd[mo % 2], ("wd", mo % 2)
                    tsl = slice(tg * 512, (tg + 1) * 512)
                    if tg == 0:
                        wload(ph, w[:], wdn_d[l, mo], wk, 4 if mo == 0 else 1)
                    for k in range(32):
                        ph.op("pe", lambda e, k=k: e.matmul(pt[:], lhsT=w[:, k, :], rhs=hm[:, k, tsl], start=(k == 0), stop=(k == 31)),
                              reads=[wk + ((k // 8),)], writes=[pk])
                deepnorm_ln(nc, ph, ese2, ymm2, xg, prm, l, 5, 8, 9, pbank, ones_b, ntg=2, conc=2)
                ph.dma("sp", xs3[:, :, g0:g0 + 1024], xg[:], reads=[("x", mo, tg) for mo in range(8) for tg in range(2)], writes=["xst"])
                ph.emit()


def ln_stats(ph, s1p, s2p, st1, st2, st3, mhalf, inv_n):
    ph.op("dve", lambda e: e.tensor_scalar(out=st1[:], in0=s1p[:], scalar1=inv_n, scalar2=None, op0=ALU.mult), reads=["s1p"], writes=["mean"])
    ph.op("dve", lambda e: e.tensor_tensor(out=st2[:], in0=st1[:], in1=st1[:], op=ALU.mult), reads=["mean"], writes=["msq"])
    ph.op("dve", lambda e: e.scalar_tensor_tensor(out=st2[:], in0=s2p[:], scalar=inv_n, in1=st2[:], op0=ALU.mult, op1=ALU.subtract),
          reads=["s2p", "msq"], writes=["var"])
    ph.op("dve", lambda e: e.tensor_scalar(out=st2[:], in0=st2[:], scalar1=0.0, scalar2=EPS, op0=ALU.max, op1=ALU.add), reads=["var"], writes=["var2"])
    ph.op("act", lambda e: e.activation(out=st2[:], in_=st2[:], func=AF.Sqrt), reads=["var2"], writes=["sd"])
    ph.op("dve", lambda e: e.reciprocal(out=st3[:], in_=st2[:]), reads=["sd"], writes=["rstd"])


def deepnorm_ln(nc, ph, es, ymm, xT, prm, l, i_gt, i_g, i_b, pbank, ones_b, ntg, conc):
    sbx = lambda name, shape, dt=F32: es.enter_context(_sbt(nc, name, shape, dt))
    tb = [sbx(f"tb{i}", [128, 512], BF16) for i in range(4)]
    tq = [sbx(f"tq{i}", [128, 512], BF16) for i in range(4)]
    pendS = []
    st1 = [sbx(f"lst1{i}", [128, 512], F32) for i in range(conc)]
    st2 = sbx("lst2", [128, 512], F32)
    st3 = [sbx(f"lst3{i}", [128, 512], F32) for i in range(conc)]
    mhalf = sbx("lmhalf", [128, 512], F32)
    ph.op("pool", lambda e: e.memset(mhalf[:], -0.5), writes=["mhalf"])
    nyb = 8 - 2 * conc
    rot = PsumRot(pbank[0:nyb], "y")
    it = 0
    for base in range(0, ntg, conc):
        for mo in range(8):
            for ci in range(conc):
                tg = base + ci
                xsl = slice(tg * 512, (tg + 1) * 512)
                s1p, s2p = pbank[nyb + 2 * ci], pbank[nyb + 2 * ci + 1]
                pt, pk = rot.next()
                ymm(ph, pt, pk, mo, tg)
                xk = ("x", mo, tg)
                ph.op("dve", lambda e, pt=pt, mo=mo, xsl=xsl: e.scalar_tensor_tensor(
                    out=xT[:, mo, xsl], in0=pt[:], scalar=prm[:, l, i_gt * 8 + mo:i_gt * 8 + mo + 1], in1=xT[:, mo, xsl], op0=ALU.mult, op1=ALU.add),
                    reads=[pk, ("xld", tg)], writes=[xk])
                i2 = it % 4
                it += 1
                ph.op("act", lambda e, i2=i2, mo=mo, xsl=xsl: e.activation(out=tb[i2][:], in_=xT[:, mo, xsl], func=AF.Copy), reads=[xk], writes=[("tb", i2)])
                ph.op("act", lambda e, i2=i2, mo=mo, xsl=xsl: e.activation(out=tq[i2][:], in_=xT[:, mo, xsl], func=AF.Square), reads=[xk], writes=[("tq", i2)])
                def stats(i2=i2, mo=mo, s1p=s1p, s2p=s2p, ci=ci):
                    ph.op("pe", lambda e: e.matmul(s1p[:], lhsT=ones_b[:], rhs=tb[i2][:], start=(mo == 0), stop=(mo == 7)),
                          reads=[("tb", i2)], writes=[("s1p", ci)])
                    ph.op("pe", lambda e: e.matmul(s2p[:], lhsT=ones_b[:], rhs=tq[i2][:], start=(mo == 0), stop=(mo == 7)),
                          reads=[("tq", i2)], writes=[("s2p", ci)])
                pendS.append(stats)
                while len(pendS) > 2:
                    pendS.pop(0)()
        while pendS:
            pendS.pop(0)()
        for ci in range(conc):
            tg = base + ci
            xsl = slice(tg * 512, (tg + 1) * 512)
            s1p, s2p = pbank[nyb + 2 * ci], pbank[nyb + 2 * ci + 1]
            a, c = st1[ci], st3[ci]
            ph.op("dve", lambda e, a=a, s1p=s1p: e.tensor_scalar(out=a[:], in0=s1p[:], scalar1=1.0 / D, scalar2=None, op0=ALU.mult),
                  reads=[("s1p", ci)], writes=[("mean", ci)])
            ph.op("dve", lambda e, a=a: e.tensor_tensor(out=st2[:], in0=a[:], in1=a[:], op=ALU.mult), reads=[("mean", ci)], writes=["msq"])
            ph.op("dve", lambda e, s2p=s2p: e.scalar_tensor_tensor(out=st2[:], in0=s2p[:], scalar=1.0 / D, in1=st2[:], op0=ALU.mult, op1=ALU.subtract),
                  reads=[("s2p", ci), "msq"], writes=["var"])
            ph.op("dve", lambda e: e.tensor_scalar(out=st2[:], in0=st2[:], scalar1=0.0, scalar2=EPS, op0=ALU.max, op1=ALU.add), reads=["var"], writes=["var2"])
            ph.op("act", lambda e: e.activation(out=st2[:], in_=st2[:], func=AF.Sqrt), reads=["var2"], writes=["sd"])
            ph.op("dve", lambda e, c=c: e.reciprocal(out=c[:], in_=st2[:]), reads=["sd"], writes=[("rstd", ci)])
            for mo in range(8):
                xk = ("x", mo, tg)
                ph.op("dve", lambda e, mo=mo, xsl=xsl, a=a: e.tensor_tensor(out=xT[:, mo, xsl], in0=xT[:, mo, xsl], in1=a[:], op=ALU.subtract),
                      reads=[xk, ("mean", ci)], writes=[xk])
                ph.op("dve", lambda e, mo=mo, xsl=xsl, c=c: e.tensor_tensor(out=xT[:, mo, xsl], in0=xT[:, mo, xsl], in1=c[:], op=ALU.mult),
                      reads=[xk, ("rstd", ci)], writes=[xk])
                ph.op("act", lambda e, mo=mo, xsl=xsl: e.activation(
                    out=xT[:, mo, xsl], in_=xT[:, mo, xsl], func=AF.Identity, scale=prm[:, l, i_g * 8 + mo:i_g * 8 + mo + 1],
                    bias=prm[:, l, i_b * 8 + mo:i_b * 8 + mo + 1]), reads=[xk], writes=[xk])


def _cols(v):
    n = v.shape[-1] // 128
    return np.swapaxes(v.reshape(v.shape[:-1] + (n, 128)), -1, -2)


def _blk(w, bw):
    Lw, K, N = w.shape
    return np.ascontiguousarray(w.reshape(Lw, K // 128, 128, N // bw, bw).transpose(0, 3, 2, 1, 4)).reshape(Lw, N // bw, 128, (K // 128) * bw)


def prep_shared(inp):
    f = lambda a: np.asarray(a, dtype=np.float32)
    pvec = np.concatenate([_cols(f(inp["ln_mix_g"])), _cols(f(inp["ln_mix_b"])), _cols(f(inp["ln_ffn_g"])), _cols(f(inp["ln_ffn_b"])),
                           _cols(f(inp["b_ada"])), _cols(f(inp["conv_dw_b"])), _cols(f(inp["conv_ln_g"])), _cols(f(inp["conv_ln_b"]))], axis=-1)
    assert pvec.shape == (L, 128, 92), pvec.shape
    dwt = f(inp["conv_dw_w"])
    dww = np.ascontiguousarray(dwt.reshape(L, KC, 4, 128).transpose(0, 3, 2, 1)).reshape(L, 128, 4 * KC)
    wbr = f(inp["w_attn_br"]).reshape(L, H, 64, 2, 512).transpose(0, 3, 2, 1, 4)
    wbr = np.ascontiguousarray(wbr).reshape(L, 2, 64, 8 * 512)
    return {
        "pvec": np.ascontiguousarray(pvec), "dww": dww,
        "wada": _blk(f(inp["w_ada"]), 512), "win": _blk(f(inp["w_in"]), 512), "wbr": wbr,
        "wpw": _blk(f(inp["w_conv_pw"]), 512), "wout": _blk(f(inp["w_out"]), 512),
        "wup": _blk(f(inp["w_up"]), 512), "wdn": _blk(f(inp["w_down"]), 128),
    }


NPV = 92
_NC_CACHE = {}


def kernel(**inputs):
    shared = prep_shared(inputs)
    x = np.asarray(inputs["x"], dtype=np.float32)
    c = np.asarray(inputs["c"], dtype=np.float32)
    pos = np.asarray(inputs["positions"], dtype=np.int32)
    if "nc" not in _NC_CACHE:
        _NC_CACHE["nc"] = build(L)
    nc = _NC_CACHE["nc"]
    in_maps = []
    for core in range(8):
        b = core % B
        m = dict(shared)
        m["x"] = np.ascontiguousarray(x[b])
        m["cT"] = np.ascontiguousarray(c[b].reshape(8, 128).T)
        m["pos"] = np.ascontiguousarray(pos[b].reshape(32, 128).T)
        in_maps.append(m)
    res = run_bass_kernel_spmd(nc, in_maps, core_ids=list(range(8)))
    return np.stack([res.results[b]["out"] for b in range(B)], axis=0).astype(np.float32)
```

```python
import contextlib
import math
import numpy as np
import concourse.bass as bass
import concourse.mybir as mybir
from concourse.bass_utils import run_bass_kernel_spmd

F32 = mybir.dt.float32
BF16 = mybir.dt.bfloat16
I32 = mybir.dt.int32
AF = mybir.ActivationFunctionType
ALU = mybir.AluOpType
AX = mybir.AxisListType

D = 1024
S = 4096
B = 4
L = 4
H = 8
HD = 64
CW = 512
KC = 31
DFF = 4096
HALF = 2048
ALPHA = (2 * L) ** 0.25
EPS = 1e-5
BIG = 30000.0
NPV = 28 + 48 + 12


class _Op:
    __slots__ = ("eng", "fn", "deps", "is_dma", "idx", "sig", "sem", "semval", "prev_slot_val")


class Phase:
    NDMA = 12

    def __init__(self, nc, name):
        self.nc, self.name = nc, name
        self.ops, self.last_w, self.readers = [], {}, {}

    def op(self, eng, fn, reads=(), writes=(), dma=False):
        o = _Op()
        o.eng, o.fn, o.is_dma = eng, fn, dma
        deps = set()
        for k in reads:
            w = self.last_w.get(k)
            if w is not None:
                deps.add(w)
        for k in writes:
            w = self.last_w.get(k)
            if w is not None:
                deps.add(w)
            deps.update(self.readers.get(k, ()))
        o.idx = len(self.ops)
        deps.discard(o.idx)
        o.deps = deps
        self.ops.append(o)
        for k in reads:
            self.readers.setdefault(k, []).append(o.idx)
        for k in writes:
            self.last_w[k] = o.idx
            self.readers[k] = []
        return o.idx

    def dma(self, eng, out, in_, reads=(), writes=(), **kw):
        return self.op(eng, lambda e: e.dma_start(out=out, in_=in_, **kw), reads, writes, dma=True)

    def emit(self):
        for suf, n in TRUNC.items():
            if self.name.endswith(suf):
                print("phase", self.name, "ops", len(self.ops), "-> trunc", n)
                self.ops = self.ops[:n]
        nc, ops = self.nc, self.ops
        engs = ["pe", "act", "dve", "pool", "sp"]
        streams = {e: [o for o in ops if o.eng == e] for e in engs}

        def skip(p, o):
            return p.eng == "pe" and o.eng == "pe" and not p.is_dma and not o.is_dma

        for o in ops:
            o.sig = o.is_dma
        for o in ops:
            for d in o.deps:
                if not skip(ops[d], o):
                    ops[d].sig = True
        if id(nc) not in _SEMS:
            esem = {e: nc.semaphore(f"trk_{e}").__enter__() for e in engs}
            dsem = {e: [nc.semaphore(f"trk_d{e}{i}").__enter__() for i in range(self.NDMA)] for e in ("sp", "pool")}
            _SEMS[id(nc)] = (esem, dsem, {e: [0] * self.NDMA for e in dsem}, {e: 0 for e in dsem})
        esem, dsem, dcount, dnext = _SEMS[id(nc)]
        with nc.Block() as b0:
            def clr(e):
                for s_ in list(esem.values()):
                    e.sem_clear(s_)
            b0.gpsimd(clr)
        if True:
            ecount = {e: 0 for e in engs}
            dstart = {e: list(dcount[e]) for e in dsem}
            for o in ops:
                if o.is_dma:
                    s = dnext[o.eng] % self.NDMA
                    dnext[o.eng] += 1
                    o.prev_slot_val = dcount[o.eng][s]
                    dcount[o.eng][s] += 16
                    o.sem, o.semval = dsem[o.eng][s], dcount[o.eng][s]
                elif o.sig:
                    ecount[o.eng] += 1
                    o.sem, o.semval = esem[o.eng], ecount[o.eng]
                else:
                    o.sem, o.semval = None, 0
            with nc.Block() as blk:
                def run(e_name):
                    def body(e):
                        waited = {}
                        for o in streams[e_name]:
                            need = {}
                            for d in o.deps:
                                p = ops[d]
                                if skip(p, o):
                                    continue
                                key = id(p.sem)
                                if key not in need or need[key][1] < p.semval:
                                    need[key] = (p.sem, p.semval)
                            if o.is_dma and o.prev_slot_val > 0:
                                key = id(o.sem)
                                if key not in need or need[key][1] < o.prev_slot_val:
                                    need[key] = (o.sem, o.prev_slot_val)
                            for key, (s, v) in need.items():
                                if waited.get(key, 0) >= v:
                                    continue
                                e.wait_ge(s, v)
                                waited[key] = v
                            ins = o.fn(e)
                            if o.is_dma:
                                ins.then_inc(o.sem, 16)
                            elif o.sig:
                                ins.then_inc(o.sem, 1)
                        if e_name in dsem:
                            for i, s in enumerate(dsem[e_name]):
                                v = dcount[e_name][i]
                                if v > dstart[e_name][i] and waited.get(id(s), 0) < v:
                                    e.wait_ge(s, v)
                    return body
                for e_name, reg in (("pe", blk.tensor), ("act", blk.scalar), ("dve", blk.vector),
                                    ("pool", blk.gpsimd), ("sp", blk.sync)):
                    if streams[e_name]:
                        reg(run(e_name))
        if STOP[0] is not None and self.name.endswith(STOP[0]):
            raise StopBuild()


_UID = [0]


def _sbt(nc, name, shape, dt):
    _UID[0] += 1
    return nc.sbuf_tensor(f"{name}_u{_UID[0]}", shape, dt)


_SEMS = {}


class StopBuild(Exception):
    pass


STOP = [None]
TRUNC = {}


def wload(ph, dst, src, key, nsplit, npieces=4):
    K, N = dst.shape[1], dst.shape[2]
    step = K // nsplit
    for pc in range(nsplit):
        keys = [key + (q,) for q in range(pc * npieces // nsplit, (pc + 1) * npieces // nsplit)]
        ph.dma("pool", dst[:, pc * step:(pc + 1) * step, :].rearrange("p k n -> p (k n)"),
               src[:, pc * step * N:(pc + 1) * step * N], writes=keys, max_dma_last_dim=8192)


class PsumRot:
    def __init__(self, tiles, tag):
        self.tiles, self.tag, self.i = tiles, tag, 0

    def next(self):
        t = self.tiles[self.i % len(self.tiles)]
        k = (self.tag, self.i % len(self.tiles))
        self.i += 1
        return t, k


def build(depth=L, dbg=None):
    nc = bass.Bass("TRN2", target_bir_lowering=False)
    dt_in = lambda name, shape, dt=F32: nc.dram_tensor(name, shape, dt, kind="ExternalInput").ap()
    x_d = dt_in("x", [S, D])
    cT_d = dt_in("cT", [128, 8])
    pos_d = dt_in("pos", [128, 32], I32)
    pv_d = dt_in("pvec", [L, 128, NPV])
    dw_d = dt_in("dww", [L, 128, 4 * KC])
    wada_d = dt_in("wada", [L, 12, 128, 8 * 512])
    win_d = dt_in("win", [L, 9, 128, 8 * 512])
    wbr_d = dt_in("wbr", [L, 2, 64, 8 * 512])
    wpw_d = dt_in("wpw", [L, 2, 128, 4 * 512])
    wout_d = dt_in("wout", [L, 2, 128, 8 * 512])
    wup_d = dt_in("wup", [L, 8, 128, 8 * 512])
    wdn_d = dt_in("wdn", [L, 8, 128, 32 * 128])
    out_d = nc.dram_tensor("out", [S, D], F32, kind="ExternalOutput").ap()
    dbg_d = {}
    if dbg:
        for name, shape, dt in dbg:
            dbg_d[name] = nc.dram_tensor("dbg_" + name, shape, dt, kind="ExternalOutput").ap()

    xs_d = nc.dram_tensor("xs", [2, 128, 8 * HALF], F32).ap()
    qT_d = nc.dram_tensor("qTd", [H, 64, HALF], BF16).ap()
    kT_d = nc.dram_tensor("kTd", [H, 64, S], BF16).ap()
    v_d = nc.dram_tensor("vd", [S, 512], BF16).ap()
    bT_d = nc.dram_tensor("bTd", [H, 16, HALF], BF16).ap()

    es = contextlib.ExitStack()
    sb = lambda name, shape, dt=F32: es.enter_context(_sbt(nc, name, shape, dt))
    ps = lambda name, shape, dt=F32: es.enter_context(nc.psum_tensor(name, shape, dt))

    try:
      with es:
        _build_body(nc, es, sb, ps, locals(), depth)
    except StopBuild:
        pass
    return nc


def _build_body(nc, es, sb, ps, G, depth):
    x_d, cT_d, pos_d, pv_d, dw_d, wada_d, out_d = G["x_d"], G["cT_d"], G["pos_d"], G["pv_d"], G["dw_d"], G["wada_d"], G["out_d"]
    win_d, wbr_d, wpw_d, wout_d, wup_d, wdn_d = G["win_d"], G["wbr_d"], G["wpw_d"], G["wout_d"], G["wup_d"], G["wdn_d"]
    xs_d, qT_d, kT_d, v_d, bT_d, dbg_d = G["xs_d"], G["qT_d"], G["kT_d"], G["v_d"], G["bT_d"], G["dbg_d"]
    if True:
        ident_b = sb("ident_b", [128, 128], BF16)
        ident_f = sb("ident_f", [128, 128], F32)
        ones_b = sb("ones_b", [128, 128], BF16)
        trimask = sb("trimask", [128, 128], BF16)
        indic = sb("indic", [16, S], BF16)
        cos_t = sb("cos_t", [128, 32, 8], F32)
        sin_t = sb("sin_t", [128, 32, 8], F32)
        pv = sb("pv", [128, L, NPV], F32)
        prm = sb("prm", [128, L, 96], F32)
        dww = sb("dww", [128, L, 4 * KC], F32)
        kmT = sb("kmT", [64, H, 16], BF16)
        halo = sb("halo", [128, 4, 32], BF16)
        pbank = [ps(f"pb{i}", [128, 512], F32) for i in range(8)]

        def P(l, i, k=None):
            if k is None:
                return prm[:, l, i * 8:(i + 1) * 8]
            return prm[:, l, i * 8 + k:i * 8 + k + 1]


        with contextlib.ExitStack() as es0:
            sb0 = lambda name, shape, dt=F32: es0.enter_context(_sbt(nc, name, shape, dt))
            posi = sb0("posi", [128, 32], I32)
            posf = sb0("posf", [128, 32], F32)
            ang = sb0("ang", [128, 32, 8], F32)
            kk = sb0("kk", [128, 32, 8], F32)
            kki = sb0("kki", [128, 32, 8], I32)
            tmpa = sb0("tmpa", [128, 32, 8], F32)
            iot = sb0("iot", [128, 128], F32)
            iop = sb0("iop", [128, 1], F32)
            iot_i = sb0("iot_i", [128, 128], I32)
            iop_i = sb0("iop_i", [128, 1], I32)
            indi = sb0("indi", [16, S], I32)
            cT = sb0("cTs", [128, 8], F32)
            cth = sb0("cth", [128, 8], F32)
            cTb = sb0("cTb", [128, 8], BF16)
            wa = [sb0(f"wa{i}", [128, 8, 512], BF16) for i in range(2)]
            adas = sb0("adas", [128, L, 48], F32)
            indf = sb0("indf", [16, S], F32)
            halfpi = sb0("halfpi", [128, 1], F32)
            ph = Phase(nc, "p0")
            ph.op("pool", lambda e: e.iota(iot_i[:], pattern=[[1, 128]], base=0, channel_multiplier=0), writes=["iot_i"])
            ph.op("pool", lambda e: e.iota(iop_i[:], pattern=[[0, 1]], base=0, channel_multiplier=1), writes=["iop_i"])
            ph.op("dve", lambda e: e.tensor_copy(out=iot[:], in_=iot_i[:]), reads=["iot_i"], writes=["iot"])
            ph.op("dve", lambda e: e.tensor_copy(out=iop[:], in_=iop_i[:]), reads=["iop_i"], writes=["iop"])
            ph.op("dve", lambda e: e.memset(kmT[:], 0.0), writes=["kmT"])
            ph.op("dve", lambda e: e.tensor_scalar(out=ident_f[:], in0=iot[:], scalar1=iop[:, 0:1], scalar2=None,
                                                   op0=ALU.is_equal), reads=["iot", "iop"], writes=["idf"])
            ph.op("dve", lambda e: e.tensor_copy(out=ident_b[:], in_=ident_f[:]), reads=["idf"], writes=["idb"])
            ph.op("dve", lambda e: e.memset(ones_b[:], 1.0), writes=["ones"])
            ph.op("dve", lambda e: e.tensor_scalar(out=trimask[:], in0=iot[:], scalar1=iop[:, 0:1], scalar2=-BIG,
                                                   op0=ALU.is_lt, op1=ALU.mult), reads=["iot", "iop"], writes=["tri"])
            tmpi = sb0("tmpi", [16, S], F32)
            ph.op("pool", lambda e: e.iota(indi[:], pattern=[[1, S]], base=0, channel_multiplier=-256), writes=["indi"])
            ph.op("dve", lambda e: e.tensor_copy(out=indf[:], in_=indi[:]), reads=["indi"], writes=["indf"])
            ph.op("dve", lambda e: e.tensor_scalar(out=tmpi[:], in0=indf[:], scalar1=255.5, scalar2=None, op0=ALU.is_lt),
                  reads=["indf"], writes=["tmpi1"])
            ph.op("dve", lambda e: e.tensor_scalar(out=indf[:], in0=indf[:], scalar1=-0.5, scalar2=None, op0=ALU.is_gt),
                  reads=["indf", "tmpi1"], writes=["indf1"])
            ph.op("dve", lambda e: e.tensor_tensor(out=indic[:], in0=indf[:], in1=tmpi[:], op=ALU.mult),
                  reads=["indf1", "tmpi1"], writes=["indic"])
            ph.dma("sp", posi[:], pos_d, writes=["posi"])
            ph.op("dve", lambda e: e.tensor_copy(out=posf[:], in_=posi[:]), reads=["posi"], writes=["posf"])
            for i in range(8):
                invf = float(np.float32(500000.0) ** np.float32(-(2.0 * i) / 16.0))
                ph.op("dve", lambda e, i=i, invf=invf: e.tensor_scalar(out=ang[:, :, i], in0=posf[:], scalar1=invf,
                                                                       scalar2=None, op0=ALU.mult),
                      reads=["posf"], writes=["ang"])
            ph.op("dve", lambda e: e.tensor_scalar(out=kk[:], in0=ang[:], scalar1=float(1.0 / (2 * math.pi)), scalar2=None,
                                                   op0=ALU.mult), reads=["ang"], writes=["kk"])
            ph.op("dve", lambda e: e.tensor_copy(out=kki[:], in_=kk[:]), reads=["kk"], writes=["kki"])
            ph.op("dve", lambda e: e.tensor_copy(out=kk[:], in_=kki[:]), reads=["kki"], writes=["kk2"])
            C1, C2 = 6.28125, float(2 * math.pi - 6.28125)
            ph.op("dve", lambda e: e.scalar_tensor_tensor(out=ang[:], in0=kk[:], scalar=-C1, in1=ang[:], op0=ALU.mult,
                                                          op1=ALU.add), reads=["kk2", "ang"], writes=["ang"])
            ph.op("dve", lambda e: e.scalar_tensor_tensor(out=ang[:], in0=kk[:], scalar=-C2, in1=ang[:], op0=ALU.mult,
                                                          op1=ALU.add), reads=["kk2", "ang"], writes=["ang"])
            ph.op("dve", lambda e: e.tensor_scalar(out=tmpa[:], in0=ang[:], scalar1=math.pi, scalar2=-2 * math.pi,
                                                   op0=ALU.is_gt, op1=ALU.mult), reads=["ang"], writes=["tmpa"])
            ph.op("dve", lambda e: e.tensor_tensor(out=ang[:], in0=ang[:], in1=tmpa[:], op=ALU.add),
                  reads=["ang", "tmpa"], writes=["ang"])
            ph.op("dve", lambda e: e.tensor_scalar(out=tmpa[:], in0=ang[:], scalar1=-math.pi, scalar2=2 * math.pi,
                                                   op0=ALU.is_lt, op1=ALU.mult), reads=["ang"], writes=["tmpa"])
            ph.op("dve", lambda e: e.tensor_tensor(out=ang[:], in0=ang[:], in1=tmpa[:], op=ALU.add),
                  reads=["ang", "tmpa"], writes=["ang"])
            ph.op("act", lambda e: e.activation(out=sin_t[:], in_=ang[:], func=AF.Sin), reads=["ang"], writes=["sin"])
            ph.op("dve", lambda e: e.tensor_scalar(out=tmpa[:], in0=ang[:], scalar1=-1.0, scalar2=None, op0=ALU.mult),
                  reads=["ang"], writes=["tmpa"])
            ph.op("dve", lambda e: e.tensor_tensor(out=tmpa[:], in0=tmpa[:], in1=ang[:], op=ALU.max),
                  reads=["ang", "tmpa"], writes=["tmpa"])
            ph.op("dve", lambda e: e.memset(halfpi[:], math.pi / 2), writes=["halfpi"])
            ph.op("act", lambda e: e.activation(out=cos_t[:], in_=tmpa[:], func=AF.Sin, scale=-1.0, bias=halfpi[:, 0:1]),
                  reads=["tmpa", "halfpi"], writes=["cos"])
            ph.dma("sp", pv[:], pv_d.rearrange("l p n -> p l n"), writes=["pv"])
            ph.dma("sp", dww[:], dw_d.rearrange("l p n -> p l n"), writes=["dww"])
            ph.dma("sp", cT[:], cT_d, writes=["cT"])
            ph.op("act", lambda e: e.activation(out=cth[:], in_=cT[:], func=AF.Tanh, scale=0.5), reads=["cT"], writes=["cth"])
            ph.op("dve", lambda e: e.scalar_tensor_tensor(out=cth[:], in0=cth[:], scalar=1.0, in1=cT[:], op0=ALU.add,
                                                          op1=ALU.mult), reads=["cth", "cT"], writes=["cth2"])
            ph.op("dve", lambda e: e.tensor_scalar(out=cTb[:], in0=cth[:], scalar1=0.5, scalar2=None, op0=ALU.mult),
                  reads=["cth2"], writes=["cTb"])
            rot = PsumRot(pbank[0:2], "adaps")
            it = 0
            for l in range(depth):
                for cb in range(12):
                    w = wa[it % 2]
                    wk = ("wa", it % 2)
                    it += 1
                    ph.dma("pool", w[:].rearrange("p k n -> p (k n)"), wada_d[l, cb], writes=[wk], max_dma_last_dim=8192)
                    pt, pk = rot.next()
                    for m in range(4):
                        for k in range(8):
                            ph.op("pe", lambda e, w=w, m=m, k=k, pt=pt: e.matmul(
                                pt[:, m:m + 1], lhsT=w[:, k, m * 128:(m + 1) * 128], rhs=cTb[:, k:k + 1],
                                start=(k == 0), stop=(k == 7)), reads=[wk, "cTb"], writes=[pk])
                    ph.op("dve", lambda e, l=l, cb=cb, pt=pt: e.tensor_tensor(
                        out=adas[:, l, cb * 4:(cb + 1) * 4], in0=pt[:, 0:4], in1=pv[:, l, 32 + cb * 4:32 + (cb + 1) * 4],
                        op=ALU.add), reads=[pk, "pv"], writes=["adas"])
            for l in range(depth):
                A = lambda i: adas[:, l, i * 8:(i + 1) * 8]
                def ts(out, in0, s1, s2, op0, op1=ALU.bypass, l=l):
                    ph.op("dve", lambda e: e.tensor_scalar(out=out, in0=in0, scalar1=s1, scalar2=s2, op0=op0, op1=op1),
                          reads=["adas", "pv", "dww"], writes=["prm"])
                ts(P(l, 0), A(1), 1.0, 1.0 / ALPHA, ALU.add, ALU.mult)
                ts(P(l, 1), A(0), 1.0, None, ALU.mult)
                ts(P(l, 2), A(2), 1.0, 0.5, ALU.add, ALU.mult)
                ts(P(l, 3), A(4), 1.0, 1.0 / ALPHA, ALU.add, ALU.mult)
                ts(P(l, 4), A(3), 1.0, None, ALU.mult)
                ts(P(l, 5), A(5), 1.0, None, ALU.add)
                ts(P(l, 6), pv[:, l, 0:8], ALPHA, None, ALU.mult)
                ts(P(l, 7), pv[:, l, 8:16], ALPHA, None, ALU.mult)
                ts(P(l, 8), pv[:, l, 16:24], ALPHA, None, ALU.mult)
                ts(P(l, 9), pv[:, l, 24:32], ALPHA, None, ALU.mult)
                ts(prm[:, l, 80:84], pv[:, l, 80:84], 1.0, None, ALU.mult)
                ts(prm[:, l, 84:88], pv[:, l, 84:88], 0.5, None, ALU.mult)
                ts(prm[:, l, 88:92], pv[:, l, 88:92], 0.5, None, ALU.mult)
                ph.op("dve", lambda e, l=l: e.tensor_scalar(out=dww[:, l, :], in0=dww[:, l, :], scalar1=0.5, scalar2=None,
                                                            op0=ALU.mult), reads=["dww"], writes=["dww"])
            ph.emit()

        for hf in range(2):
            with contextlib.ExitStack() as es1:
                xT = es1.enter_context(_sbt(nc, "xT", [128, 8, HALF], F32))
                xin = [es1.enter_context(_sbt(nc, f"xin{i}", [128, D], F32)) for i in range(2)]
                ph = Phase(nc, f"px{hf}")
                rot = PsumRot(pbank[0:4], "xps")
                for tt in range(16):
                    xi, xk = xin[tt % 2], ("xin", tt % 2)
                    t0 = hf * HALF + tt * 128
                    ph.dma("sp", xi[:], x_d[t0:t0 + 128, :], writes=[xk])
                    for kq in range(2):
                        pt, pk = rot.next()
                        for j in range(4):
                            k = kq * 4 + j
                            ph.op("pe", lambda e, pt=pt, j=j, k=k, xi=xi: e.transpose(
                                pt[:, j * 128:(j + 1) * 128], xi[:, k * 128:(k + 1) * 128], ident_f[:]),
                                reads=[xk], writes=[pk])
                        eng = "act" if kq == 0 else "dve"
                        dst = xT[:, kq * 4:(kq + 1) * 4, tt * 128:(tt + 1) * 128]
                        src = pt[:].rearrange("p (j t) -> p j t", j=4)
                        if eng == "act":
                            ph.op("act", lambda e, dst=dst, src=src: e.activation(out=dst, in_=src, func=AF.Copy, scale=ALPHA),
                                  reads=[pk], writes=[("xT", tt)])
                        else:
                            ph.op("dve", lambda e, dst=dst, src=src: e.tensor_scalar(out=dst, in0=src, scalar1=ALPHA, scalar2=None,
                                                                                     op0=ALU.mult), reads=[pk], writes=[("xT", tt)])
                ph.dma("sp", xs_d[hf], xT[:].rearrange("p k t -> p (k t)"), reads=[("xT", tt) for tt in range(16)], writes=["xs"])
                ph.emit()

        for l in range(depth):
            for hf in range(2):
                layer_half(nc, l, hf, locals())

        for hf in range(2):
            with contextlib.ExitStack() as es1:
                xT = es1.enter_context(_sbt(nc, "xT", [128, 8, HALF], F32))
                xo = [es1.enter_context(_sbt(nc, f"xo{i}", [128, D], F32)) for i in range(2)]
                ph = Phase(nc, f"pf{hf}")
                ph.dma("sp", xT[:].rearrange("p k t -> p (k t)"), xs_d[hf], writes=["xT"])
                rot = PsumRot(pbank[0:4], "ops")
                for tt in range(16):
                    xi, xk = xo[tt % 2], ("xo", tt % 2)
                    for kq in range(2):
                        pt, pk = rot.next()
                        for j in range(4):
                            k = kq * 4 + j
                            ph.op("pe", lambda e, pt=pt, j=j, k=k, tt=tt: e.transpose(
                                pt[:, j * 128:(j + 1) * 128], xT[:, k, tt * 128:(tt + 1) * 128], ident_f[:]),
                                reads=["xT"], writes=[pk])
                        dst = xi[:, kq * 512:(kq + 1) * 512]
                        if kq == 0:
                            ph.op("act", lambda e, dst=dst, pt=pt: e.activation(out=dst, in_=pt[:], func=AF.Copy, scale=1.0 / ALPHA),
                                  reads=[pk], writes=[(xk, kq)])
                        else:
                            ph.op("dve", lambda e, dst=dst, pt=pt: e.tensor_scalar(out=dst, in0=pt[:], scalar1=1.0 / ALPHA, scalar2=None,
                                                                                   op0=ALU.mult), reads=[pk], writes=[(xk, kq)])
                    t0 = hf * HALF + tt * 128
                    ph.dma("sp", out_d[t0:t0 + 128, :], xi[:], reads=[(xk, 0), (xk, 1)], writes=["out"])
                ph.emit()
    return nc


def layer_half(nc, l, hf, env):
    g = env
    prm, dww, pbank = g["prm"], g["dww"], g["pbank"]
    ident_b, ones_b, trimask, indic = g["ident_b"], g["ones_b"], g["trimask"], g["indic"]
    cos_t, sin_t, kmT, halo = g["cos_t"], g["sin_t"], g["kmT"], g["halo"]
    xs_d, qT_d, kT_d, v_d, bT_d = g["xs_d"], g["qT_d"], g["kT_d"], g["v_d"], g["bT_d"]
    win_d, wbr_d, wpw_d, wout_d, wup_d, wdn_d = g["win_d"], g["wbr_d"], g["wpw_d"], g["wout_d"], g["wup_d"], g["wdn_d"]
    dbg_d = g["dbg_d"]
    P = g["P"]
    T0 = hf * HALF
    nm = f"l{l}h{hf}"
    xs3 = xs_d[hf].rearrange("p (k t) -> p k t", k=8)
    es_mg = contextlib.ExitStack()
    mg = es_mg.enter_context(_sbt(nc, "mg", [128, 8, HALF], BF16))
    esh = contextlib.ExitStack()
    uT = esh.enter_context(_sbt(nc, "uT", [128, 8, HALF], BF16))
    es_x = contextlib.ExitStack()
    xT = es_x.enter_context(_sbt(nc, "xT", [128, 8, HALF], F32))
    if True:
        with contextlib.ExitStack() as esa:
            sba = lambda name, shape, dt=F32: esa.enter_context(_sbt(nc, name, shape, dt))
            wqkv = sba("wqkv", [128, 3, 8, 512], BF16)
            qk_sb = [sba(f"qk_sb{i}", [128, 512], BF16) for i in range(4)]
            v_sb = [sba(f"v_sb{i}", [128, 512], BF16) for i in range(2)]
            rtmp = [sba(f"rtmp{i}", [128, H, 8], F32) for i in range(8)]
            qT_st = sba("qT_st", [64, H, 512], BF16)
            kT_st = sba("kT_st", [64, H, 512], BF16)
            kms = sba("kms", [64, 16], F32)
            gsb = sba("gsb", [128, H, 16], F32)
            top8 = sba("top8", [128, H, 8], F32)
            msk = sba("msk", [128, H, 16], F32)
            bias_sb = sba("bias_sb", [128, H, 16], BF16)
            bT_st = sba("bT_st", [16, H, 512], BF16)
            VB = sba("VB", [128, 8, 16], F32)
            NB = sba("NB", [128, 8, 16], F32)
            VS = sba("VS", [128, 8, 16], F32)
            ph = Phase(nc, nm + "a")
            ph.dma("sp", xT[:], xs3, writes=["xT"])
            for i in range(3):
                wload(ph, wqkv[:, i], win_d[l, i], ("wqkv", i), 4)
            own0 = 8 * hf
            ph.op("pool", lambda e: e.memset(VB[:], -1e30), writes=["VB"])
            ph.op("pool", lambda e: e.memset(NB[:], -BIG), writes=["NB"])
            ph.op("pool", lambda e: e.memset(VS[:], -BIG), writes=["VS"])
            for ob in range(8):
                own = own0 + ob
                if own > 0:
                    ph.op("pool", lambda e, ob=ob, own=own: e.memset(VB[:, ob, 0:own], 0.0), reads=["VB"], writes=["VB"])
                ph.op("pool", lambda e, ob=ob, own=own: e.memset(VS[:, ob, 0:own + 1], 0.0), reads=["VS"], writes=["VS"])
                ph.op("pool", lambda e, ob=ob, own=own: e.memset(NB[:, ob, own:own + 1], 0.0), reads=["NB"], writes=["NB"])
            for k in range(8):
                ph.op("dve", lambda e, k=k: e.tensor_scalar(out=uT[:, k, :], in0=xT[:, k, :], scalar1=P(l, 0, k), scalar2=P(l, 1, k),
                                                            op0=ALU.mult, op1=ALU.add), reads=["xT"], writes=[("uT", k)])
            rot = PsumRot(pbank[0:3], "qkv")
            trot = PsumRot(pbank[3:5], "tr")
            grot = PsumRot(pbank[5:7], "gate")
            pendA = []
            for tg in range(4):
                for t4 in range(4):
                    tt = tg * 4 + t4
                    gt = hf * 16 + tt
                    tsl = slice(tt * 128, (tt + 1) * 128)
                    for blk in range(3):
                        pt, pk = rot.next()
                        for k in range(8):
                            ph.op("pe", lambda e, pt=pt, k=k, blk=blk, tsl=tsl: e.matmul(
                                pt[:], lhsT=uT[:, k, tsl], rhs=wqkv[:, blk, k, :], start=(k == 0), stop=(k == 7)),
                                reads=[("wqkv", blk, k // 2), ("uT", k)], writes=[pk])
                        if blk == 0:
                            while pendA:
                                pendA.pop(0)()
                        if blk == 2:
                            vs, vk = v_sb[tt % 2], ("v_sb", tt % 2)
                            ph.op("act", lambda e, vs=vs, pt=pt: e.activation(out=vs[:], in_=pt[:], func=AF.Copy), reads=[pk], writes=[vk])
                            ph.dma("sp", v_d[T0 + tt * 128:T0 + (tt + 1) * 128, :], vs[:], reads=[vk], writes=["v_d"])
                            continue
                        qs, qk = qk_sb[blk * 2 + tt % 2], ("qk_sb", blk * 2 + tt % 2)
                        p3 = pt[:].rearrange("p (h d) -> p h d", h=H)
                        q3 = qs[:].rearrange("p (h d) -> p h d", h=H)
                        ph.op("dve", lambda e, q3=q3, p3=p3: e.tensor_copy(out=q3[:, :, 16:64], in_=p3[:, :, 16:64]),
                              reads=[pk], writes=[(qk, "nr")])
                        cosb = cos_t[:, gt, :].unsqueeze(1).to_broadcast([128, H, 8])
                        sinb = sin_t[:, gt, :].unsqueeze(1).to_broadcast([128, H, 8])
                        x1, x2 = p3[:, :, 0:8], p3[:, :, 8:16]
                        r = rtmp[blk * 4:blk * 4 + 4]
                        ro = blk * 4
                        def tt_(out, a, b, op, rk, wk, extra_r=()):
                            ph.op("dve", lambda e: e.tensor_tensor(out=out, in0=a, in1=b, op=op), reads=list(rk) + list(extra_r), writes=wk)
                        tt_(r[0][:], x1, cosb, ALU.mult, [pk], [("rt", ro + 0)])
                        tt_(r[1][:], x2, sinb, ALU.mult, [pk], [("rt", ro + 1)])
                        tt_(r[2][:], x2, cosb, ALU.mult, [pk], [("rt", ro + 2)])
                        tt_(r[3][:], x1, sinb, ALU.mult, [pk], [("rt", ro + 3)])
                        tt_(q3[:, :, 0:8], r[0][:], r[1][:], ALU.subtract, [("rt", ro + 0), ("rt", ro + 1)], [(qk, "r1")])
                        tt_(q3[:, :, 8:16], r[2][:], r[3][:], ALU.add, [("rt", ro + 2), ("rt", ro + 3)], [(qk, "r2")])
                        def trans(qs=qs, qk=qk, blk=blk, t4=t4):
                            tp, tk = trot.next()
                            tpb = tp[:].bitcast(BF16)
                            for h in range(H):
                                ph.op("pe", lambda e, tpb=tpb, h=h, qs=qs: e.transpose(
                                    tpb[0:64, h * 128:(h + 1) * 128], qs[:, h * 64:(h + 1) * 64], ident_b[:]),
                                    reads=[(qk, "nr"), (qk, "r1"), (qk, "r2")], writes=[tk])
                            st = qT_st if blk == 0 else kT_st
                            stk = ("qT_st" if blk == 0 else "kT_st")
                            ph.op("act", lambda e, st=st, tpb=tpb, t4=t4: e.activation(
                                out=st[:, :, t4 * 128:(t4 + 1) * 128], in_=tpb[0:64, 0:1024].rearrange("p (h t) -> p h t", h=H), func=AF.Copy),
                                reads=[tk], writes=[(stk, t4)])
                        pendA.append(trans)
                while pendA:
                    pendA.pop(0)()
                ph.dma("sp", kT_d[:, :, T0 + tg * 512:T0 + (tg + 1) * 512].rearrange("h d t -> d h t"), kT_st[:],
                       reads=[("kT_st", i) for i in range(4)], writes=["kT_d"])
                ph.dma("sp", qT_d[:, :, tg * 512:(tg + 1) * 512].rearrange("h d t -> d h t"), qT_st[:],
                       reads=[("qT_st", i) for i in range(4)], writes=["qT_d"])
                b0 = own0 + 2 * tg
                ph.op("dve", lambda e: e.tensor_reduce(out=kms[:], in_=kT_st[:].rearrange("p h (b t) -> p (h b) t", b=2),
                                                       axis=AX.X, op=ALU.add), reads=[("kT_st", i) for i in range(4)], writes=["kms"])
                ph.op("dve", lambda e, b0=b0: e.tensor_scalar(out=kmT[:, :, b0:b0 + 2], in0=kms[:].rearrange("p (h b) -> p h b", b=2),
                                                              scalar1=1.0 / 256, scalar2=None, op0=ALU.mult), reads=["kms"], writes=["kmT"])
                for t4 in range(4):
                    tt = tg * 4 + t4
                    ob = tt // 2
                    gp, gk = grot.next()
                    for h in range(H):
                        ph.op("pe", lambda e, gp=gp, h=h, t4=t4: e.matmul(
                            gp[:, h * 16:(h + 1) * 16], lhsT=qT_st[:, h, t4 * 128:(t4 + 1) * 128], rhs=kmT[:, h, :], start=True, stop=True),
                            reads=[("qT_st", t4), "kmT"], writes=[gk])
                    g3 = gp[:, 0:128].rearrange("p (h s) -> p h s", h=H)
                    bc = lambda t: t[:, ob, :].unsqueeze(1).to_broadcast([128, H, 16])
                    ph.op("dve", lambda e, g3=g3, ob=ob: e.tensor_tensor(out=gsb[:], in0=g3, in1=VB[:, ob, :].unsqueeze(1).to_broadcast([128, H, 16]),
                                                                         op=ALU.add), reads=[gk, "VB"], writes=["gsb"])
                    for h in range(H):
                        ph.op("dve", lambda e, h=h: e.max(out=top8[:, h, :], in_=gsb[:, h, :]), reads=["gsb"], writes=[("top8", h)])
                    ph.op("dve", lambda e: e.tensor_tensor(out=msk[:], in0=gsb[:], in1=top8[:, :, 2:3].to_broadcast([128, H, 16]), op=ALU.is_lt),
                          reads=["gsb"] + [("top8", h) for h in range(H)], writes=["msk"])
                    ph.op("dve", lambda e, ob=ob: e.tensor_tensor(out=msk[:], in0=msk[:], in1=NB[:, ob, :].unsqueeze(1).to_broadcast([128, H, 16]),
                                                                  op=ALU.mult), reads=["msk", "NB"], writes=["msk"])
                    ph.op("dve", lambda e, ob=ob: e.tensor_tensor(out=bias_sb[:], in0=msk[:], in1=VS[:, ob, :].unsqueeze(1).to_broadcast([128, H, 16]),
                                                                  op=ALU.add), reads=["msk", "VS"], writes=["bias_sb"])
                    tp, tk = trot.next()
                    tpb = tp[:].bitcast(BF16)
                    for h in range(H):
                        ph.op("pe", lambda e, tpb=tpb, h=h: e.transpose(tpb[0:16, h * 128:(h + 1) * 128], bias_sb[:, h, :], ident_b[:]),
                              reads=["bias_sb"], writes=[tk])
                    ph.op("act", lambda e, tpb=tpb, t4=t4: e.activation(
                        out=bT_st[:, :, t4 * 128:(t4 + 1) * 128], in_=tpb[0:16, 0:1024].rearrange("p (h t) -> p h t", h=H), func=AF.Copy),
                        reads=[tk], writes=[("bT_st", t4)])
                ph.dma("sp", bT_d[:, :, tg * 512:(tg + 1) * 512].rearrange("h s t -> s h t"), bT_st[:],
                       reads=[("bT_st", i) for i in range(4)], writes=["bT_d"])
            ph.emit()
        es_x.close()
        es_cv = contextlib.ExitStack()
        cvT = es_cv.enter_context(_sbt(nc, "cvT", [128, 4, HALF], BF16))

        with contextlib.ExitStack() as esb:
            sbb = lambda name, shape, dt=F32: esb.enter_context(_sbt(nc, name, shape, dt))
            hT = sbb("hT", [128, 4, 32 + HALF], BF16)
            wgl = sbb("wgl", [128, 2, 8, 512], BF16)
            dmat = sbb("dmat", [128, 4 * KC, 128], BF16)
            sg = [sbb(f"sg{i}", [128, 512], F32) for i in range(2)]
            cv = sbb("cv", [128, 4, 512], F32)
            cvb = sbb("cvb", [128, 4, 512], BF16)
            csq = sbb("csq", [128, 4, 512], BF16)
            st1 = sbb("st1", [128, 512], F32)
            st2 = sbb("st2", [128, 512], F32)
            st3 = sbb("st3", [128, 512], F32)
            mhalf = sbb("mhalf", [128, 512], F32)
            zt = [sbb(f"zt{i}", [128, 512], F32) for i in range(2)]
            th = [sbb(f"th{i}", [128, 512], F32) for i in range(2)]
            ph = Phase(nc, nm + "b")
            for i in range(2):
                wload(ph, wgl[:, i], win_d[l, 3 + i], ("wgl", i), 4)
            ph.op("pool", lambda e: e.memset(mhalf[:], -0.5), writes=["mhalf"])
            for j in range(4 * KC):
                ph.op("dve", lambda e, j=j: e.tensor_scalar(out=dmat[:, j, :], in0=ident_b[:], scalar1=dww[:, l, j:j + 1], scalar2=None,
                                                          op0=ALU.mult), writes=[("dmat", j)])
            if hf == 0:
                ph.op("dve", lambda e: e.memset(hT[:, :, 0:32], 0.0), writes=["hT_halo"])
            else:
                ph.op("dve", lambda e: e.tensor_copy(out=hT[:, :, 0:32], in_=halo[:]), writes=["hT_halo"])
            rot = PsumRot(pbank[0:4], "glu")
            for tg in range(4):
                tsl = slice(tg * 512, (tg + 1) * 512)
                for c in range(4):
                    pa, pak = rot.next()
                    pg, pgk = rot.next()
                    for which, pt, pk in ((0, pa, pak), (1, pg, pgk)):
                        for k in range(8):
                            ph.op("pe", lambda e, pt=pt, k=k, which=which, c=c, tsl=tsl: e.matmul(
                                pt[:], lhsT=wgl[:, which, k, c * 128:(c + 1) * 128], rhs=uT[:, k, tsl], start=(k == 0), stop=(k == 7)),
                                reads=[("wgl", which, k // 2)], writes=[pk])
                    s, sk = sg[c % 2], ("sg", c % 2)
                    ph.op("act", lambda e, s=s, pg=pg: e.activation(out=s[:], in_=pg[:], func=AF.Tanh, scale=0.5), reads=[pgk], writes=[sk])
                    ph.op("dve", lambda e, s=s, pa=pa, c=c, tg=tg: e.scalar_tensor_tensor(
                        out=hT[:, c, 32 + tg * 512:32 + (tg + 1) * 512], in0=s[:], scalar=1.0, in1=pa[:], op0=ALU.add, op1=ALU.mult),
                        reads=[sk, pak], writes=[("hT", tg)])
            if hf == 0:
                ph.op("dve", lambda e: e.tensor_copy(out=halo[:], in_=hT[:, :, HALF:HALF + 32]), reads=[("hT", 3)], writes=["halo"])
            crot = PsumRot(pbank[4:6], "conv")
            for tg in range(4):
                hk = ["hT_halo"] + [("hT", i) for i in range(max(0, tg - 1), tg + 1)]
                s1p, s2p = pbank[6], pbank[7]
                for c in range(4):
                    pt, pk = crot.next()
                    for k in range(KC):
                        c0 = 32 + tg * 512 - 30 + k
                        ph.op("pe", lambda e, pt=pt, c=c, k=k, c0=c0: e.matmul(
                            pt[:], lhsT=dmat[:, c * KC + k, :], rhs=hT[:, c, c0:c0 + 512], start=(k == 0), stop=(k == KC - 1)),
                            reads=hk + [("dmat", c * KC + k)], writes=[pk])
                    ph.op("act", lambda e, pt=pt, c=c: e.activation(out=cv[:, c, :], in_=pt[:], func=AF.Identity, bias=prm[:, l, 80 + c:81 + c]),
                          reads=[pk], writes=[("cv", c)])
                    ph.op("act", lambda e, c=c: e.activation(out=csq[:, c, :], in_=cv[:, c, :], func=AF.Square), reads=[("cv", c)], writes=[("csq", c)])
                    ph.op("act", lambda e, c=c: e.activation(out=cvb[:, c, :], in_=cv[:, c, :], func=AF.Copy), reads=[("cv", c)], writes=[("cvb", c)])
                for c in range(4):
                    ph.op("pe", lambda e, c=c: e.matmul(s1p[:], lhsT=ones_b[:], rhs=cvb[:, c, :], start=(c == 0), stop=(c == 3)),
                          reads=[("cvb", c)], writes=["s1p"])
                for c in range(4):
                    ph.op("pe", lambda e, c=c: e.matmul(s2p[:], lhsT=ones_b[:], rhs=csq[:, c, :], start=(c == 0), stop=(c == 3)),
                          reads=[("csq", c)], writes=["s2p"])
                ln_stats(ph, s1p, s2p, st1, st2, st3, mhalf, 1.0 / CW)
                for c in range(4):
                    z, zk = zt[c % 2], ("zt", c % 2)
                    t_, tk_ = th[c % 2], ("th", c % 2)
                    ph.op("dve", lambda e, c=c: e.tensor_tensor(out=cv[:, c, :], in0=cv[:, c, :], in1=st1[:], op=ALU.subtract),
                          reads=[("cv", c), "mean"], writes=[("cv", c)])
                    ph.op("dve", lambda e, c=c: e.tensor_tensor(out=cv[:, c, :], in0=cv[:, c, :], in1=st3[:], op=ALU.mult),
                          reads=[("cv", c), "rstd"], writes=[("cv", c)])
                    ph.op("dve", lambda e, c=c, z=z: e.tensor_scalar(out=z[:], in0=cv[:, c, :], scalar1=prm[:, l, 84 + c:85 + c],
                                                                     scalar2=prm[:, l, 88 + c:89 + c], op0=ALU.mult, op1=ALU.add),
                          reads=[("cv", c)], writes=[zk])
                    ph.op("act", lambda e, z=z, t_=t_: e.activation(out=t_[:], in_=z[:], func=AF.Tanh), reads=[zk], writes=[tk_])
                    ph.op("dve", lambda e, c=c, z=z, t_=t_, tg=tg: e.scalar_tensor_tensor(
                        out=cvT[:, c, tg * 512:(tg + 1) * 512], in0=t_[:], scalar=1.0, in1=z[:], op0=ALU.add, op1=ALU.mult),
                        reads=[zk, tk_], writes=[("cvT", tg)])
            ph.emit()
        es_at = contextlib.ExitStack()
        attnT = es_at.enter_context(_sbt(nc, "attnT", [64, H, HALF], BF16))

        nk = HALF * (hf + 1)
        nkt = nk // 128
        with contextlib.ExitStack() as esc:
            sbc = lambda name, shape, dt=F32: esc.enter_context(_sbt(nc, name, shape, dt))
            Qa = [sbc(f"Qa{i}", [80, HALF], BF16) for i in range(2)]
            Ka = [sbc(f"Ka{i}", [80, S], BF16) for i in range(2)]
            Va = [sbc(f"Va{i}", [128, 32, 128], BF16) for i in range(2)]
            pT = [sbc(f"pT{i}", [128, 512], BF16) for i in range(4)]
            rec = [sbc(f"rec{i}", [64, 512], F32) for i in range(2)]
            ph = Phase(nc, nm + "c")
            for i in range(2):
                ph.op("dve", lambda e, i=i: e.memset(Va[i][:, :, 64:128], 1.0), writes=[("Va1", i)])
                ph.dma("sp", Ka[i][64:80, 0:nk], indic[:, 0:nk] if hf == 1 else indic[:, 0:nk], writes=[("Kai", i)])
            if hf == 0:
                pass
            srot = PsumRot(pbank[0:4], "sT")
            orot = PsumRot(pbank[4:6], "oT")
            pi = 0
            pend = []
            LAG = 2

            def flush(n_keep):
                while len(pend) > n_keep:
                    pend.pop(0)()

            for h in range(H):
                b = h % 2
                ph.dma("sp", Qa[b][0:64, :], qT_d[h], writes=[("Qa", b)])
                ph.dma("sp", Qa[b][64:80, :], bT_d[h], writes=[("Qab", b)])
                ph.dma("sp", Ka[b][0:64, 0:nk], kT_d[h, :, 0:nk], writes=[("Ka", b)])
                ph.dma("sp", Va[b][:, 0:nkt, 0:64], v_d[0:nk, h * 64:(h + 1) * 64].rearrange("(t p) d -> p t d", p=128),
                       writes=[("Va", b)])
                rd = [("Qa", b), ("Qab", b), ("Ka", b), ("Kai", b)]
                for qg in range(4):
                    op_, ok_ = orot.next()
                    ktiles = list(range(hf * 16 + qg * 4 + 4))
                    for kt in ktiles:
                        lt = kt - hf * 16
                        r0 = max(0, lt - qg * 4)
                        q0 = qg * 512 + r0 * 128
                        n = 512 - r0 * 128
                        sp_, sk_ = srot.next()
                        diag = lt >= qg * 4
                        ph.op("pe", lambda e, sp_=sp_, b=b, kt=kt, q0=q0, n=n, diag=diag: e.matmul(
                            sp_[:, 0:n], lhsT=Ka[b][:, kt * 128:(kt + 1) * 128], rhs=Qa[b][:, q0:q0 + n], start=True, stop=not diag),
                            reads=rd, writes=[sk_])
                        if diag:
                            ph.op("pe", lambda e, sp_=sp_: e.matmul(sp_[:, 0:128], lhsT=ident_b[:], rhs=trimask[:], start=False, stop=True),
                                  reads=[], writes=[sk_])
                        p_, pk_ = pT[pi % 4], ("pT", pi % 4)
                        pi += 1
                        ph.op("act", lambda e, p_=p_, sp_=sp_, n=n: e.activation(out=p_[:, 0:n], in_=sp_[:, 0:n], func=AF.Exp, scale=HD ** -0.5),
                              reads=[sk_], writes=[pk_])
                        c0 = r0 * 128

                        def pv(op_=op_, ok_=ok_, p_=p_, pk_=pk_, b=b, kt=kt, c0=c0, n=n, first=(kt == 0), last=(kt == ktiles[-1])):
                            ph.op("pe", lambda e: e.matmul(op_[:, c0:c0 + n], lhsT=Va[b][:, kt, :], rhs=p_[:, 0:n], start=first, stop=last),
                                  reads=[pk_, ("Va", b), ("Va1", b)], writes=[ok_])
                        pend.append(pv)
                        flush(LAG)

                    def norm(op_=op_, ok_=ok_, h=h, qg=qg):
                        rc, rk = rec[qg % 2], ("rec", qg % 2)
                        ph.op("dve", lambda e: e.reciprocal(out=rc[:], in_=op_[64:128, :]), reads=[ok_], writes=[rk])
                        ph.op("dve", lambda e: e.tensor_tensor(out=attnT[:, h, qg * 512:(qg + 1) * 512], in0=op_[0:64, :], in1=rc[:], op=ALU.mult),
                              reads=[ok_, rk], writes=[("attnT", h)])
                    pend.append(norm)
            flush(0)
            ph.emit()

        with contextlib.ExitStack() as esd:
            sbd = lambda name, shape, dt=F32: esd.enter_context(_sbt(nc, name, shape, dt))
            wga = [sbd(f"wga{i}", [128, 8, 512], BF16) for i in range(2)]
            wgc = [sbd(f"wgc{i}", [128, 8, 512], BF16) for i in range(2)]
            wbr = [sbd(f"wbr{i}", [64, 8, 512], BF16) for i in range(2)]
            wpw = [sbd(f"wpw{i}", [128, 4, 512], BF16) for i in range(2)]
            sa = [sbd(f"sa{i}", [128, 512], F32) for i in range(2)]
            sc_ = [sbd(f"sc{i}", [128, 512], F32) for i in range(2)]
            m1 = [sbd(f"m1{i}", [128, 512], F32) for i in range(2)]
            ph = Phase(nc, nm + "d1")
            rot = PsumRot(pbank[0:8], "mrg")
            for cb in range(2):
                wload(ph, wbr[cb][:], wbr_d[l, cb], ("wbr", cb), 4 if cb == 0 else 1)
                wload(ph, wpw[cb][:], wpw_d[l, cb], ("wpw", cb), 2 if cb == 0 else 1, npieces=2)
                wload(ph, wga[cb][:], win_d[l, 5 + cb], ("wga", cb), 4 if cb == 0 else 1)
                wload(ph, wgc[cb][:], win_d[l, 7 + cb], ("wgc", cb), 4 if cb == 0 else 1)
            it = 0
            for m in range(8):
                cb, mc = m // 4, (m % 4) * 128
                for tg in range(4):
                    tsl = slice(tg * 512, (tg + 1) * 512)
                    pya, kya = rot.next()
                    pyc, kyc = rot.next()
                    pga, kga = rot.next()
                    pgc, kgc = rot.next()
                    for hh in range(H):
                        ph.op("pe", lambda e, pya=pya, hh=hh, cb=cb, mc=mc, tsl=tsl: e.matmul(
                            pya[:], lhsT=wbr[cb][:, hh, mc:mc + 128], rhs=attnT[:, hh, tsl], start=(hh == 0), stop=(hh == H - 1)),
                            reads=[("wbr", cb, hh // 2)], writes=[kya])
                    for k in range(4):
                        ph.op("pe", lambda e, pyc=pyc, k=k, cb=cb, mc=mc, tsl=tsl: e.matmul(
                            pyc[:], lhsT=wpw[cb][:, k, mc:mc + 128], rhs=cvT[:, k, tsl], start=(k == 0), stop=(k == 3)),
                            reads=[("wpw", cb, k // 2)], writes=[kyc])
                    for k in range(8):
                        ph.op("pe", lambda e, pga=pga, k=k, cb=cb, mc=mc, tsl=tsl: e.matmul(
                            pga[:], lhsT=wga[cb][:, k, mc:mc + 128], rhs=uT[:, k, tsl], start=(k == 0), stop=(k == 7)),
                            reads=[("wga", cb, k // 2)], writes=[kga])
                    for k in range(8):
                        ph.op("pe", lambda e, pgc=pgc, k=k, cb=cb, mc=mc, tsl=tsl: e.matmul(
                            pgc[:], lhsT=wgc[cb][:, k, mc:mc + 128], rhs=uT[:, k, tsl], start=(k == 0), stop=(k == 7)),
                            reads=[("wgc", cb, k // 2)], writes=[kgc])
                    i2 = it % 2
                    it += 1
                    ph.op("act", lambda e, i2=i2, pga=pga: e.activation(out=sa[i2][:], in_=pga[:], func=AF.Tanh, scale=0.5), reads=[kga], writes=[("sa", i2)])
                    ph.op("act", lambda e, i2=i2, pgc=pgc: e.activation(out=sc_[i2][:], in_=pgc[:], func=AF.Tanh, scale=0.5), reads=[kgc], writes=[("sc", i2)])
                    ph.op("dve", lambda e, i2=i2, pya=pya: e.scalar_tensor_tensor(out=m1[i2][:], in0=sa[i2][:], scalar=1.0, in1=pya[:], op0=ALU.add, op1=ALU.mult),
                          reads=[("sa", i2), kya], writes=[("m1", i2)])
                    ph.op("dve", lambda e, i2=i2, pyc=pyc: e.scalar_tensor_tensor(out=sc_[i2][:], in0=sc_[i2][:], scalar=1.0, in1=pyc[:], op0=ALU.add, op1=ALU.mult),
                          reads=[("sc", i2), kyc], writes=[("sc", i2)])
                    ph.op("dve", lambda e, i2=i2, m=m, tsl=tsl: e.tensor_tensor(out=mg[:, m, tsl], in0=m1[i2][:], in1=sc_[i2][:], op=ALU.add),
                          reads=[("m1", i2), ("sc", i2)], writes=[("mg", m)])
            ph.emit()
        es_at.close()
        es_cv.close()
        esh.close()
        with contextlib.ExitStack() as esd2:
            xT = esd2.enter_context(_sbt(nc, "xT", [128, 8, HALF], F32))
            wo = [esd2.enter_context(_sbt(nc, f"wo{i}", [128, 8, 512], BF16)) for i in range(2)]
            ph = Phase(nc, nm + "d2")
            for tg in range(4):
                ph.dma("sp", xT[:, :, tg * 512:(tg + 1) * 512], xs3[:, :, tg * 512:(tg + 1) * 512], writes=[("xld", tg)])
            for cb in range(2):
                wload(ph, wo[cb][:], wout_d[l, cb], ("wo", cb), 4 if cb == 0 else 1)
            def ymm(ph, pt, pk, mo, tg):
                cb, mc = mo // 4, (mo % 4) * 128
                tsl = slice(tg * 512, (tg + 1) * 512)
                for k in range(8):
                    ph.op("pe", lambda e, k=k: e.matmul(pt[:], lhsT=wo[cb][:, k, mc:mc + 128], rhs=mg[:, k, tsl], start=(k == 0), stop=(k == 7)),
                          reads=[("wo", cb, k // 2)], writes=[pk])
            deepnorm_ln(nc, ph, esd2, ymm, xT, prm, l, 2, 6, 7, pbank, ones_b, ntg=4, conc=2)
            for tg in range(4):
                ph.dma("sp", xs3[:, :, tg * 512:(tg + 1) * 512], xT[:, :, tg * 512:(tg + 1) * 512],
                       reads=[("x", mo, tg) for mo in range(8)], writes=[("xst", tg)])
            if dbg_d.get("x1") is not None and l == 0:
                ph.dma("sp", dbg_d["x1"][hf], xT[:].rearrange("p k t -> p (k t)"), reads=[("x", mo, tg) for mo in range(8) for tg in range(4)])
            ph.emit()
    es_mg.close()

    for grp in range(2):
        g0 = grp * 1024
        with contextlib.ExitStack() as ese:
            sbe = lambda name, shape, dt=F32: ese.enter_context(_sbt(nc, name, shape, dt))
            xg = sbe("xg", [128, 8, 1024], F32)
            hm = sbe("hm", [128, 32, 1024], BF16)
            with contextlib.ExitStack() as ese1:
                sb1 = lambda name, shape, dt=F32: ese1.enter_context(_sbt(nc, name, shape, dt))
                u2 = sb1("u2", [128, 8, 1024], BF16)
                wu = [sb1(f"wu{i}", [128, 8, 512], BF16) for i in range(2)]
                rl = [sb1(f"rl{i}", [128, 512], BF16) for i in range(3)]
                ph = Phase(nc, nm + f"e{grp}")
                ph.dma("sp", xg[:], xs3[:, :, g0:g0 + 1024], writes=["xT"])
                for k in range(8):
                    ph.op("dve", lambda e, k=k: e.tensor_scalar(out=u2[:, k, :], in0=xg[:, k, :], scalar1=P(l, 3, k), scalar2=P(l, 4, k),
                                                                op0=ALU.mult, op1=ALU.add), reads=["xT"], writes=["u2"])
                rot = PsumRot(pbank[0:4], "up")
                it = 0
                for jb in range(8):
                    w, wk = wu[jb % 2], ("wu", jb % 2)
                    wload(ph, w[:], wup_d[l, jb], wk, 4 if jb == 0 else 1)
                    for jj in range(4):
                        j = jb * 4 + jj
                        for sub in range(2):
                            pt, pk = rot.next()
                            for k in range(8):
                                ph.op("pe", lambda e, pt=pt, k=k, w=w, jj=jj, sub=sub: e.matmul(
                                    pt[:], lhsT=w[:, k, jj * 128:(jj + 1) * 128], rhs=u2[:, k, sub * 512:(sub + 1) * 512], start=(k == 0), stop=(k == 7)),
                                    reads=[wk + ((k // 2),), "u2"], writes=[pk])
                            r, rk = rl[it % 3], ("rl", it % 3)
                            ph.op("act", lambda e, r=r, pt=pt: e.activation(out=r[:], in_=pt[:], func=AF.Relu), reads=[pk], writes=[rk])
                            ph.op("dve", lambda e, r=r, j=j, sub=sub: e.tensor_tensor(out=hm[:, j, sub * 512:(sub + 1) * 512], in0=r[:], in1=r[:], op=ALU.mult),
                                  reads=[rk], writes=[("hm", j)])
                            it += 1
                ph.emit()
            with contextlib.ExitStack() as ese2:
                wd = [ese2.enter_context(_sbt(nc, f"wd{i}", [128, 32, 128], BF16)) for i in range(2)]
                ph = Phase(nc, nm + f"f{grp}")
                def ymm2(ph, pt, pk, mo, tg):
                    w, wk = wd[mo % 2], ("wd", mo % 2)
                    tsl = slice(tg * 512, (tg + 1) * 512)
                    if tg == 0:
                        wload(ph, w[:], wdn_d[l, mo], wk, 4 if mo == 0 else 1)
                    for k in range(32):
                        ph.op("pe", lambda e, k=k: e.matmul(pt[:], lhsT=w[:, k, :], rhs=hm[:, k, tsl], start=(k == 0), stop=(k == 31)),
                              reads=[wk + ((k // 8),)], writes=[pk])
                deepnorm_ln(nc, ph, ese2, ymm2, xg, prm, l, 5, 8, 9, pbank, ones_b, ntg=2, conc=2)
                ph.dma("sp", xs3[:, :, g0:g0 + 1024], xg[:], reads=[("x", mo, tg) for mo in range(8) for tg in range(2)], writes=["xst"])
                ph.emit()


def ln_stats(ph, s1p, s2p, st1, st2, st3, mhalf, inv_n):
    ph.op("dve", lambda e: e.tensor_scalar(out=st1[:], in0=s1p[:], scalar1=inv_n, scalar2=None, op0=ALU.mult), reads=["s1p"], writes=["mean"])
    ph.op("dve", lambda e: e.tensor_tensor(out=st2[:], in0=st1[:], in1=st1[:], op=ALU.mult), reads=["mean"], writes=["msq"])
    ph.op("dve", lambda e: e.scalar_tensor_tensor(out=st2[:], in0=s2p[:], scalar=inv_n, in1=st2[:], op0=ALU.mult, op1=ALU.subtract),
          reads=["s2p", "msq"], writes=["var"])
    ph.op("dve", lambda e: e.tensor_scalar(out=st2[:], in0=st2[:], scalar1=0.0, scalar2=EPS, op0=ALU.max, op1=ALU.add), reads=["var"], writes=["var2"])
    ph.op("act", lambda e: e.activation(out=st2[:], in_=st2[:], func=AF.Sqrt), reads=["var2"], writes=["sd"])
    ph.op("dve", lambda e: e.reciprocal(out=st3[:], in_=st2[:]), reads=["sd"], writes=["rstd"])


def deepnorm_ln(nc, ph, es, ymm, xT, prm, l, i_gt, i_g, i_b, pbank, ones_b, ntg, conc):
    sbx = lambda name, shape, dt=F32: es.enter_context(_sbt(nc, name, shape, dt))
    tb = [sbx(f"tb{i}", [128, 512], BF16) for i in range(4)]
    tq = [sbx(f"tq{i}", [128, 512], BF16) for i in range(4)]
    pendS = []
    st1 = [sbx(f"lst1{i}", [128, 512], F32) for i in range(conc)]
    st2 = sbx("lst2", [128, 512], F32)
    st3 = [sbx(f"lst3{i}", [128, 512], F32) for i in range(conc)]
    mhalf = sbx("lmhalf", [128, 512], F32)
    ph.op("pool", lambda e: e.memset(mhalf[:], -0.5), writes=["mhalf"])
    nyb = 8 - 2 * conc
    rot = PsumRot(pbank[0:nyb], "y")
    it = 0
    for base in range(0, ntg, conc):
        for mo in range(8):
            for ci in range(conc):
                tg = base + ci
                xsl = slice(tg * 512, (tg + 1) * 512)
                s1p, s2p = pbank[nyb + 2 * ci], pbank[nyb + 2 * ci + 1]
                pt, pk = rot.next()
                ymm(ph, pt, pk, mo, tg)
                xk = ("x", mo, tg)
                ph.op("dve", lambda e, pt=pt, mo=mo, xsl=xsl: e.scalar_tensor_tensor(
                    out=xT[:, mo, xsl], in0=pt[:], scalar=prm[:, l, i_gt * 8 + mo:i_gt * 8 + mo + 1], in1=xT[:, mo, xsl], op0=ALU.mult, op1=ALU.add),
                    reads=[pk, ("xld", tg)], writes=[xk])
                i2 = it % 4
                it += 1
                ph.op("act", lambda e, i2=i2, mo=mo, xsl=xsl: e.activation(out=tb[i2][:], in_=xT[:, mo, xsl], func=AF.Copy), reads=[xk], writes=[("tb", i2)])
                ph.op("act", lambda e, i2=i2, mo=mo, xsl=xsl: e.activation(out=tq[i2][:], in_=xT[:, mo, xsl], func=AF.Square), reads=[xk], writes=[("tq", i2)])
                def stats(i2=i2, mo=mo, s1p=s1p, s2p=s2p, ci=ci):
                    ph.op("pe", lambda e: e.matmul(s1p[:], lhsT=ones_b[:], rhs=tb[i2][:], start=(mo == 0), stop=(mo == 7)),
                          reads=[("tb", i2)], writes=[("s1p", ci)])
                    ph.op("pe", lambda e: e.matmul(s2p[:], lhsT=ones_b[:], rhs=tq[i2][:], start=(mo == 0), stop=(mo == 7)),
                          reads=[("tq", i2)], writes=[("s2p", ci)])
                pendS.append(stats)
                while len(pendS) > 2:
                    pendS.pop(0)()
        while pendS:
            pendS.pop(0)()
        for ci in range(conc):
            tg = base + ci
            xsl = slice(tg * 512, (tg + 1) * 512)
            s1p, s2p = pbank[nyb + 2 * ci], pbank[nyb + 2 * ci + 1]
            a, c = st1[ci], st3[ci]
            ph.op("dve", lambda e, a=a, s1p=s1p: e.tensor_scalar(out=a[:], in0=s1p[:], scalar1=1.0 / D, scalar2=None, op0=ALU.mult),
                  reads=[("s1p", ci)], writes=[("mean", ci)])
            ph.op("dve", lambda e, a=a: e.tensor_tensor(out=st2[:], in0=a[:], in1=a[:], op=ALU.mult), reads=[("mean", ci)], writes=["msq"])
            ph.op("dve", lambda e, s2p=s2p: e.scalar_tensor_tensor(out=st2[:], in0=s2p[:], scalar=1.0 / D, in1=st2[:], op0=ALU.mult, op1=ALU.subtract),
                  reads=[("s2p", ci), "msq"], writes=["var"])
            ph.op("dve", lambda e: e.tensor_scalar(out=st2[:], in0=st2[:], scalar1=0.0, scalar2=EPS, op0=ALU.max, op1=ALU.add), reads=["var"], writes=["var2"])
            ph.op("act", lambda e: e.activation(out=st2[:], in_=st2[:], func=AF.Sqrt), reads=["var2"], writes=["sd"])
            ph.op("dve", lambda e, c=c: e.reciprocal(out=c[:], in_=st2[:]), reads=["sd"], writes=[("rstd", ci)])
            for mo in range(8):
                xk = ("x", mo, tg)
                ph.op("dve", lambda e, mo=mo, xsl=xsl, a=a: e.tensor_tensor(out=xT[:, mo, xsl], in0=xT[:, mo, xsl], in1=a[:], op=ALU.subtract),
                      reads=[xk, ("mean", ci)], writes=[xk])
                ph.op("dve", lambda e, mo=mo, xsl=xsl, c=c: e.tensor_tensor(out=xT[:, mo, xsl], in0=xT[:, mo, xsl], in1=c[:], op=ALU.mult),
                      reads=[xk, ("rstd", ci)], writes=[xk])
                ph.op("act", lambda e, mo=mo, xsl=xsl: e.activation(
                    out=xT[:, mo, xsl], in_=xT[:, mo, xsl], func=AF.Identity, scale=prm[:, l, i_g * 8 + mo:i_g * 8 + mo + 1],
                    bias=prm[:, l, i_b * 8 + mo:i_b * 8 + mo + 1]), reads=[xk], writes=[xk])


def _cols(v):
    n = v.shape[-1] // 128
    return np.swapaxes(v.reshape(v.shape[:-1] + (n, 128)), -1, -2)


def _blk(w, bw):
    Lw, K, N = w.shape
    return np.ascontiguousarray(w.reshape(Lw, K // 128, 128, N // bw, bw).transpose(0, 3, 2, 1, 4)).reshape(Lw, N // bw, 128, (K // 128) * bw)


def prep_shared(inp):
    f = lambda a: np.asarray(a, dtype=np.float32)
    pvec = np.concatenate([_cols(f(inp["ln_mix_g"])), _cols(f(inp["ln_mix_b"])), _cols(f(inp["ln_ffn_g"])), _cols(f(inp["ln_ffn_b"])),
                           _cols(f(inp["b_ada"])), _cols(f(inp["conv_dw_b"])), _cols(f(inp["conv_ln_g"])), _cols(f(inp["conv_ln_b"]))], axis=-1)
    assert pvec.shape == (L, 128, 92), pvec.shape
    dwt = f(inp["conv_dw_w"])
    dww = np.ascontiguousarray(dwt.reshape(L, KC, 4, 128).transpose(0, 3, 2, 1)).reshape(L, 128, 4 * KC)
    wbr = f(inp["w_attn_br"]).reshape(L, H, 64, 2, 512).transpose(0, 3, 2, 1, 4)
    wbr = np.ascontiguousarray(wbr).reshape(L, 2, 64, 8 * 512)
    return {
        "pvec": np.ascontiguousarray(pvec), "dww": dww,
        "wada": _blk(f(inp["w_ada"]), 512), "win": _blk(f(inp["w_in"]), 512), "wbr": wbr,
        "wpw": _blk(f(inp["w_conv_pw"]), 512), "wout": _blk(f(inp["w_out"]), 512),
        "wup": _blk(f(inp["w_up"]), 512), "wdn": _blk(f(inp["w_down"]), 128),
    }


NPV = 92
_NC_CACHE = {}


def kernel(**inputs):
    shared = prep_shared(inputs)
    x = np.asarray(inputs["x"], dtype=np.float32)
    c = np.asarray(inputs["c"], dtype=np.float32)
    pos = np.asarray(inputs["positions"], dtype=np.int32)
    if "nc" not in _NC_CACHE:
        _NC_CACHE["nc"] = build(L)
    nc = _NC_CACHE["nc"]
    work = {0: 0, 1: 1, 4: 2, 5: 3}
    zeros = {k: np.zeros_like(v) for k, v in shared.items()}
    zeros["x"] = np.zeros((S, D), np.float32)
    zeros["cT"] = np.zeros((128, 8), np.float32)
    zeros["pos"] = np.zeros((128, 32), np.int32)
    in_maps = []
    for core in range(8):
        if core not in work:
            in_maps.append(zeros)
            continue
        b = work[core]
        m = dict(shared)
        m["x"] = np.ascontiguousarray(x[b])
        m["cT"] = np.ascontiguousarray(c[b].reshape(8, 128).T)
        m["pos"] = np.ascontiguousarray(pos[b].reshape(32, 128).T)
        in_maps.append(m)
    res = run_bass_kernel_spmd(nc, in_maps, core_ids=list(range(8)))
    inv = {b: core for core, b in work.items()}
    return np.stack([res.results[inv[b]]["out"] for b in range(B)], axis=0).astype(np.float32)
```

```python
import contextlib
import math
import numpy as np
import concourse.bass as bass
import concourse.mybir as mybir
from concourse.bass_utils import run_bass_kernel_spmd

F32 = mybir.dt.float32
BF16 = mybir.dt.bfloat16
I32 = mybir.dt.int32
AF = mybir.ActivationFunctionType
ALU = mybir.AluOpType
AX = mybir.AxisListType

D = 1024
S = 4096
B = 4
L = 4
H = 8
HD = 64
CW = 512
KC = 31
DFF = 4096
HALF = 2048
ALPHA = (2 * L) ** 0.25
EPS = 1e-5
BIG = 30000.0
NPV = 28 + 48 + 12


class _Op:
    __slots__ = ("eng", "fn", "deps", "is_dma", "idx", "sig", "sem", "semval", "prev_slot_val")


class Phase:
    NDMA = 12

    def __init__(self, nc, name):
        self.nc, self.name = nc, name
        self.ops, self.last_w, self.readers = [], {}, {}

    def op(self, eng, fn, reads=(), writes=(), dma=False):
        o = _Op()
        o.eng, o.fn, o.is_dma = eng, fn, dma
        deps = set()
        for k in reads:
            w = self.last_w.get(k)
            if w is not None:
                deps.add(w)
        for k in writes:
            w = self.last_w.get(k)
            if w is not None:
                deps.add(w)
            deps.update(self.readers.get(k, ()))
        o.idx = len(self.ops)
        deps.discard(o.idx)
        o.deps = deps
        self.ops.append(o)
        for k in reads:
            self.readers.setdefault(k, []).append(o.idx)
        for k in writes:
            self.last_w[k] = o.idx
            self.readers[k] = []
        return o.idx

    def dma(self, eng, out, in_, reads=(), writes=(), **kw):
        return self.op(eng, lambda e: e.dma_start(out=out, in_=in_, **kw), reads, writes, dma=True)

    def emit(self):
        for suf, n in TRUNC.items():
            if self.name.endswith(suf):
                print("phase", self.name, "ops", len(self.ops), "-> trunc", n)
                self.ops = self.ops[:n]
        nc, ops = self.nc, self.ops
        engs = ["pe", "act", "dve", "pool", "sp"]
        streams = {e: [o for o in ops if o.eng == e] for e in engs}

        def skip(p, o):
            return p.eng == "pe" and o.eng == "pe" and not p.is_dma and not o.is_dma

        for o in ops:
            o.sig = o.is_dma
        for o in ops:
            for d in o.deps:
                if not skip(ops[d], o):
                    ops[d].sig = True
        if id(nc) not in _SEMS:
            esem = {e: nc.semaphore(f"trk_{e}").__enter__() for e in engs}
            dsem = {e: [nc.semaphore(f"trk_d{e}{i}").__enter__() for i in range(self.NDMA)] for e in ("sp", "pool")}
            _SEMS[id(nc)] = (esem, dsem, {e: [0] * self.NDMA for e in dsem}, {e: 0 for e in dsem})
        esem, dsem, dcount, dnext = _SEMS[id(nc)]
        with nc.Block() as b0:
            def clr(e):
                for s_ in list(esem.values()):
                    e.sem_clear(s_)
            b0.gpsimd(clr)
        if True:
            ecount = {e: 0 for e in engs}
            dstart = {e: list(dcount[e]) for e in dsem}
            for o in ops:
                if o.is_dma:
                    s = dnext[o.eng] % self.NDMA
                    dnext[o.eng] += 1
                    o.prev_slot_val = dcount[o.eng][s]
                    dcount[o.eng][s] += 16
                    o.sem, o.semval = dsem[o.eng][s], dcount[o.eng][s]
                elif o.sig:
                    ecount[o.eng] += 1
                    o.sem, o.semval = esem[o.eng], ecount[o.eng]
                else:
                    o.sem, o.semval = None, 0
            with nc.Block() as blk:
                def run(e_name):
                    def body(e):
                        waited = {}
                        for o in streams[e_name]:
                            need = {}
                            for d in o.deps:
                                p = ops[d]
                                if skip(p, o):
                                    continue
                                key = id(p.sem)
                                if key not in need or need[key][1] < p.semval:
                                    need[key] = (p.sem, p.semval)
                            if o.is_dma and o.prev_slot_val > 0:
                                key = id(o.sem)
                                if key not in need or need[key][1] < o.prev_slot_val:
                                    need[key] = (o.sem, o.prev_slot_val)
                            for key, (s, v) in need.items():
                                if waited.get(key, 0) >= v:
                                    continue
                                e.wait_ge(s, v)
                                waited[key] = v
                            ins = o.fn(e)
                            if o.is_dma:
                                ins.then_inc(o.sem, 16)
                            elif o.sig:
                                ins.then_inc(o.sem, 1)
                        if e_name in dsem:
                            for i, s in enumerate(dsem[e_name]):
                                v = dcount[e_name][i]
                                if v > dstart[e_name][i] and waited.get(id(s), 0) < v:
                                    e.wait_ge(s, v)
                    return body
                for e_name, reg in (("pe", blk.tensor), ("act", blk.scalar), ("dve", blk.vector),
                                    ("pool", blk.gpsimd), ("sp", blk.sync)):
                    if streams[e_name]:
                        reg(run(e_name))
        if STOP[0] is not None and self.name.endswith(STOP[0]):
            raise StopBuild()


_UID = [0]


def _sbt(nc, name, shape, dt):
    _UID[0] += 1
    return nc.sbuf_tensor(f"{name}_u{_UID[0]}", shape, dt)


_SEMS = {}


class StopBuild(Exception):
    pass


STOP = [None]
TRUNC = {}


def wload(ph, dst, src, key, nsplit, npieces=4):
    K, N = dst.shape[1], dst.shape[2]
    step = K // nsplit
    for pc in range(nsplit):
        keys = [key + (q,) for q in range(pc * npieces // nsplit, (pc + 1) * npieces // nsplit)]
        ph.dma("pool", dst[:, pc * step:(pc + 1) * step, :].rearrange("p k n -> p (k n)"),
               src[:, pc * step * N:(pc + 1) * step * N], writes=keys, max_dma_last_dim=8192)


class PsumRot:
    def __init__(self, tiles, tag):
        self.tiles, self.tag, self.i = tiles, tag, 0

    def next(self):
        t = self.tiles[self.i % len(self.tiles)]
        k = (self.tag, self.i % len(self.tiles))
        self.i += 1
        return t, k


def build(depth=L, dbg=None):
    nc = bass.Bass("TRN2", target_bir_lowering=False)
    dt_in = lambda name, shape, dt=F32: nc.dram_tensor(name, shape, dt, kind="ExternalInput").ap()
    x_d = dt_in("x", [S, D])
    cT_d = dt_in("cT", [128, 8])
    pos_d = dt_in("pos", [128, 32], I32)
    pv_d = dt_in("pvec", [L, 128, NPV])
    dw_d = dt_in("dww", [L, 128, 4 * KC])
    wada_d = dt_in("wada", [L, 12, 128, 8 * 512])
    win_d = dt_in("win", [L, 9, 128, 8 * 512])
    wbr_d = dt_in("wbr", [L, 2, 64, 8 * 512])
    wpw_d = dt_in("wpw", [L, 2, 128, 4 * 512])
    wout_d = dt_in("wout", [L, 2, 128, 8 * 512])
    wup_d = dt_in("wup", [L, 8, 128, 8 * 512])
    wdn_d = dt_in("wdn", [L, 8, 128, 32 * 128])
    out_d = nc.dram_tensor("out", [S, D], F32, kind="ExternalOutput").ap()
    dbg_d = {}
    if dbg:
        for name, shape, dt in dbg:
            dbg_d[name] = nc.dram_tensor("dbg_" + name, shape, dt, kind="ExternalOutput").ap()

    xs_d = nc.dram_tensor("xs", [2, 128, 8 * HALF], F32).ap()
    qT_d = nc.dram_tensor("qTd", [H, 64, HALF], BF16).ap()
    kT_d = nc.dram_tensor("kTd", [H, 64, S], BF16).ap()
    v_d = nc.dram_tensor("vd", [S, 512], BF16).ap()
    bT_d = nc.dram_tensor("bTd", [H, 16, HALF], BF16).ap()

    es = contextlib.ExitStack()
    sb = lambda name, shape, dt=F32: es.enter_context(_sbt(nc, name, shape, dt))
    ps = lambda name, shape, dt=F32: es.enter_context(nc.psum_tensor(name, shape, dt))

    try:
      with es:
        _build_body(nc, es, sb, ps, locals(), depth)
    except StopBuild:
        pass
    return nc


def _build_body(nc, es, sb, ps, G, depth):
    x_d, cT_d, pos_d, pv_d, dw_d, wada_d, out_d = G["x_d"], G["cT_d"], G["pos_d"], G["pv_d"], G["dw_d"], G["wada_d"], G["out_d"]
    win_d, wbr_d, wpw_d, wout_d, wup_d, wdn_d = G["win_d"], G["wbr_d"], G["wpw_d"], G["wout_d"], G["wup_d"], G["wdn_d"]
    xs_d, qT_d, kT_d, v_d, bT_d, dbg_d = G["xs_d"], G["qT_d"], G["kT_d"], G["v_d"], G["bT_d"], G["dbg_d"]
    if True:
        ident_b = sb("ident_b", [128, 128], BF16)
        ident_f = sb("ident_f", [128, 128], F32)
        ones_b = sb("ones_b", [128, 128], BF16)
        trimask = sb("trimask", [128, 128], BF16)
        indic = sb("indic", [16, S], BF16)
        cos_t = sb("cos_t", [128, 32, 8], F32)
        sin_t = sb("sin_t", [128, 32, 8], F32)
        pv = sb("pv", [128, L, NPV], F32)
        prm = sb("prm", [128, L, 96], F32)
        dww = sb("dww", [128, L, 4 * KC], F32)
        kmT = sb("kmT", [64, H, 16], BF16)
        halo = sb("halo", [128, 4, 32], BF16)
        pbank = [ps(f"pb{i}", [128, 512], F32) for i in range(8)]

        def P(l, i, k=None):
            if k is None:
                return prm[:, l, i * 8:(i + 1) * 8]
            return prm[:, l, i * 8 + k:i * 8 + k + 1]


        with contextlib.ExitStack() as es0:
            sb0 = lambda name, shape, dt=F32: es0.enter_context(_sbt(nc, name, shape, dt))
            posi = sb0("posi", [128, 32], I32)
            posf = sb0("posf", [128, 32], F32)
            ang = sb0("ang", [128, 32, 8], F32)
            kk = sb0("kk", [128, 32, 8], F32)
            kki = sb0("kki", [128, 32, 8], I32)
            tmpa = sb0("tmpa", [128, 32, 8], F32)
            iot = sb0("iot", [128, 128], F32)
            iop = sb0("iop", [128, 1], F32)
            iot_i = sb0("iot_i", [128, 128], I32)
            iop_i = sb0("iop_i", [128, 1], I32)
            indi = sb0("indi", [16, S], I32)
            cT = sb0("cTs", [128, 8], F32)
            cth = sb0("cth", [128, 8], F32)
            cTb = sb0("cTb", [128, 8], BF16)
            wa = [sb0(f"wa{i}", [128, 8, 512], BF16) for i in range(2)]
            adas = sb0("adas", [128, L, 48], F32)
            indf = sb0("indf", [16, S], F32)
            halfpi = sb0("halfpi", [128, 1], F32)
            ph = Phase(nc, "p0")
            ph.op("pool", lambda e: e.iota(iot_i[:], pattern=[[1, 128]], base=0, channel_multiplier=0), writes=["iot_i"])
            ph.op("pool", lambda e: e.iota(iop_i[:], pattern=[[0, 1]], base=0, channel_multiplier=1), writes=["iop_i"])
            ph.op("dve", lambda e: e.tensor_copy(out=iot[:], in_=iot_i[:]), reads=["iot_i"], writes=["iot"])
            ph.op("dve", lambda e: e.tensor_copy(out=iop[:], in_=iop_i[:]), reads=["iop_i"], writes=["iop"])
            ph.op("dve", lambda e: e.memset(kmT[:], 0.0), writes=["kmT"])
            ph.op("dve", lambda e: e.tensor_scalar(out=ident_f[:], in0=iot[:], scalar1=iop[:, 0:1], scalar2=None,
                                                   op0=ALU.is_equal), reads=["iot", "iop"], writes=["idf"])
            ph.op("dve", lambda e: e.tensor_copy(out=ident_b[:], in_=ident_f[:]), reads=["idf"], writes=["idb"])
            ph.op("dve", lambda e: e.memset(ones_b[:], 1.0), writes=["ones"])
            ph.op("dve", lambda e: e.tensor_scalar(out=trimask[:], in0=iot[:], scalar1=iop[:, 0:1], scalar2=-BIG,
                                                   op0=ALU.is_lt, op1=ALU.mult), reads=["iot", "iop"], writes=["tri"])
            tmpi = sb0("tmpi", [16, S], F32)
            ph.op("pool", lambda e: e.iota(indi[:], pattern=[[1, S]], base=0, channel_multiplier=-256), writes=["indi"])
            ph.op("dve", lambda e: e.tensor_copy(out=indf[:], in_=indi[:]), reads=["indi"], writes=["indf"])
            ph.op("dve", lambda e: e.tensor_scalar(out=tmpi[:], in0=indf[:], scalar1=255.5, scalar2=None, op0=ALU.is_lt),
                  reads=["indf"], writes=["tmpi1"])
            ph.op("dve", lambda e: e.tensor_scalar(out=indf[:], in0=indf[:], scalar1=-0.5, scalar2=None, op0=ALU.is_gt),
                  reads=["indf", "tmpi1"], writes=["indf1"])
            ph.op("dve", lambda e: e.tensor_tensor(out=indic[:], in0=indf[:], in1=tmpi[:], op=ALU.mult),
                  reads=["indf1", "tmpi1"], writes=["indic"])
            ph.dma("sp", posi[:], pos_d, writes=["posi"])
            ph.op("dve", lambda e: e.tensor_copy(out=posf[:], in_=posi[:]), reads=["posi"], writes=["posf"])
            for i in range(8):
                invf = float(np.float32(500000.0) ** np.float32(-(2.0 * i) / 16.0))
                ph.op("dve", lambda e, i=i, invf=invf: e.tensor_scalar(out=ang[:, :, i], in0=posf[:], scalar1=invf,
                                                                       scalar2=None, op0=ALU.mult),
                      reads=["posf"], writes=["ang"])
            ph.op("dve", lambda e: e.tensor_scalar(out=kk[:], in0=ang[:], scalar1=float(1.0 / (2 * math.pi)), scalar2=None,
                                                   op0=ALU.mult), reads=["ang"], writes=["kk"])
            ph.op("dve", lambda e: e.tensor_copy(out=kki[:], in_=kk[:]), reads=["kk"], writes=["kki"])
            ph.op("dve", lambda e: e.tensor_copy(out=kk[:], in_=kki[:]), reads=["kki"], writes=["kk2"])
            C1, C2 = 6.28125, float(2 * math.pi - 6.28125)
            ph.op("dve", lambda e: e.scalar_tensor_tensor(out=ang[:], in0=kk[:], scalar=-C1, in1=ang[:], op0=ALU.mult,
                                                          op1=ALU.add), reads=["kk2", "ang"], writes=["ang"])
            ph.op("dve", lambda e: e.scalar_tensor_tensor(out=ang[:], in0=kk[:], scalar=-C2, in1=ang[:], op0=ALU.mult,
                                                          op1=ALU.add), reads=["kk2", "ang"], writes=["ang"])
            ph.op("dve", lambda e: e.tensor_scalar(out=tmpa[:], in0=ang[:], scalar1=math.pi, scalar2=-2 * math.pi,
                                                   op0=ALU.is_gt, op1=ALU.mult), reads=["ang"], writes=["tmpa"])
            ph.op("dve", lambda e: e.tensor_tensor(out=ang[:], in0=ang[:], in1=tmpa[:], op=ALU.add),
                  reads=["ang", "tmpa"], writes=["ang"])
            ph.op("dve", lambda e: e.tensor_scalar(out=tmpa[:], in0=ang[:], scalar1=-math.pi, scalar2=2 * math.pi,
                                                   op0=ALU.is_lt, op1=ALU.mult), reads=["ang"], writes=["tmpa"])
            ph.op("dve", lambda e: e.tensor_tensor(out=ang[:], in0=ang[:], in1=tmpa[:], op=ALU.add),
                  reads=["ang", "tmpa"], writes=["ang"])
            ph.op("act", lambda e: e.activation(out=sin_t[:], in_=ang[:], func=AF.Sin), reads=["ang"], writes=["sin"])
            ph.op("dve", lambda e: e.tensor_scalar(out=tmpa[:], in0=ang[:], scalar1=-1.0, scalar2=None, op0=ALU.mult),
                  reads=["ang"], writes=["tmpa"])
            ph.op("dve", lambda e: e.tensor_tensor(out=tmpa[:], in0=tmpa[:], in1=ang[:], op=ALU.max),
                  reads=["ang", "tmpa"], writes=["tmpa"])
            ph.op("dve", lambda e: e.memset(halfpi[:], math.pi / 2), writes=["halfpi"])
            ph.op("act", lambda e: e.activation(out=cos_t[:], in_=tmpa[:], func=AF.Sin, scale=-1.0, bias=halfpi[:, 0:1]),
                  reads=["tmpa", "halfpi"], writes=["cos"])
            ph.dma("sp", pv[:], pv_d.rearrange("l p n -> p l n"), writes=["pv"])
            ph.dma("sp", dww[:], dw_d.rearrange("l p n -> p l n"), writes=["dww"])
            ph.dma("sp", cT[:], cT_d, writes=["cT"])
            ph.op("act", lambda e: e.activation(out=cth[:], in_=cT[:], func=AF.Tanh, scale=0.5), reads=["cT"], writes=["cth"])
            ph.op("dve", lambda e: e.scalar_tensor_tensor(out=cth[:], in0=cth[:], scalar=1.0, in1=cT[:], op0=ALU.add,
                                                          op1=ALU.mult), reads=["cth", "cT"], writes=["cth2"])
            ph.op("dve", lambda e: e.tensor_scalar(out=cTb[:], in0=cth[:], scalar1=0.5, scalar2=None, op0=ALU.mult),
                  reads=["cth2"], writes=["cTb"])
            rot = PsumRot(pbank[0:2], "adaps")
            it = 0
            for l in range(depth):
                for cb in range(12):
                    w = wa[it % 2]
                    wk = ("wa", it % 2)
                    it += 1
                    ph.dma("pool", w[:].rearrange("p k n -> p (k n)"), wada_d[l, cb], writes=[wk], max_dma_last_dim=8192)
                    pt, pk = rot.next()
                    for m in range(4):
                        for k in range(8):
                            ph.op("pe", lambda e, w=w, m=m, k=k, pt=pt: e.matmul(
                                pt[:, m:m + 1], lhsT=w[:, k, m * 128:(m + 1) * 128], rhs=cTb[:, k:k + 1],
                                start=(k == 0), stop=(k == 7)), reads=[wk, "cTb"], writes=[pk])
                    ph.op("dve", lambda e, l=l, cb=cb, pt=pt: e.tensor_tensor(
                        out=adas[:, l, cb * 4:(cb + 1) * 4], in0=pt[:, 0:4], in1=pv[:, l, 32 + cb * 4:32 + (cb + 1) * 4],
                        op=ALU.add), reads=[pk, "pv"], writes=["adas"])
            for l in range(depth):
                A = lambda i: adas[:, l, i * 8:(i + 1) * 8]
                def ts(out, in0, s1, s2, op0, op1=ALU.bypass, l=l):
                    ph.op("dve", lambda e: e.tensor_scalar(out=out, in0=in0, scalar1=s1, scalar2=s2, op0=op0, op1=op1),
                          reads=["adas", "pv", "dww"], writes=["prm"])
                ts(P(l, 0), A(1), 1.0, 1.0 / ALPHA, ALU.add, ALU.mult)
                ts(P(l, 1), A(0), 1.0, None, ALU.mult)
                ts(P(l, 2), A(2), 1.0, 0.5, ALU.add, ALU.mult)
                ts(P(l, 3), A(4), 1.0, 1.0 / ALPHA, ALU.add, ALU.mult)
                ts(P(l, 4), A(3), 1.0, None, ALU.mult)
                ts(P(l, 5), A(5), 1.0, None, ALU.add)
                ts(P(l, 6), pv[:, l, 0:8], ALPHA, None, ALU.mult)
                ts(P(l, 7), pv[:, l, 8:16], ALPHA, None, ALU.mult)
                ts(P(l, 8), pv[:, l, 16:24], ALPHA, None, ALU.mult)
                ts(P(l, 9), pv[:, l, 24:32], ALPHA, None, ALU.mult)
                ts(prm[:, l, 80:84], pv[:, l, 80:84], 1.0, None, ALU.mult)
                ts(prm[:, l, 84:88], pv[:, l, 84:88], 0.5, None, ALU.mult)
                ts(prm[:, l, 88:92], pv[:, l, 88:92], 0.5, None, ALU.mult)
                ph.op("dve", lambda e, l=l: e.tensor_scalar(out=dww[:, l, :], in0=dww[:, l, :], scalar1=0.5, scalar2=None,
                                                            op0=ALU.mult), reads=["dww"], writes=["dww"])
            ph.emit()

        for hf in range(2):
            with contextlib.ExitStack() as es1:
                xT = es1.enter_context(_sbt(nc, "xT", [128, 8, HALF], F32))
                xin = [es1.enter_context(_sbt(nc, f"xin{i}", [128, D], F32)) for i in range(2)]
                ph = Phase(nc, f"px{hf}")
                rot = PsumRot(pbank[0:4], "xps")
                for tt in range(16):
                    xi, xk = xin[tt % 2], ("xin", tt % 2)
                    t0 = hf * HALF + tt * 128
                    ph.dma("sp", xi[:], x_d[t0:t0 + 128, :], writes=[xk])
                    for kq in range(2):
                        pt, pk = rot.next()
                        for j in range(4):
                            k = kq * 4 + j
                            ph.op("pe", lambda e, pt=pt, j=j, k=k, xi=xi: e.transpose(
                                pt[:, j * 128:(j + 1) * 128], xi[:, k * 128:(k + 1) * 128], ident_f[:]),
                                reads=[xk], writes=[pk])
                        eng = "act" if kq == 0 else "dve"
                        dst = xT[:, kq * 4:(kq + 1) * 4, tt * 128:(tt + 1) * 128]
                        src = pt[:].rearrange("p (j t) -> p j t", j=4)
                        if eng == "act":
                            ph.op("act", lambda e, dst=dst, src=src: e.activation(out=dst, in_=src, func=AF.Copy, scale=ALPHA),
                                  reads=[pk], writes=[("xT", tt)])
                        else:
                            ph.op("dve", lambda e, dst=dst, src=src: e.tensor_scalar(out=dst, in0=src, scalar1=ALPHA, scalar2=None,
                                                                                     op0=ALU.mult), reads=[pk], writes=[("xT", tt)])
                ph.dma("sp", xs_d[hf], xT[:].rearrange("p k t -> p (k t)"), reads=[("xT", tt) for tt in range(16)], writes=["xs"])
                ph.emit()

        for l in range(depth):
            for hf in range(2):
                layer_half(nc, l, hf, locals())

        for hf in range(2):
            with contextlib.ExitStack() as es1:
                xT = es1.enter_context(_sbt(nc, "xT", [128, 8, HALF], F32))
                xo = [es1.enter_context(_sbt(nc, f"xo{i}", [128, D], F32)) for i in range(2)]
                ph = Phase(nc, f"pf{hf}")
                ph.dma("sp", xT[:].rearrange("p k t -> p (k t)"), xs_d[hf], writes=["xT"])
                rot = PsumRot(pbank[0:4], "ops")
                for tt in range(16):
                    xi, xk = xo[tt % 2], ("xo", tt % 2)
                    for kq in range(2):
                        pt, pk = rot.next()
                        for j in range(4):
                            k = kq * 4 + j
                            ph.op("pe", lambda e, pt=pt, j=j, k=k, tt=tt: e.transpose(
                                pt[:, j * 128:(j + 1) * 128], xT[:, k, tt * 128:(tt + 1) * 128], ident_f[:]),
                                reads=["xT"], writes=[pk])
                        dst = xi[:, kq * 512:(kq + 1) * 512]
                        if kq == 0:
                            ph.op("act", lambda e, dst=dst, pt=pt: e.activation(out=dst, in_=pt[:], func=AF.Copy, scale=1.0 / ALPHA),
                                  reads=[pk], writes=[(xk, kq)])
                        else:
                            ph.op("dve", lambda e, dst=dst, pt=pt: e.tensor_scalar(out=dst, in0=pt[:], scalar1=1.0 / ALPHA, scalar2=None,
                                                                                   op0=ALU.mult), reads=[pk], writes=[(xk, kq)])
                    t0 = hf * HALF + tt * 128
                    ph.dma("sp", out_d[t0:t0 + 128, :], xi[:], reads=[(xk, 0), (xk, 1)], writes=["out"])
                ph.emit()
    return nc


def layer_half(nc, l, hf, env):
    g = env
    prm, dww, pbank = g["prm"], g["dww"], g["pbank"]
    ident_b, ones_b, trimask, indic = g["ident_b"], g["ones_b"], g["trimask"], g["indic"]
    cos_t, sin_t, kmT, halo = g["cos_t"], g["sin_t"], g["kmT"], g["halo"]
    xs_d, qT_d, kT_d, v_d, bT_d = g["xs_d"], g["qT_d"], g["kT_d"], g["v_d"], g["bT_d"]
    win_d, wbr_d, wpw_d, wout_d, wup_d, wdn_d = g["win_d"], g["wbr_d"], g["wpw_d"], g["wout_d"], g["wup_d"], g["wdn_d"]
    dbg_d = g["dbg_d"]
    P = g["P"]
    T0 = hf * HALF
    nm = f"l{l}h{hf}"
    xs3 = xs_d[hf].rearrange("p (k t) -> p k t", k=8)
    es_mg = contextlib.ExitStack()
    mg = es_mg.enter_context(_sbt(nc, "mg", [128, 8, HALF], BF16))
    esh = contextlib.ExitStack()
    uT = esh.enter_context(_sbt(nc, "uT", [128, 8, HALF], BF16))
    es_x = contextlib.ExitStack()
    xT = es_x.enter_context(_sbt(nc, "xT", [128, 8, HALF], F32))
    if True:
        with contextlib.ExitStack() as esa:
            sba = lambda name, shape, dt=F32: esa.enter_context(_sbt(nc, name, shape, dt))
            wqkv = sba("wqkv", [128, 3, 8, 512], BF16)
            qk_sb = [sba(f"qk_sb{i}", [128, 512], BF16) for i in range(4)]
            v_sb = [sba(f"v_sb{i}", [128, 512], BF16) for i in range(2)]
            rtmp = [sba(f"rtmp{i}", [128, H, 8], F32) for i in range(8)]
            qT_st = sba("qT_st", [64, H, 512], BF16)
            kT_st = sba("kT_st", [64, H, 512], BF16)
            kms = sba("kms", [64, 16], F32)
            gsb = sba("gsb", [128, H, 16], F32)
            top8 = sba("top8", [128, H, 8], F32)
            msk = sba("msk", [128, H, 16], F32)
            bias_sb = sba("bias_sb", [128, H, 16], BF16)
            bT_st = sba("bT_st", [16, H, 512], BF16)
            VB = sba("VB", [128, 8, 16], F32)
            NB = sba("NB", [128, 8, 16], F32)
            VS = sba("VS", [128, 8, 16], F32)
            ph = Phase(nc, nm + "a")
            ph.dma("sp", xT[:], xs3, writes=["xT"])
            for i in range(3):
                wload(ph, wqkv[:, i], win_d[l, i], ("wqkv", i), 4)
            own0 = 8 * hf
            ph.op("pool", lambda e: e.memset(VB[:], -1e30), writes=["VB"])
            ph.op("pool", lambda e: e.memset(NB[:], -BIG), writes=["NB"])
            ph.op("pool", lambda e: e.memset(VS[:], -BIG), writes=["VS"])
            for ob in range(8):
                own = own0 + ob
                if own > 0:
                    ph.op("pool", lambda e, ob=ob, own=own: e.memset(VB[:, ob, 0:own], 0.0), reads=["VB"], writes=["VB"])
                ph.op("pool", lambda e, ob=ob, own=own: e.memset(VS[:, ob, 0:own + 1], 0.0), reads=["VS"], writes=["VS"])
                ph.op("pool", lambda e, ob=ob, own=own: e.memset(NB[:, ob, own:own + 1], 0.0), reads=["NB"], writes=["NB"])
            for k in range(8):
                ph.op("dve", lambda e, k=k: e.tensor_scalar(out=uT[:, k, :], in0=xT[:, k, :], scalar1=P(l, 0, k), scalar2=P(l, 1, k),
                                                            op0=ALU.mult, op1=ALU.add), reads=["xT"], writes=[("uT", k)])
            rot = PsumRot(pbank[0:3], "qkv")
            trot = PsumRot(pbank[3:5], "tr")
            grot = PsumRot(pbank[5:7], "gate")
            pendA = []
            for tg in range(4):
                for t4 in range(4):
                    tt = tg * 4 + t4
                    gt = hf * 16 + tt
                    tsl = slice(tt * 128, (tt + 1) * 128)
                    for blk in range(3):
                        pt, pk = rot.next()
                        for k in range(8):
                            ph.op("pe", lambda e, pt=pt, k=k, blk=blk, tsl=tsl: e.matmul(
                                pt[:], lhsT=uT[:, k, tsl], rhs=wqkv[:, blk, k, :], start=(k == 0), stop=(k == 7)),
                                reads=[("wqkv", blk, k // 2), ("uT", k)], writes=[pk])
                        if blk == 0:
                            while pendA:
                                pendA.pop(0)()
                        if blk == 2:
                            vs, vk = v_sb[tt % 2], ("v_sb", tt % 2)
                            ph.op("act", lambda e, vs=vs, pt=pt: e.activation(out=vs[:], in_=pt[:], func=AF.Copy), reads=[pk], writes=[vk])
                            ph.dma("sp", v_d[T0 + tt * 128:T0 + (tt + 1) * 128, :], vs[:], reads=[vk], writes=["v_d"])
                            continue
                        qs, qk = qk_sb[blk * 2 + tt % 2], ("qk_sb", blk * 2 + tt % 2)
                        p3 = pt[:].rearrange("p (h d) -> p h d", h=H)
                        q3 = qs[:].rearrange("p (h d) -> p h d", h=H)
                        ph.op("dve", lambda e, q3=q3, p3=p3: e.tensor_copy(out=q3[:, :, 16:64], in_=p3[:, :, 16:64]),
                              reads=[pk], writes=[(qk, "nr")])
                        cosb = cos_t[:, gt, :].unsqueeze(1).to_broadcast([128, H, 8])
                        sinb = sin_t[:, gt, :].unsqueeze(1).to_broadcast([128, H, 8])
                        x1, x2 = p3[:, :, 0:8], p3[:, :, 8:16]
                        r = rtmp[blk * 4:blk * 4 + 4]
                        ro = blk * 4
                        def tt_(out, a, b, op, rk, wk, extra_r=()):
                            ph.op("dve", lambda e: e.tensor_tensor(out=out, in0=a, in1=b, op=op), reads=list(rk) + list(extra_r), writes=wk)
                        tt_(r[0][:], x1, cosb, ALU.mult, [pk], [("rt", ro + 0)])
                        tt_(r[1][:], x2, sinb, ALU.mult, [pk], [("rt", ro + 1)])
                        tt_(r[2][:], x2, cosb, ALU.mult, [pk], [("rt", ro + 2)])
                        tt_(r[3][:], x1, sinb, ALU.mult, [pk], [("rt", ro + 3)])
                        tt_(q3[:, :, 0:8], r[0][:], r[1][:], ALU.subtract, [("rt", ro + 0), ("rt", ro + 1)], [(qk, "r1")])
                        tt_(q3[:, :, 8:16], r[2][:], r[3][:], ALU.add, [("rt", ro + 2), ("rt", ro + 3)], [(qk, "r2")])
                        def trans(qs=qs, qk=qk, blk=blk, t4=t4):
                            tp, tk = trot.next()
                            tpb = tp[:].bitcast(BF16)
                            for h in range(H):
                                ph.op("pe", lambda e, tpb=tpb, h=h, qs=qs: e.transpose(
                                    tpb[0:64, h * 128:(h + 1) * 128], qs[:, h * 64:(h + 1) * 64], ident_b[:]),
                                    reads=[(qk, "nr"), (qk, "r1"), (qk, "r2")], writes=[tk])
                            st = qT_st if blk == 0 else kT_st
                            stk = ("qT_st" if blk == 0 else "kT_st")
                            ph.op("act", lambda e, st=st, tpb=tpb, t4=t4: e.activation(
                                out=st[:, :, t4 * 128:(t4 + 1) * 128], in_=tpb[0:64, 0:1024].rearrange("p (h t) -> p h t", h=H), func=AF.Copy),
                                reads=[tk], writes=[(stk, t4)])
                        pendA.append(trans)
                while pendA:
                    pendA.pop(0)()
                ph.dma("sp", kT_d[:, :, T0 + tg * 512:T0 + (tg + 1) * 512].rearrange("h d t -> d h t"), kT_st[:],
                       reads=[("kT_st", i) for i in range(4)], writes=["kT_d"])
                ph.dma("sp", qT_d[:, :, tg * 512:(tg + 1) * 512].rearrange("h d t -> d h t"), qT_st[:],
                       reads=[("qT_st", i) for i in range(4)], writes=["qT_d"])
                b0 = own0 + 2 * tg
                ph.op("dve", lambda e: e.tensor_reduce(out=kms[:], in_=kT_st[:].rearrange("p h (b t) -> p (h b) t", b=2),
                                                       axis=AX.X, op=ALU.add), reads=[("kT_st", i) for i in range(4)], writes=["kms"])
                ph.op("dve", lambda e, b0=b0: e.tensor_scalar(out=kmT[:, :, b0:b0 + 2], in0=kms[:].rearrange("p (h b) -> p h b", b=2),
                                                              scalar1=1.0 / 256, scalar2=None, op0=ALU.mult), reads=["kms"], writes=["kmT"])
                for t4 in range(4):
                    tt = tg * 4 + t4
                    ob = tt // 2
                    gp, gk = grot.next()
                    for h in range(H):
                        ph.op("pe", lambda e, gp=gp, h=h, t4=t4: e.matmul(
                            gp[:, h * 16:(h + 1) * 16], lhsT=qT_st[:, h, t4 * 128:(t4 + 1) * 128], rhs=kmT[:, h, :], start=True, stop=True),
                            reads=[("qT_st", t4), "kmT"], writes=[gk])
                    g3 = gp[:, 0:128].rearrange("p (h s) -> p h s", h=H)
                    bc = lambda t: t[:, ob, :].unsqueeze(1).to_broadcast([128, H, 16])
                    ph.op("dve", lambda e, g3=g3, ob=ob: e.tensor_tensor(out=gsb[:], in0=g3, in1=VB[:, ob, :].unsqueeze(1).to_broadcast([128, H, 16]),
                                                                         op=ALU.add), reads=[gk, "VB"], writes=["gsb"])
                    for h in range(H):
                        ph.op("dve", lambda e, h=h: e.max(out=top8[:, h, :], in_=gsb[:, h, :]), reads=["gsb"], writes=[("top8", h)])
                    ph.op("dve", lambda e: e.tensor_tensor(out=msk[:], in0=gsb[:], in1=top8[:, :, 2:3].to_broadcast([128, H, 16]), op=ALU.is_lt),
                          reads=["gsb"] + [("top8", h) for h in range(H)], writes=["msk"])
                    ph.op("dve", lambda e, ob=ob: e.tensor_tensor(out=msk[:], in0=msk[:], in1=NB[:, ob, :].unsqueeze(1).to_broadcast([128, H, 16]),
                                                                  op=ALU.mult), reads=["msk", "NB"], writes=["msk"])
                    ph.op("dve", lambda e, ob=ob: e.tensor_tensor(out=bias_sb[:], in0=msk[:], in1=VS[:, ob, :].unsqueeze(1).to_broadcast([128, H, 16]),
                                                                  op=ALU.add), reads=["msk", "VS"], writes=["bias_sb"])
                    tp, tk = trot.next()
                    tpb = tp[:].bitcast(BF16)
                    for h in range(H):
                        ph.op("pe", lambda e, tpb=tpb, h=h: e.transpose(tpb[0:16, h * 128:(h + 1) * 128], bias_sb[:, h, :], ident_b[:]),
                              reads=["bias_sb"], writes=[tk])
                    ph.op("act", lambda e, tpb=tpb, t4=t4: e.activation(
                        out=bT_st[:, :, t4 * 128:(t4 + 1) * 128], in_=tpb[0:16, 0:1024].rearrange("p (h t) -> p h t", h=H), func=AF.Copy),
                        reads=[tk], writes=[("bT_st", t4)])
                ph.dma("sp", bT_d[:, :, tg * 512:(tg + 1) * 512].rearrange("h s t -> s h t"), bT_st[:],
                       reads=[("bT_st", i) for i in range(4)], writes=["bT_d"])
            ph.emit()
        es_x.close()
        es_cv = contextlib.ExitStack()
        cvT = es_cv.enter_context(_sbt(nc, "cvT", [128, 4, HALF], BF16))

        with contextlib.ExitStack() as esb:
            sbb = lambda name, shape, dt=F32: esb.enter_context(_sbt(nc, name, shape, dt))
            hT = sbb("hT", [128, 4, 32 + HALF], BF16)
            wgl = sbb("wgl", [128, 2, 8, 512], BF16)
            dmat = sbb("dmat", [128, 4 * KC, 128], BF16)
            sg = [sbb(f"sg{i}", [128, 512], F32) for i in range(2)]
            cv = sbb("cv", [128, 4, 512], F32)
            cvb = sbb("cvb", [128, 4, 512], BF16)
            csq = sbb("csq", [128, 4, 512], BF16)
            st1 = sbb("st1", [128, 512], F32)
            st2 = sbb("st2", [128, 512], F32)
            st3 = sbb("st3", [128, 512], F32)
            mhalf = sbb("mhalf", [128, 512], F32)
            zt = [sbb(f"zt{i}", [128, 512], F32) for i in range(2)]
            th = [sbb(f"th{i}", [128, 512], F32) for i in range(2)]
            ph = Phase(nc, nm + "b")
            for i in range(2):
                wload(ph, wgl[:, i], win_d[l, 3 + i], ("wgl", i), 4)
            ph.op("pool", lambda e: e.memset(mhalf[:], -0.5), writes=["mhalf"])
            for j in range(4 * KC):
                ph.op("dve", lambda e, j=j: e.tensor_scalar(out=dmat[:, j, :], in0=ident_b[:], scalar1=dww[:, l, j:j + 1], scalar2=None,
                                                          op0=ALU.mult), writes=[("dmat", j)])
            if hf == 0:
                ph.op("dve", lambda e: e.memset(hT[:, :, 0:32], 0.0), writes=["hT_halo"])
            else:
                ph.op("dve", lambda e: e.tensor_copy(out=hT[:, :, 0:32], in_=halo[:]), writes=["hT_halo"])
            rot = PsumRot(pbank[0:4], "glu")
            for tg in range(4):
                tsl = slice(tg * 512, (tg + 1) * 512)
                for c in range(4):
                    pa, pak = rot.next()
                    pg, pgk = rot.next()
                    for which, pt, pk in ((0, pa, pak), (1, pg, pgk)):
                        for k in range(8):
                            ph.op("pe", lambda e, pt=pt, k=k, which=which, c=c, tsl=tsl: e.matmul(
                                pt[:], lhsT=wgl[:, which, k, c * 128:(c + 1) * 128], rhs=uT[:, k, tsl], start=(k == 0), stop=(k == 7)),
                                reads=[("wgl", which, k // 2)], writes=[pk])
                    s, sk = sg[c % 2], ("sg", c % 2)
                    ph.op("act", lambda e, s=s, pg=pg: e.activation(out=s[:], in_=pg[:], func=AF.Tanh, scale=0.5), reads=[pgk], writes=[sk])
                    ph.op("dve", lambda e, s=s, pa=pa, c=c, tg=tg: e.scalar_tensor_tensor(
                        out=hT[:, c, 32 + tg * 512:32 + (tg + 1) * 512], in0=s[:], scalar=1.0, in1=pa[:], op0=ALU.add, op1=ALU.mult),
                        reads=[sk, pak], writes=[("hT", tg)])
            if hf == 0:
                ph.op("dve", lambda e: e.tensor_copy(out=halo[:], in_=hT[:, :, HALF:HALF + 32]), reads=[("hT", 3)], writes=["halo"])
            crot = PsumRot(pbank[4:6], "conv")
            for tg in range(4):
                hk = ["hT_halo"] + [("hT", i) for i in range(max(0, tg - 1), tg + 1)]
                s1p, s2p = pbank[6], pbank[7]
                for c in range(4):
                    pt, pk = crot.next()
                    for k in range(KC):
                        c0 = 32 + tg * 512 - 30 + k
                        ph.op("pe", lambda e, pt=pt, c=c, k=k, c0=c0: e.matmul(
                            pt[:], lhsT=dmat[:, c * KC + k, :], rhs=hT[:, c, c0:c0 + 512], start=(k == 0), stop=(k == KC - 1)),
                            reads=hk + [("dmat", c * KC + k)], writes=[pk])
                    ph.op("act", lambda e, pt=pt, c=c: e.activation(out=cv[:, c, :], in_=pt[:], func=AF.Identity, bias=prm[:, l, 80 + c:81 + c]),
                          reads=[pk], writes=[("cv", c)])
                    ph.op("act", lambda e, c=c: e.activation(out=csq[:, c, :], in_=cv[:, c, :], func=AF.Square), reads=[("cv", c)], writes=[("csq", c)])
                    ph.op("act", lambda e, c=c: e.activation(out=cvb[:, c, :], in_=cv[:, c, :], func=AF.Copy), reads=[("cv", c)], writes=[("cvb", c)])
                for c in range(4):
                    ph.op("pe", lambda e, c=c: e.matmul(s1p[:], lhsT=ones_b[:], rhs=cvb[:, c, :], start=(c == 0), stop=(c == 3)),
                          reads=[("cvb", c)], writes=["s1p"])
                for c in range(4):
                    ph.op("pe", lambda e, c=c: e.matmul(s2p[:], lhsT=ones_b[:], rhs=csq[:, c, :], start=(c == 0), stop=(c == 3)),
                          reads=[("csq", c)], writes=["s2p"])
                ln_stats(ph, s1p, s2p, st1, st2, st3, mhalf, 1.0 / CW)
                for c in range(4):
                    z, zk = zt[c % 2], ("zt", c % 2)
                    t_, tk_ = th[c % 2], ("th", c % 2)
                    ph.op("dve", lambda e, c=c: e.tensor_tensor(out=cv[:, c, :], in0=cv[:, c, :], in1=st1[:], op=ALU.subtract),
                          reads=[("cv", c), "mean"], writes=[("cv", c)])
                    ph.op("dve", lambda e, c=c: e.tensor_tensor(out=cv[:, c, :], in0=cv[:, c, :], in1=st3[:], op=ALU.mult),
                          reads=[("cv", c), "rstd"], writes=[("cv", c)])
                    ph.op("dve", lambda e, c=c, z=z: e.tensor_scalar(out=z[:], in0=cv[:, c, :], scalar1=prm[:, l, 84 + c:85 + c],
                                                                     scalar2=prm[:, l, 88 + c:89 + c], op0=ALU.mult, op1=ALU.add),
                          reads=[("cv", c)], writes=[zk])
                    ph.op("act", lambda e, z=z, t_=t_: e.activation(out=t_[:], in_=z[:], func=AF.Tanh), reads=[zk], writes=[tk_])
                    ph.op("dve", lambda e, c=c, z=z, t_=t_, tg=tg: e.scalar_tensor_tensor(
                        out=cvT[:, c, tg * 512:(tg + 1) * 512], in0=t_[:], scalar=1.0, in1=z[:], op0=ALU.add, op1=ALU.mult),
                        reads=[zk, tk_], writes=[("cvT", tg)])
            ph.emit()
        es_at = contextlib.ExitStack()
        attnT = es_at.enter_context(_sbt(nc, "attnT", [64, H, HALF], BF16))

        nk = HALF * (hf + 1)
        nkt = nk // 128
        with contextlib.ExitStack() as esc:
            sbc = lambda name, shape, dt=F32: esc.enter_context(_sbt(nc, name, shape, dt))
            Qa = [sbc(f"Qa{i}", [80, HALF], BF16) for i in range(2)]
            Ka = [sbc(f"Ka{i}", [80, S], BF16) for i in range(2)]
            Va = [sbc(f"Va{i}", [128, 32, 128], BF16) for i in range(2)]
            pT = [sbc(f"pT{i}", [128, 512], BF16) for i in range(4)]
            rec = [sbc(f"rec{i}", [64, 512], F32) for i in range(2)]
            ph = Phase(nc, nm + "c")
            for i in range(2):
                ph.op("dve", lambda e, i=i: e.memset(Va[i][:, :, 64:128], 1.0), writes=[("Va1", i)])
                ph.dma("sp", Ka[i][64:80, 0:nk], indic[:, 0:nk] if hf == 1 else indic[:, 0:nk], writes=[("Kai", i)])
            if hf == 0:
                pass
            srot = PsumRot(pbank[0:4], "sT")
            orot = PsumRot(pbank[4:6], "oT")
            pi = 0
            pend = []
            LAG = 2

            def flush(n_keep):
                while len(pend) > n_keep:
                    pend.pop(0)()

            for h in range(H):
                b = h % 2
                ph.dma("sp", Qa[b][0:64, :], qT_d[h], writes=[("Qa", b)])
                ph.dma("sp", Qa[b][64:80, :], bT_d[h], writes=[("Qab", b)])
                ph.dma("sp", Ka[b][0:64, 0:nk], kT_d[h, :, 0:nk], writes=[("Ka", b)])
                ph.dma("sp", Va[b][:, 0:nkt, 0:64], v_d[0:nk, h * 64:(h + 1) * 64].rearrange("(t p) d -> p t d", p=128),
                       writes=[("Va", b)])
                rd = [("Qa", b), ("Qab", b), ("Ka", b), ("Kai", b)]
                for qg in range(4):
                    op_, ok_ = orot.next()
                    ktiles = list(range(hf * 16 + qg * 4 + 4))
                    for kt in ktiles:
                        lt = kt - hf * 16
                        r0 = max(0, lt - qg * 4)
                        q0 = qg * 512 + r0 * 128
                        n = 512 - r0 * 128
                        sp_, sk_ = srot.next()
                        diag = lt >= qg * 4
                        ph.op("pe", lambda e, sp_=sp_, b=b, kt=kt, q0=q0, n=n, diag=diag: e.matmul(
                            sp_[:, 0:n], lhsT=Ka[b][:, kt * 128:(kt + 1) * 128], rhs=Qa[b][:, q0:q0 + n], start=True, stop=not diag),
                            reads=rd, writes=[sk_])
                        if diag:
                            ph.op("pe", lambda e, sp_=sp_: e.matmul(sp_[:, 0:128], lhsT=ident_b[:], rhs=trimask[:], start=False, stop=True),
                                  reads=[], writes=[sk_])
                        p_, pk_ = pT[pi % 4], ("pT", pi % 4)
                        pi += 1
                        ph.op("act", lambda e, p_=p_, sp_=sp_, n=n: e.activation(out=p_[:, 0:n], in_=sp_[:, 0:n], func=AF.Exp, scale=HD ** -0.5),
                              reads=[sk_], writes=[pk_])
                        c0 = r0 * 128

                        def pv(op_=op_, ok_=ok_, p_=p_, pk_=pk_, b=b, kt=kt, c0=c0, n=n, first=(kt == 0), last=(kt == ktiles[-1])):
                            ph.op("pe", lambda e: e.matmul(op_[:, c0:c0 + n], lhsT=Va[b][:, kt, :], rhs=p_[:, 0:n], start=first, stop=last),
                                  reads=[pk_, ("Va", b), ("Va1", b)], writes=[ok_])
                        pend.append(pv)
                        flush(LAG)

                    def norm(op_=op_, ok_=ok_, h=h, qg=qg):
                        rc, rk = rec[qg % 2], ("rec", qg % 2)
                        ph.op("dve", lambda e: e.reciprocal(out=rc[:], in_=op_[64:128, :]), reads=[ok_], writes=[rk])
                        ph.op("dve", lambda e: e.tensor_tensor(out=attnT[:, h, qg * 512:(qg + 1) * 512], in0=op_[0:64, :], in1=rc[:], op=ALU.mult),
                              reads=[ok_, rk], writes=[("attnT", h)])
                    pend.append(norm)
            flush(0)
            ph.emit()

        with contextlib.ExitStack() as esd:
            sbd = lambda name, shape, dt=F32: esd.enter_context(_sbt(nc, name, shape, dt))
            wga = [sbd(f"wga{i}", [128, 8, 512], BF16) for i in range(2)]
            wgc = [sbd(f"wgc{i}", [128, 8, 512], BF16) for i in range(2)]
            wbr = [sbd(f"wbr{i}", [64, 8, 512], BF16) for i in range(2)]
            wpw = [sbd(f"wpw{i}", [128, 4, 512], BF16) for i in range(2)]
            sa = [sbd(f"sa{i}", [128, 512], F32) for i in range(2)]
            sc_ = [sbd(f"sc{i}", [128, 512], F32) for i in range(2)]
            m1 = [sbd(f"m1{i}", [128, 512], F32) for i in range(2)]
            ph = Phase(nc, nm + "d1")
            rot = PsumRot(pbank[0:8], "mrg")
            for cb in range(2):
                wload(ph, wbr[cb][:], wbr_d[l, cb], ("wbr", cb), 4 if cb == 0 else 1)
                wload(ph, wpw[cb][:], wpw_d[l, cb], ("wpw", cb), 2 if cb == 0 else 1, npieces=2)
                wload(ph, wga[cb][:], win_d[l, 5 + cb], ("wga", cb), 4 if cb == 0 else 1)
                wload(ph, wgc[cb][:], win_d[l, 7 + cb], ("wgc", cb), 4 if cb == 0 else 1)
            it = 0
            for m in range(8):
                cb, mc = m // 4, (m % 4) * 128
                for tg in range(4):
                    tsl = slice(tg * 512, (tg + 1) * 512)
                    pya, kya = rot.next()
                    pyc, kyc = rot.next()
                    pga, kga = rot.next()
                    pgc, kgc = rot.next()
                    for hh in range(H):
                        ph.op("pe", lambda e, pya=pya, hh=hh, cb=cb, mc=mc, tsl=tsl: e.matmul(
                            pya[:], lhsT=wbr[cb][:, hh, mc:mc + 128], rhs=attnT[:, hh, tsl], start=(hh == 0), stop=(hh == H - 1)),
                            reads=[("wbr", cb, hh // 2)], writes=[kya])
                    for k in range(4):
                        ph.op("pe", lambda e, pyc=pyc, k=k, cb=cb, mc=mc, tsl=tsl: e.matmul(
                            pyc[:], lhsT=wpw[cb][:, k, mc:mc + 128], rhs=cvT[:, k, tsl], start=(k == 0), stop=(k == 3)),
                            reads=[("wpw", cb, k // 2)], writes=[kyc])
                    for k in range(8):
                        ph.op("pe", lambda e, pga=pga, k=k, cb=cb, mc=mc, tsl=tsl: e.matmul(
                            pga[:], lhsT=wga[cb][:, k, mc:mc + 128], rhs=uT[:, k, tsl], start=(k == 0), stop=(k == 7)),
                            reads=[("wga", cb, k // 2)], writes=[kga])
                    for k in range(8):
                        ph.op("pe", lambda e, pgc=pgc, k=k, cb=cb, mc=mc, tsl=tsl: e.matmul(
                            pgc[:], lhsT=wgc[cb][:, k, mc:mc + 128], rhs=uT[:, k, tsl], start=(k == 0), stop=(k == 7)),
                            reads=[("wgc", cb, k // 2)], writes=[kgc])
                    i2 = it % 2
                    it += 1
                    ph.op("act", lambda e, i2=i2, pga=pga: e.activation(out=sa[i2][:], in_=pga[:], func=AF.Tanh, scale=0.5), reads=[kga], writes=[("sa", i2)])
                    ph.op("act", lambda e, i2=i2, pgc=pgc: e.activation(out=sc_[i2][:], in_=pgc[:], func=AF.Tanh, scale=0.5), reads=[kgc], writes=[("sc", i2)])
                    ph.op("dve", lambda e, i2=i2, pya=pya: e.scalar_tensor_tensor(out=m1[i2][:], in0=sa[i2][:], scalar=1.0, in1=pya[:], op0=ALU.add, op1=ALU.mult),
                          reads=[("sa", i2), kya], writes=[("m1", i2)])
                    ph.op("dve", lambda e, i2=i2, pyc=pyc: e.scalar_tensor_tensor(out=sc_[i2][:], in0=sc_[i2][:], scalar=1.0, in1=pyc[:], op0=ALU.add, op1=ALU.mult),
                          reads=[("sc", i2), kyc], writes=[("sc", i2)])
                    ph.op("dve", lambda e, i2=i2, m=m, tsl=tsl: e.tensor_tensor(out=mg[:, m, tsl], in0=m1[i2][:], in1=sc_[i2][:], op=ALU.add),
                          reads=[("m1", i2), ("sc", i2)], writes=[("mg", m)])
            ph.emit()
        es_at.close()
        es_cv.close()
        esh.close()
        with contextlib.ExitStack() as esd2:
            xT = esd2.enter_context(_sbt(nc, "xT", [128, 8, HALF], F32))
            wo = [esd2.enter_context(_sbt(nc, f"wo{i}", [128, 8, 512], BF16)) for i in range(2)]
            ph = Phase(nc, nm + "d2")
            for tg in range(4):
                ph.dma("sp", xT[:, :, tg * 512:(tg + 1) * 512], xs3[:, :, tg * 512:(tg + 1) * 512], writes=[("xld", tg)])
            for cb in range(2):
                wload(ph, wo[cb][:], wout_d[l, cb], ("wo", cb), 4 if cb == 0 else 1)
            def ymm(ph, pt, pk, mo, tg):
                cb, mc = mo // 4, (mo % 4) * 128
                tsl = slice(tg * 512, (tg + 1) * 512)
                for k in range(8):
                    ph.op("pe", lambda e, k=k: e.matmul(pt[:], lhsT=wo[cb][:, k, mc:mc + 128], rhs=mg[:, k, tsl], start=(k == 0), stop=(k == 7)),
                          reads=[("wo", cb, k // 2)], writes=[pk])
            deepnorm_ln(nc, ph, esd2, ymm, xT, prm, l, 2, 6, 7, pbank, ones_b, ntg=4, conc=2)
            for tg in range(4):
                ph.dma("sp", xs3[:, :, tg * 512:(tg + 1) * 512], xT[:, :, tg * 512:(tg + 1) * 512],
                       reads=[("x", mo, tg) for mo in range(8)], writes=[("xst", tg)])
            if dbg_d.get("x1") is not None and l == 0:
                ph.dma("sp", dbg_d["x1"][hf], xT[:].rearrange("p k t -> p (k t)"), reads=[("x", mo, tg) for mo in range(8) for tg in range(4)])
            ph.emit()
    es_mg.close()

    for grp in range(2):
        g0 = grp * 1024
        with contextlib.ExitStack() as ese:
            sbe = lambda name, shape, dt=F32: ese.enter_context(_sbt(nc, name, shape, dt))
            xg = sbe("xg", [128, 8, 1024], F32)
            hm = sbe("hm", [128, 32, 1024], BF16)
            with contextlib.ExitStack() as ese1:
                sb1 = lambda name, shape, dt=F32: ese1.enter_context(_sbt(nc, name, shape, dt))
                u2 = sb1("u2", [128, 8, 1024], BF16)
                wu = [sb1(f"wu{i}", [128, 8, 512], BF16) for i in range(2)]
                rl = [sb1(f"rl{i}", [128, 512], BF16) for i in range(3)]
                ph = Phase(nc, nm + f"e{grp}")
                ph.dma("sp", xg[:], xs3[:, :, g0:g0 + 1024], writes=["xT"])
                for k in range(8):
                    ph.op("dve", lambda e, k=k: e.tensor_scalar(out=u2[:, k, :], in0=xg[:, k, :], scalar1=P(l, 3, k), scalar2=P(l, 4, k),
                                                                op0=ALU.mult, op1=ALU.add), reads=["xT"], writes=["u2"])
                rot = PsumRot(pbank[0:4], "up")
                it = 0
                for jb in range(8):
                    w, wk = wu[jb % 2], ("wu", jb % 2)
                    wload(ph, w[:], wup_d[l, jb], wk, 4 if jb == 0 else 1)
                    for jj in range(4):
                        j = jb * 4 + jj
                        for sub in range(2):
                            pt, pk = rot.next()
                            for k in range(8):
                                ph.op("pe", lambda e, pt=pt, k=k, w=w, jj=jj, sub=sub: e.matmul(
                                    pt[:], lhsT=w[:, k, jj * 128:(jj + 1) * 128], rhs=u2[:, k, sub * 512:(sub + 1) * 512], start=(k == 0), stop=(k == 7)),
                                    reads=[wk + ((k // 2),), "u2"], writes=[pk])
                            r, rk = rl[it % 3], ("rl", it % 3)
                            if it % 4 == 3:
                                ph.op("dve", lambda e, r=r, pt=pt: e.tensor_scalar(out=r[:], in0=pt[:], scalar1=0.0, scalar2=None, op0=ALU.max),
                                      reads=[pk], writes=[rk])
                            else:
                                ph.op("act", lambda e, r=r, pt=pt: e.activation(out=r[:], in_=pt[:], func=AF.Relu), reads=[pk], writes=[rk])
                            ph.op("dve", lambda e, r=r, j=j, sub=sub: e.tensor_tensor(out=hm[:, j, sub * 512:(sub + 1) * 512], in0=r[:], in1=r[:], op=ALU.mult),
                                  reads=[rk], writes=[("hm", j)])
                            it += 1
                ph.emit()
            with contextlib.ExitStack() as ese2:
                wd = [ese2.enter_context(_sbt(nc, f"wd{i}", [128, 32, 128], BF16)) for i in range(2)]
                ph = Phase(nc, nm + f"f{grp}")
                def ymm2(ph, pt, pk, mo, tg):
                    w, wk = wd[mo % 2], ("wd", mo % 2)
                    tsl = slice(tg * 512, (tg + 1) * 512)
                    if tg == 0:
                        wload(ph, w[:], wdn_d[l, mo], wk, 4 if mo == 0 else 1)
                    for k in range(32):
                        ph.op("pe", lambda e, k=k: e.matmul(pt[:], lhsT=w[:, k, :], rhs=hm[:, k, tsl], start=(k == 0), stop=(k == 31)),
                              reads=[wk + ((k // 8),)], writes=[pk])
                deepnorm_ln(nc, ph, ese2, ymm2, xg, prm, l, 5, 8, 9, pbank, ones_b, ntg=2, conc=2)
                ph.dma("sp", xs3[:, :, g0:g0 + 1024], xg[:], reads=[("x", mo, tg) for mo in range(8) for tg in range(2)], writes=["xst"])
                ph.emit()


def ln_stats(ph, s1p, s2p, st1, st2, st3, mhalf, inv_n):
    ph.op("dve", lambda e: e.tensor_scalar(out=st1[:], in0=s1p[:], scalar1=inv_n, scalar2=None, op0=ALU.mult), reads=["s1p"], writes=["mean"])
    ph.op("dve", lambda e: e.tensor_tensor(out=st2[:], in0=st1[:], in1=st1[:], op=ALU.mult), reads=["mean"], writes=["msq"])
    ph.op("dve", lambda e: e.scalar_tensor_tensor(out=st2[:], in0=s2p[:], scalar=inv_n, in1=st2[:], op0=ALU.mult, op1=ALU.subtract),
          reads=["s2p", "msq"], writes=["var"])
    ph.op("dve", lambda e: e.tensor_scalar(out=st2[:], in0=st2[:], scalar1=0.0, scalar2=EPS, op0=ALU.max, op1=ALU.add), reads=["var"], writes=["var2"])
    ph.op("act", lambda e: e.activation(out=st2[:], in_=st2[:], func=AF.Sqrt), reads=["var2"], writes=["sd"])
    ph.op("dve", lambda e: e.reciprocal(out=st3[:], in_=st2[:]), reads=["sd"], writes=["rstd"])


def deepnorm_ln(nc, ph, es, ymm, xT, prm, l, i_gt, i_g, i_b, pbank, ones_b, ntg, conc):
    sbx = lambda name, shape, dt=F32: es.enter_context(_sbt(nc, name, shape, dt))
    tb = [sbx(f"tb{i}", [128, 512], BF16) for i in range(4)]
    tq = [sbx(f"tq{i}", [128, 512], BF16) for i in range(4)]
    pendS = []
    st1 = [sbx(f"lst1{i}", [128, 512], F32) for i in range(conc)]
    st2 = sbx("lst2", [128, 512], F32)
    st3 = [sbx(f"lst3{i}", [128, 512], F32) for i in range(conc)]
    mhalf = sbx("lmhalf", [128, 512], F32)
    ph.op("pool", lambda e: e.memset(mhalf[:], -0.5), writes=["mhalf"])
    nyb = 8 - 2 * conc
    rot = PsumRot(pbank[0:nyb], "y")
    it = 0
    for base in range(0, ntg, conc):
        for mo in range(8):
            for ci in range(conc):
                tg = base + ci
                xsl = slice(tg * 512, (tg + 1) * 512)
                s1p, s2p = pbank[nyb + 2 * ci], pbank[nyb + 2 * ci + 1]
                pt, pk = rot.next()
                ymm(ph, pt, pk, mo, tg)
                xk = ("x", mo, tg)
                ph.op("dve", lambda e, pt=pt, mo=mo, xsl=xsl: e.scalar_tensor_tensor(
                    out=xT[:, mo, xsl], in0=pt[:], scalar=prm[:, l, i_gt * 8 + mo:i_gt * 8 + mo + 1], in1=xT[:, mo, xsl], op0=ALU.mult, op1=ALU.add),
                    reads=[pk, ("xld", tg)], writes=[xk])
                i2 = it % 4
                it += 1
                ph.op("act", lambda e, i2=i2, mo=mo, xsl=xsl: e.activation(out=tb[i2][:], in_=xT[:, mo, xsl], func=AF.Copy), reads=[xk], writes=[("tb", i2)])
                ph.op("act", lambda e, i2=i2, mo=mo, xsl=xsl: e.activation(out=tq[i2][:], in_=xT[:, mo, xsl], func=AF.Square), reads=[xk], writes=[("tq", i2)])
                def stats(i2=i2, mo=mo, s1p=s1p, s2p=s2p, ci=ci):
                    ph.op("pe", lambda e: e.matmul(s1p[:], lhsT=ones_b[:], rhs=tb[i2][:], start=(mo == 0), stop=(mo == 7)),
                          reads=[("tb", i2)], writes=[("s1p", ci)])
                    ph.op("pe", lambda e: e.matmul(s2p[:], lhsT=ones_b[:], rhs=tq[i2][:], start=(mo == 0), stop=(mo == 7)),
                          reads=[("tq", i2)], writes=[("s2p", ci)])
                pendS.append(stats)
                while len(pendS) > 2:
                    pendS.pop(0)()
        while pendS:
            pendS.pop(0)()
        for ci in range(conc):
            tg = base + ci
            xsl = slice(tg * 512, (tg + 1) * 512)
            s1p, s2p = pbank[nyb + 2 * ci], pbank[nyb + 2 * ci + 1]
            a, c = st1[ci], st3[ci]
            ph.op("dve", lambda e, a=a, s1p=s1p: e.tensor_scalar(out=a[:], in0=s1p[:], scalar1=1.0 / D, scalar2=None, op0=ALU.mult),
                  reads=[("s1p", ci)], writes=[("mean", ci)])
            ph.op("dve", lambda e, a=a: e.tensor_tensor(out=st2[:], in0=a[:], in1=a[:], op=ALU.mult), reads=[("mean", ci)], writes=["msq"])
            ph.op("dve", lambda e, s2p=s2p: e.scalar_tensor_tensor(out=st2[:], in0=s2p[:], scalar=1.0 / D, in1=st2[:], op0=ALU.mult, op1=ALU.subtract),
                  reads=[("s2p", ci), "msq"], writes=["var"])
            ph.op("dve", lambda e: e.tensor_scalar(out=st2[:], in0=st2[:], scalar1=0.0, scalar2=EPS, op0=ALU.max, op1=ALU.add), reads=["var"], writes=["var2"])
            ph.op("act", lambda e: e.activation(out=st2[:], in_=st2[:], func=AF.Sqrt), reads=["var2"], writes=["sd"])
            ph.op("dve", lambda e, c=c: e.reciprocal(out=c[:], in_=st2[:]), reads=["sd"], writes=[("rstd", ci)])
            for mo in range(8):
                xk = ("x", mo, tg)
                ph.op("dve", lambda e, mo=mo, xsl=xsl, a=a: e.tensor_tensor(out=xT[:, mo, xsl], in0=xT[:, mo, xsl], in1=a[:], op=ALU.subtract),
                      reads=[xk, ("mean", ci)], writes=[xk])
                ph.op("dve", lambda e, mo=mo, xsl=xsl, c=c: e.tensor_tensor(out=xT[:, mo, xsl], in0=xT[:, mo, xsl], in1=c[:], op=ALU.mult),
                      reads=[xk, ("rstd", ci)], writes=[xk])
                ph.op("act", lambda e, mo=mo, xsl=xsl: e.activation(
                    out=xT[:, mo, xsl], in_=xT[:, mo, xsl], func=AF.Identity, scale=prm[:, l, i_g * 8 + mo:i_g * 8 + mo + 1],
                    bias=prm[:, l, i_b * 8 + mo:i_b * 8 + mo + 1]), reads=[xk], writes=[xk])


def _cols(v):
    n = v.shape[-1] // 128
    return np.swapaxes(v.reshape(v.shape[:-1] + (n, 128)), -1, -2)


def _blk(w, bw):
    Lw, K, N = w.shape
    return np.ascontiguousarray(w.reshape(Lw, K // 128, 128, N // bw, bw).transpose(0, 3, 2, 1, 4)).reshape(Lw, N // bw, 128, (K // 128) * bw)


def prep_shared(inp):
    f = lambda a: np.asarray(a, dtype=np.float32)
    pvec = np.concatenate([_cols(f(inp["ln_mix_g"])), _cols(f(inp["ln_mix_b"])), _cols(f(inp["ln_ffn_g"])), _cols(f(inp["ln_ffn_b"])),
                           _cols(f(inp["b_ada"])), _cols(f(inp["conv_dw_b"])), _cols(f(inp["conv_ln_g"])), _cols(f(inp["conv_ln_b"]))], axis=-1)
    assert pvec.shape == (L, 128, 92), pvec.shape
    dwt = f(inp["conv_dw_w"])
    dww = np.ascontiguousarray(dwt.reshape(L, KC, 4, 128).transpose(0, 3, 2, 1)).reshape(L, 128, 4 * KC)
    wbr = f(inp["w_attn_br"]).reshape(L, H, 64, 2, 512).transpose(0, 3, 2, 1, 4)
    wbr = np.ascontiguousarray(wbr).reshape(L, 2, 64, 8 * 512)
    return {
        "pvec": np.ascontiguousarray(pvec), "dww": dww,
        "wada": _blk(f(inp["w_ada"]), 512), "win": _blk(f(inp["w_in"]), 512), "wbr": wbr,
        "wpw": _blk(f(inp["w_conv_pw"]), 512), "wout": _blk(f(inp["w_out"]), 512),
        "wup": _blk(f(inp["w_up"]), 512), "wdn": _blk(f(inp["w_down"]), 128),
    }


NPV = 92
_NC_CACHE = {}


def kernel(**inputs):
    shared = prep_shared(inputs)
    x = np.asarray(inputs["x"], dtype=np.float32)
    c = np.asarray(inputs["c"], dtype=np.float32)
    pos = np.asarray(inputs["positions"], dtype=np.int32)
    if "nc" not in _NC_CACHE:
        _NC_CACHE["nc"] = build(L)
    nc = _NC_CACHE["nc"]
    work = {0: 0, 1: 1, 4: 2, 5: 3}
    zeros = {k: np.zeros_like(v) for k, v in shared.items()}
    zeros["x"] = np.zeros((S, D), np.float32)
    zeros["cT"] = np.zeros((128, 8), np.float32)
    zeros["pos"] = np.zeros((128, 32), np.int32)
    in_maps = []
    for core in range(8):
        if core not in work:
            in_maps.append(zeros)
            continue
        b = work[core]
        m = dict(shared)
        m["x"] = np.ascontiguousarray(x[b])
        m["cT"] = np.ascontiguousarray(c[b].reshape(8, 128).T)
        m["pos"] = np.ascontiguousarray(pos[b].reshape(32, 128).T)
        in_maps.append(m)
    res = run_bass_kernel_spmd(nc, in_maps, core_ids=list(range(8)))
    inv = {b: core for core, b in work.items()}
    return np.stack([res.results[inv[b]]["out"] for b in range(B)], axis=0).astype(np.float32)
```

```python
import contextlib
import math
import numpy as np
import concourse.bass as bass
import concourse.mybir as mybir
from concourse.bass_utils import run_bass_kernel_spmd

F32 = mybir.dt.float32
BF16 = mybir.dt.bfloat16
I32 = mybir.dt.int32
AF = mybir.ActivationFunctionType
ALU = mybir.AluOpType
AX = mybir.AxisListType

D = 1024
S = 4096
B = 4
L = 4
H = 8
HD = 64
CW = 512
KC = 31
DFF = 4096
HALF = 2048
ALPHA = (2 * L) ** 0.25
EPS = 1e-5
BIG = 30000.0
NPV = 28 + 48 + 12


class _Op:
    __slots__ = ("eng", "fn", "deps", "is_dma", "idx", "sig", "sem", "semval", "prev_slot_val")


class Phase:
    NDMA = 12

    def __init__(self, nc, name):
        self.nc, self.name = nc, name
        self.ops, self.last_w, self.readers = [], {}, {}

    def op(self, eng, fn, reads=(), writes=(), dma=False):
        o = _Op()
        o.eng, o.fn, o.is_dma = eng, fn, dma
        deps = set()
        for k in reads:
            w = self.last_w.get(k)
            if w is not None:
                deps.add(w)
        for k in writes:
            w = self.last_w.get(k)
            if w is not None:
                deps.add(w)
            deps.update(self.readers.get(k, ()))
        o.idx = len(self.ops)
        deps.discard(o.idx)
        o.deps = deps
        self.ops.append(o)
        for k in reads:
            self.readers.setdefault(k, []).append(o.idx)
        for k in writes:
            self.last_w[k] = o.idx
            self.readers[k] = []
        return o.idx

    def dma(self, eng, out, in_, reads=(), writes=(), **kw):
        return self.op(eng, lambda e: e.dma_start(out=out, in_=in_, **kw), reads, writes, dma=True)

    def emit(self):
        for suf, n in TRUNC.items():
            if self.name.endswith(suf):
                print("phase", self.name, "ops", len(self.ops), "-> trunc", n)
                self.ops = self.ops[:n]
        nc, ops = self.nc, self.ops
        engs = ["pe", "act", "dve", "pool", "sp"]
        streams = {e: [o for o in ops if o.eng == e] for e in engs}

        def skip(p, o):
            return p.eng == "pe" and o.eng == "pe" and not p.is_dma and not o.is_dma

        for o in ops:
            o.sig = o.is_dma
        for o in ops:
            for d in o.deps:
                if not skip(ops[d], o):
                    ops[d].sig = True
        if id(nc) not in _SEMS:
            esem = {e: nc.semaphore(f"trk_{e}").__enter__() for e in engs}
            dsem = {e: [nc.semaphore(f"trk_d{e}{i}").__enter__() for i in range(self.NDMA)] for e in ("sp", "pool")}
            _SEMS[id(nc)] = (esem, dsem, {e: [0] * self.NDMA for e in dsem}, {e: 0 for e in dsem})
        esem, dsem, dcount, dnext = _SEMS[id(nc)]
        with nc.Block() as b0:
            def clr(e):
                for s_ in list(esem.values()):
                    e.sem_clear(s_)
            b0.gpsimd(clr)
        if True:
            ecount = {e: 0 for e in engs}
            dstart = {e: list(dcount[e]) for e in dsem}
            for o in ops:
                if o.is_dma:
                    s = dnext[o.eng] % self.NDMA
                    dnext[o.eng] += 1
                    o.prev_slot_val = dcount[o.eng][s]
                    dcount[o.eng][s] += 16
                    o.sem, o.semval = dsem[o.eng][s], dcount[o.eng][s]
                elif o.sig:
                    ecount[o.eng] += 1
                    o.sem, o.semval = esem[o.eng], ecount[o.eng]
                else:
                    o.sem, o.semval = None, 0
            with nc.Block() as blk:
                def run(e_name):
                    def body(e):
                        waited = {}
                        for o in streams[e_name]:
                            need = {}
                            for d in o.deps:
                                p = ops[d]
                                if skip(p, o):
                                    continue
                                key = id(p.sem)
                                if key not in need or need[key][1] < p.semval:
                                    need[key] = (p.sem, p.semval)
                            if o.is_dma and o.prev_slot_val > 0:
                                key = id(o.sem)
                                if key not in need or need[key][1] < o.prev_slot_val:
                                    need[key] = (o.sem, o.prev_slot_val)
                            for key, (s, v) in need.items():
                                if waited.get(key, 0) >= v:
                                    continue
                                e.wait_ge(s, v)
                                waited[key] = v
                            ins = o.fn(e)
                            if o.is_dma:
                                ins.then_inc(o.sem, 16)
                            elif o.sig:
                                ins.then_inc(o.sem, 1)
                        if e_name in dsem:
                            for i, s in enumerate(dsem[e_name]):
                                v = dcount[e_name][i]
                                if v > dstart[e_name][i] and waited.get(id(s), 0) < v:
                                    e.wait_ge(s, v)
                    return body
                for e_name, reg in (("pe", blk.tensor), ("act", blk.scalar), ("dve", blk.vector),
                                    ("pool", blk.gpsimd), ("sp", blk.sync)):
                    if streams[e_name]:
                        reg(run(e_name))
        if STOP[0] is not None and self.name.endswith(STOP[0]):
            raise StopBuild()


_UID = [0]


def _sbt(nc, name, shape, dt):
    _UID[0] += 1
    return nc.sbuf_tensor(f"{name}_u{_UID[0]}", shape, dt)


_SEMS = {}


class StopBuild(Exception):
    pass


STOP = [None]
TRUNC = {}


def wload(ph, dst, src, key, nsplit, npieces=4):
    K, N = dst.shape[1], dst.shape[2]
    step = K // nsplit
    for pc in range(nsplit):
        keys = [key + (q,) for q in range(pc * npieces // nsplit, (pc + 1) * npieces // nsplit)]
        ph.dma("pool", dst[:, pc * step:(pc + 1) * step, :].rearrange("p k n -> p (k n)"),
               src[:, pc * step * N:(pc + 1) * step * N], writes=keys, max_dma_last_dim=8192)


class PsumRot:
    def __init__(self, tiles, tag):
        self.tiles, self.tag, self.i = tiles, tag, 0

    def next(self):
        t = self.tiles[self.i % len(self.tiles)]
        k = (self.tag, self.i % len(self.tiles))
        self.i += 1
        return t, k


def build(depth=L, dbg=None):
    nc = bass.Bass("TRN2", target_bir_lowering=False)
    dt_in = lambda name, shape, dt=F32: nc.dram_tensor(name, shape, dt, kind="ExternalInput").ap()
    x_d = dt_in("x", [S, D])
    cT_d = dt_in("cT", [128, 8])
    pos_d = dt_in("pos", [128, 32], I32)
    pv_d = dt_in("pvec", [L, 128, NPV])
    dw_d = dt_in("dww", [L, 128, 4 * KC])
    wada_d = dt_in("wada", [L, 12, 128, 8 * 512])
    win_d = dt_in("win", [L, 9, 128, 8 * 512])
    wbr_d = dt_in("wbr", [L, 2, 64, 8 * 512])
    wpw_d = dt_in("wpw", [L, 2, 128, 4 * 512])
    wout_d = dt_in("wout", [L, 2, 128, 8 * 512])
    wup_d = dt_in("wup", [L, 8, 128, 8 * 512])
    wdn_d = dt_in("wdn", [L, 8, 128, 32 * 128])
    out_d = nc.dram_tensor("out", [S, D], F32, kind="ExternalOutput").ap()
    dbg_d = {}
    if dbg:
        for name, shape, dt in dbg:
            dbg_d[name] = nc.dram_tensor("dbg_" + name, shape, dt, kind="ExternalOutput").ap()

    xs_d = nc.dram_tensor("xs", [2, 128, 8 * HALF], F32).ap()
    qT_d = nc.dram_tensor("qTd", [H, 64, HALF], BF16).ap()
    kT_d = nc.dram_tensor("kTd", [H, 64, S], BF16).ap()
    v_d = nc.dram_tensor("vd", [S, 512], BF16).ap()
    bT_d = nc.dram_tensor("bTd", [H, 16, HALF], BF16).ap()

    es = contextlib.ExitStack()
    sb = lambda name, shape, dt=F32: es.enter_context(_sbt(nc, name, shape, dt))
    ps = lambda name, shape, dt=F32: es.enter_context(nc.psum_tensor(name, shape, dt))

    try:
      with es:
        _build_body(nc, es, sb, ps, locals(), depth)
    except StopBuild:
        pass
    return nc


def _build_body(nc, es, sb, ps, G, depth):
    x_d, cT_d, pos_d, pv_d, dw_d, wada_d, out_d = G["x_d"], G["cT_d"], G["pos_d"], G["pv_d"], G["dw_d"], G["wada_d"], G["out_d"]
    win_d, wbr_d, wpw_d, wout_d, wup_d, wdn_d = G["win_d"], G["wbr_d"], G["wpw_d"], G["wout_d"], G["wup_d"], G["wdn_d"]
    xs_d, qT_d, kT_d, v_d, bT_d, dbg_d = G["xs_d"], G["qT_d"], G["kT_d"], G["v_d"], G["bT_d"], G["dbg_d"]
    if True:
        ident_b = sb("ident_b", [128, 128], BF16)
        ident_f = sb("ident_f", [128, 128], F32)
        ones_b = sb("ones_b", [128, 128], BF16)
        trimask = sb("trimask", [128, 128], BF16)
        indic = sb("indic", [16, S], BF16)
        cos_t = sb("cos_t", [128, 32, 8], F32)
        sin_t = sb("sin_t", [128, 32, 8], F32)
        pv = sb("pv", [128, L, NPV], F32)
        prm = sb("prm", [128, L, 96], F32)
        dww = sb("dww", [128, L, 4 * KC], F32)
        kmT = sb("kmT", [64, H, 16], BF16)
        halo = sb("halo", [128, 4, 32], BF16)
        pbank = [ps(f"pb{i}", [128, 512], F32) for i in range(8)]

        def P(l, i, k=None):
            if k is None:
                return prm[:, l, i * 8:(i + 1) * 8]
            return prm[:, l, i * 8 + k:i * 8 + k + 1]


        with contextlib.ExitStack() as es0:
            sb0 = lambda name, shape, dt=F32: es0.enter_context(_sbt(nc, name, shape, dt))
            posi = sb0("posi", [128, 32], I32)
            posf = sb0("posf", [128, 32], F32)
            ang = sb0("ang", [128, 32, 8], F32)
            kk = sb0("kk", [128, 32, 8], F32)
            kki = sb0("kki", [128, 32, 8], I32)
            tmpa = sb0("tmpa", [128, 32, 8], F32)
            iot = sb0("iot", [128, 128], F32)
            iop = sb0("iop", [128, 1], F32)
            iot_i = sb0("iot_i", [128, 128], I32)
            iop_i = sb0("iop_i", [128, 1], I32)
            indi = sb0("indi", [16, S], I32)
            cT = sb0("cTs", [128, 8], F32)
            cth = sb0("cth", [128, 8], F32)
            cTb = sb0("cTb", [128, 8], BF16)
            wa = [sb0(f"wa{i}", [128, 8, 512], BF16) for i in range(2)]
            adas = sb0("adas", [128, L, 48], F32)
            indf = sb0("indf", [16, S], F32)
            halfpi = sb0("halfpi", [128, 1], F32)
            ph = Phase(nc, "p0")
            ph.op("pool", lambda e: e.iota(iot_i[:], pattern=[[1, 128]], base=0, channel_multiplier=0), writes=["iot_i"])
            ph.op("pool", lambda e: e.iota(iop_i[:], pattern=[[0, 1]], base=0, channel_multiplier=1), writes=["iop_i"])
            ph.op("dve", lambda e: e.tensor_copy(out=iot[:], in_=iot_i[:]), reads=["iot_i"], writes=["iot"])
            ph.op("dve", lambda e: e.tensor_copy(out=iop[:], in_=iop_i[:]), reads=["iop_i"], writes=["iop"])
            ph.op("dve", lambda e: e.memset(kmT[:], 0.0), writes=["kmT"])
            ph.op("dve", lambda e: e.tensor_scalar(out=ident_f[:], in0=iot[:], scalar1=iop[:, 0:1], scalar2=None,
                                                   op0=ALU.is_equal), reads=["iot", "iop"], writes=["idf"])
            ph.op("dve", lambda e: e.tensor_copy(out=ident_b[:], in_=ident_f[:]), reads=["idf"], writes=["idb"])
            ph.op("dve", lambda e: e.memset(ones_b[:], 1.0), writes=["ones"])
            ph.op("dve", lambda e: e.tensor_scalar(out=trimask[:], in0=iot[:], scalar1=iop[:, 0:1], scalar2=-BIG,
                                                   op0=ALU.is_lt, op1=ALU.mult), reads=["iot", "iop"], writes=["tri"])
            tmpi = sb0("tmpi", [16, S], F32)
            ph.op("pool", lambda e: e.iota(indi[:], pattern=[[1, S]], base=0, channel_multiplier=-256), writes=["indi"])
            ph.op("dve", lambda e: e.tensor_copy(out=indf[:], in_=indi[:]), reads=["indi"], writes=["indf"])
            ph.op("dve", lambda e: e.tensor_scalar(out=tmpi[:], in0=indf[:], scalar1=255.5, scalar2=None, op0=ALU.is_lt),
                  reads=["indf"], writes=["tmpi1"])
            ph.op("dve", lambda e: e.tensor_scalar(out=indf[:], in0=indf[:], scalar1=-0.5, scalar2=None, op0=ALU.is_gt),
                  reads=["indf", "tmpi1"], writes=["indf1"])
            ph.op("dve", lambda e: e.tensor_tensor(out=indic[:], in0=indf[:], in1=tmpi[:], op=ALU.mult),
                  reads=["indf1", "tmpi1"], writes=["indic"])
            ph.dma("sp", posi[:], pos_d, writes=["posi"])
            ph.op("dve", lambda e: e.tensor_copy(out=posf[:], in_=posi[:]), reads=["posi"], writes=["posf"])
            for i in range(8):
                invf = float(np.float32(500000.0) ** np.float32(-(2.0 * i) / 16.0))
                ph.op("dve", lambda e, i=i, invf=invf: e.tensor_scalar(out=ang[:, :, i], in0=posf[:], scalar1=invf,
                                                                       scalar2=None, op0=ALU.mult),
                      reads=["posf"], writes=["ang"])
            ph.op("dve", lambda e: e.tensor_scalar(out=kk[:], in0=ang[:], scalar1=float(1.0 / (2 * math.pi)), scalar2=None,
                                                   op0=ALU.mult), reads=["ang"], writes=["kk"])
            ph.op("dve", lambda e: e.tensor_copy(out=kki[:], in_=kk[:]), reads=["kk"], writes=["kki"])
            ph.op("dve", lambda e: e.tensor_copy(out=kk[:], in_=kki[:]), reads=["kki"], writes=["kk2"])
            C1, C2 = 6.28125, float(2 * math.pi - 6.28125)
            ph.op("dve", lambda e: e.scalar_tensor_tensor(out=ang[:], in0=kk[:], scalar=-C1, in1=ang[:], op0=ALU.mult,
                                                          op1=ALU.add), reads=["kk2", "ang"], writes=["ang"])
            ph.op("dve", lambda e: e.scalar_tensor_tensor(out=ang[:], in0=kk[:], scalar=-C2, in1=ang[:], op0=ALU.mult,
                                                          op1=ALU.add), reads=["kk2", "ang"], writes=["ang"])
            ph.op("dve", lambda e: e.tensor_scalar(out=tmpa[:], in0=ang[:], scalar1=math.pi, scalar2=-2 * math.pi,
                                                   op0=ALU.is_gt, op1=ALU.mult), reads=["ang"], writes=["tmpa"])
            ph.op("dve", lambda e: e.tensor_tensor(out=ang[:], in0=ang[:], in1=tmpa[:], op=ALU.add),
                  reads=["ang", "tmpa"], writes=["ang"])
            ph.op("dve", lambda e: e.tensor_scalar(out=tmpa[:], in0=ang[:], scalar1=-math.pi, scalar2=2 * math.pi,
                                                   op0=ALU.is_lt, op1=ALU.mult), reads=["ang"], writes=["tmpa"])
            ph.op("dve", lambda e: e.tensor_tensor(out=ang[:], in0=ang[:], in1=tmpa[:], op=ALU.add),
                  reads=["ang", "tmpa"], writes=["ang"])
            ph.op("act", lambda e: e.activation(out=sin_t[:], in_=ang[:], func=AF.Sin), reads=["ang"], writes=["sin"])
            ph.op("dve", lambda e: e.tensor_scalar(out=tmpa[:], in0=ang[:], scalar1=-1.0, scalar2=None, op0=ALU.mult),
                  reads=["ang"], writes=["tmpa"])
            ph.op("dve", lambda e: e.tensor_tensor(out=tmpa[:], in0=tmpa[:], in1=ang[:], op=ALU.max),
                  reads=["ang", "tmpa"], writes=["tmpa"])
            ph.op("dve", lambda e: e.memset(halfpi[:], math.pi / 2), writes=["halfpi"])
            ph.op("act", lambda e: e.activation(out=cos_t[:], in_=tmpa[:], func=AF.Sin, scale=-1.0, bias=halfpi[:, 0:1]),
                  reads=["tmpa", "halfpi"], writes=["cos"])
            ph.dma("sp", pv[:], pv_d.rearrange("l p n -> p l n"), writes=["pv"])
            ph.dma("sp", dww[:], dw_d.rearrange("l p n -> p l n"), writes=["dww"])
            ph.dma("sp", cT[:], cT_d, writes=["cT"])
            ph.op("act", lambda e: e.activation(out=cth[:], in_=cT[:], func=AF.Tanh, scale=0.5), reads=["cT"], writes=["cth"])
            ph.op("dve", lambda e: e.scalar_tensor_tensor(out=cth[:], in0=cth[:], scalar=1.0, in1=cT[:], op0=ALU.add,
                                                          op1=ALU.mult), reads=["cth", "cT"], writes=["cth2"])
            ph.op("dve", lambda e: e.tensor_scalar(out=cTb[:], in0=cth[:], scalar1=0.5, scalar2=None, op0=ALU.mult),
                  reads=["cth2"], writes=["cTb"])
            rot = PsumRot(pbank[0:2], "adaps")
            it = 0
            for l in range(depth):
                for cb in range(12):
                    w = wa[it % 2]
                    wk = ("wa", it % 2)
                    it += 1
                    ph.dma("pool", w[:].rearrange("p k n -> p (k n)"), wada_d[l, cb], writes=[wk], max_dma_last_dim=8192)
                    pt, pk = rot.next()
                    for m in range(4):
                        for k in range(8):
                            ph.op("pe", lambda e, w=w, m=m, k=k, pt=pt: e.matmul(
                                pt[:, m:m + 1], lhsT=w[:, k, m * 128:(m + 1) * 128], rhs=cTb[:, k:k + 1],
                                start=(k == 0), stop=(k == 7)), reads=[wk, "cTb"], writes=[pk])
                    ph.op("dve", lambda e, l=l, cb=cb, pt=pt: e.tensor_tensor(
                        out=adas[:, l, cb * 4:(cb + 1) * 4], in0=pt[:, 0:4], in1=pv[:, l, 32 + cb * 4:32 + (cb + 1) * 4],
                        op=ALU.add), reads=[pk, "pv"], writes=["adas"])
            for l in range(depth):
                A = lambda i: adas[:, l, i * 8:(i + 1) * 8]
                def ts(out, in0, s1, s2, op0, op1=ALU.bypass, l=l):
                    ph.op("dve", lambda e: e.tensor_scalar(out=out, in0=in0, scalar1=s1, scalar2=s2, op0=op0, op1=op1),
                          reads=["adas", "pv", "dww"], writes=["prm"])
                ts(P(l, 0), A(1), 1.0, 1.0 / ALPHA, ALU.add, ALU.mult)
                ts(P(l, 1), A(0), 1.0, None, ALU.mult)
                ts(P(l, 2), A(2), 1.0, 0.5, ALU.add, ALU.mult)
                ts(P(l, 3), A(4), 1.0, 1.0 / ALPHA, ALU.add, ALU.mult)
                ts(P(l, 4), A(3), 1.0, None, ALU.mult)
                ts(P(l, 5), A(5), 1.0, None, ALU.add)
                ts(P(l, 6), pv[:, l, 0:8], ALPHA, None, ALU.mult)
                ts(P(l, 7), pv[:, l, 8:16], ALPHA, None, ALU.mult)
                ts(P(l, 8), pv[:, l, 16:24], ALPHA, None, ALU.mult)
                ts(P(l, 9), pv[:, l, 24:32], ALPHA, None, ALU.mult)
                ts(prm[:, l, 80:84], pv[:, l, 80:84], 1.0, None, ALU.mult)
                ts(prm[:, l, 84:88], pv[:, l, 84:88], 0.5, None, ALU.mult)
                ts(prm[:, l, 88:92], pv[:, l, 88:92], 0.5, None, ALU.mult)
                ph.op("dve", lambda e, l=l: e.tensor_scalar(out=dww[:, l, :], in0=dww[:, l, :], scalar1=0.5, scalar2=None,
                                                            op0=ALU.mult), reads=["dww"], writes=["dww"])
            ph.emit()

        for hf in range(2):
            with contextlib.ExitStack() as es1:
                xT = es1.enter_context(_sbt(nc, "xT", [128, 8, HALF], F32))
                xin = [es1.enter_context(_sbt(nc, f"xin{i}", [128, D], F32)) for i in range(2)]
                ph = Phase(nc, f"px{hf}")
                rot = PsumRot(pbank[0:4], "xps")
                for tt in range(16):
                    xi, xk = xin[tt % 2], ("xin", tt % 2)
                    t0 = hf * HALF + tt * 128
                    ph.dma("sp", xi[:], x_d[t0:t0 + 128, :], writes=[xk])
                    for kq in range(2):
                        pt, pk = rot.next()
                        for j in range(4):
                            k = kq * 4 + j
                            ph.op("pe", lambda e, pt=pt, j=j, k=k, xi=xi: e.transpose(
                                pt[:, j * 128:(j + 1) * 128], xi[:, k * 128:(k + 1) * 128], ident_f[:]),
                                reads=[xk], writes=[pk])
                        eng = "act" if kq == 0 else "dve"
                        dst = xT[:, kq * 4:(kq + 1) * 4, tt * 128:(tt + 1) * 128]
                        src = pt[:].rearrange("p (j t) -> p j t", j=4)
                        if eng == "act":
                            ph.op("act", lambda e, dst=dst, src=src: e.activation(out=dst, in_=src, func=AF.Copy, scale=ALPHA),
                                  reads=[pk], writes=[("xT", tt)])
                        else:
                            ph.op("dve", lambda e, dst=dst, src=src: e.tensor_scalar(out=dst, in0=src, scalar1=ALPHA, scalar2=None,
                                                                                     op0=ALU.mult), reads=[pk], writes=[("xT", tt)])
                ph.dma("sp", xs_d[hf], xT[:].rearrange("p k t -> p (k t)"), reads=[("xT", tt) for tt in range(16)], writes=["xs"])
                ph.emit()

        for l in range(depth):
            for hf in range(2):
                layer_half(nc, l, hf, locals())

        for hf in range(2):
            with contextlib.ExitStack() as es1:
                xT = es1.enter_context(_sbt(nc, "xT", [128, 8, HALF], F32))
                xo = [es1.enter_context(_sbt(nc, f"xo{i}", [128, D], F32)) for i in range(2)]
                ph = Phase(nc, f"pf{hf}")
                ph.dma("sp", xT[:].rearrange("p k t -> p (k t)"), xs_d[hf], writes=["xT"])
                rot = PsumRot(pbank[0:4], "ops")
                for tt in range(16):
                    xi, xk = xo[tt % 2], ("xo", tt % 2)
                    for kq in range(2):
                        pt, pk = rot.next()
                        for j in range(4):
                            k = kq * 4 + j
                            ph.op("pe", lambda e, pt=pt, j=j, k=k, tt=tt: e.transpose(
                                pt[:, j * 128:(j + 1) * 128], xT[:, k, tt * 128:(tt + 1) * 128], ident_f[:]),
                                reads=["xT"], writes=[pk])
                        dst = xi[:, kq * 512:(kq + 1) * 512]
                        if kq == 0:
                            ph.op("act", lambda e, dst=dst, pt=pt: e.activation(out=dst, in_=pt[:], func=AF.Copy, scale=1.0 / ALPHA),
                                  reads=[pk], writes=[(xk, kq)])
                        else:
                            ph.op("dve", lambda e, dst=dst, pt=pt: e.tensor_scalar(out=dst, in0=pt[:], scalar1=1.0 / ALPHA, scalar2=None,
                                                                                   op0=ALU.mult), reads=[pk], writes=[(xk, kq)])
                    t0 = hf * HALF + tt * 128
                    ph.dma("sp", out_d[t0:t0 + 128, :], xi[:], reads=[(xk, 0), (xk, 1)], writes=["out"])
                ph.emit()
    return nc


def layer_half(nc, l, hf, env):
    g = env
    prm, dww, pbank = g["prm"], g["dww"], g["pbank"]
    ident_b, ones_b, trimask, indic = g["ident_b"], g["ones_b"], g["trimask"], g["indic"]
    cos_t, sin_t, kmT, halo = g["cos_t"], g["sin_t"], g["kmT"], g["halo"]
    xs_d, qT_d, kT_d, v_d, bT_d = g["xs_d"], g["qT_d"], g["kT_d"], g["v_d"], g["bT_d"]
    win_d, wbr_d, wpw_d, wout_d, wup_d, wdn_d = g["win_d"], g["wbr_d"], g["wpw_d"], g["wout_d"], g["wup_d"], g["wdn_d"]
    dbg_d = g["dbg_d"]
    P = g["P"]
    T0 = hf * HALF
    nm = f"l{l}h{hf}"
    xs3 = xs_d[hf].rearrange("p (k t) -> p k t", k=8)
    es_mg = contextlib.ExitStack()
    mg = es_mg.enter_context(_sbt(nc, "mg", [128, 8, HALF], BF16))
    esh = contextlib.ExitStack()
    uT = esh.enter_context(_sbt(nc, "uT", [128, 8, HALF], BF16))
    es_x = contextlib.ExitStack()
    xT = es_x.enter_context(_sbt(nc, "xT", [128, 8, HALF], F32))
    if True:
        with contextlib.ExitStack() as esa:
            sba = lambda name, shape, dt=F32: esa.enter_context(_sbt(nc, name, shape, dt))
            wqkv = sba("wqkv", [128, 3, 8, 512], BF16)
            qk_sb = [sba(f"qk_sb{i}", [128, 512], BF16) for i in range(4)]
            v_sb = [sba(f"v_sb{i}", [128, 512], BF16) for i in range(2)]
            rtmp = [sba(f"rtmp{i}", [128, H, 8], F32) for i in range(8)]
            qT_st = sba("qT_st", [64, H, 512], BF16)
            kT_st = sba("kT_st", [64, H, 512], BF16)
            kms = sba("kms", [64, 16], F32)
            gsb = sba("gsb", [128, H, 16], F32)
            top8 = sba("top8", [128, H, 8], F32)
            msk = sba("msk", [128, H, 16], F32)
            bias_sb = sba("bias_sb", [128, H, 16], BF16)
            bT_st = sba("bT_st", [16, H, 512], BF16)
            VB = sba("VB", [128, 8, 16], F32)
            NB = sba("NB", [128, 8, 16], F32)
            VS = sba("VS", [128, 8, 16], F32)
            ph = Phase(nc, nm + "a")
            for k in range(8):
                ph.dma("sp", xT[:, k, :], xs3[:, k, :], writes=[("xT", k)])
            for i in range(3):
                wload(ph, wqkv[:, i], win_d[l, i], ("wqkv", i), 4)
            own0 = 8 * hf
            ph.op("pool", lambda e: e.memset(VB[:], -1e30), writes=["VB"])
            ph.op("pool", lambda e: e.memset(NB[:], -BIG), writes=["NB"])
            ph.op("pool", lambda e: e.memset(VS[:], -BIG), writes=["VS"])
            for ob in range(8):
                own = own0 + ob
                if own > 0:
                    ph.op("pool", lambda e, ob=ob, own=own: e.memset(VB[:, ob, 0:own], 0.0), reads=["VB"], writes=["VB"])
                ph.op("pool", lambda e, ob=ob, own=own: e.memset(VS[:, ob, 0:own + 1], 0.0), reads=["VS"], writes=["VS"])
                ph.op("pool", lambda e, ob=ob, own=own: e.memset(NB[:, ob, own:own + 1], 0.0), reads=["NB"], writes=["NB"])
            for k in range(8):
                ph.op("dve", lambda e, k=k: e.tensor_scalar(out=uT[:, k, :], in0=xT[:, k, :], scalar1=P(l, 0, k), scalar2=P(l, 1, k),
                                                            op0=ALU.mult, op1=ALU.add), reads=[("xT", k)], writes=[("uT", k)])
            rot = PsumRot(pbank[0:3], "qkv")
            trot = PsumRot(pbank[3:5], "tr")
            grot = PsumRot(pbank[5:7], "gate")
            pendA = []
            for tg in range(4):
                for t4 in range(4):
                    tt = tg * 4 + t4
                    gt = hf * 16 + tt
                    tsl = slice(tt * 128, (tt + 1) * 128)
                    for blk in range(3):
                        pt, pk = rot.next()
                        for k in range(8):
                            ph.op("pe", lambda e, pt=pt, k=k, blk=blk, tsl=tsl: e.matmul(
                                pt[:], lhsT=uT[:, k, tsl], rhs=wqkv[:, blk, k, :], start=(k == 0), stop=(k == 7)),
                                reads=[("wqkv", blk, k // 2), ("uT", k)], writes=[pk])
                        if blk == 0:
                            while pendA:
                                pendA.pop(0)()
                        if blk == 2:
                            vs, vk = v_sb[tt % 2], ("v_sb", tt % 2)
                            ph.op("act", lambda e, vs=vs, pt=pt: e.activation(out=vs[:], in_=pt[:], func=AF.Copy), reads=[pk], writes=[vk])
                            ph.dma("sp", v_d[T0 + tt * 128:T0 + (tt + 1) * 128, :], vs[:], reads=[vk], writes=["v_d"])
                            continue
                        qs, qk = qk_sb[blk * 2 + tt % 2], ("qk_sb", blk * 2 + tt % 2)
                        p3 = pt[:].rearrange("p (h d) -> p h d", h=H)
                        q3 = qs[:].rearrange("p (h d) -> p h d", h=H)
                        ph.op("dve", lambda e, q3=q3, p3=p3: e.tensor_copy(out=q3[:, :, 16:64], in_=p3[:, :, 16:64]),
                              reads=[pk], writes=[(qk, "nr")])
                        cosb = cos_t[:, gt, :].unsqueeze(1).to_broadcast([128, H, 8])
                        sinb = sin_t[:, gt, :].unsqueeze(1).to_broadcast([128, H, 8])
                        x1, x2 = p3[:, :, 0:8], p3[:, :, 8:16]
                        r = rtmp[blk * 4:blk * 4 + 4]
                        ro = blk * 4
                        def tt_(out, a, b, op, rk, wk, extra_r=()):
                            ph.op("dve", lambda e: e.tensor_tensor(out=out, in0=a, in1=b, op=op), reads=list(rk) + list(extra_r), writes=wk)
                        tt_(r[0][:], x1, cosb, ALU.mult, [pk], [("rt", ro + 0)])
                        tt_(r[1][:], x2, sinb, ALU.mult, [pk], [("rt", ro + 1)])
                        tt_(r[2][:], x2, cosb, ALU.mult, [pk], [("rt", ro + 2)])
                        tt_(r[3][:], x1, sinb, ALU.mult, [pk], [("rt", ro + 3)])
                        tt_(q3[:, :, 0:8], r[0][:], r[1][:], ALU.subtract, [("rt", ro + 0), ("rt", ro + 1)], [(qk, "r1")])
                        tt_(q3[:, :, 8:16], r[2][:], r[3][:], ALU.add, [("rt", ro + 2), ("rt", ro + 3)], [(qk, "r2")])
                        def trans(qs=qs, qk=qk, blk=blk, t4=t4):
                            tp, tk = trot.next()
                            tpb = tp[:].bitcast(BF16)
                            for h in range(H):
                                ph.op("pe", lambda e, tpb=tpb, h=h, qs=qs: e.transpose(
                                    tpb[0:64, h * 128:(h + 1) * 128], qs[:, h * 64:(h + 1) * 64], ident_b[:]),
                                    reads=[(qk, "nr"), (qk, "r1"), (qk, "r2")], writes=[tk])
                            st = qT_st if blk == 0 else kT_st
                            stk = ("qT_st" if blk == 0 else "kT_st")
                            ph.op("act", lambda e, st=st, tpb=tpb, t4=t4: e.activation(
                                out=st[:, :, t4 * 128:(t4 + 1) * 128], in_=tpb[0:64, 0:1024].rearrange("p (h t) -> p h t", h=H), func=AF.Copy),
                                reads=[tk], writes=[(stk, t4)])
                        pendA.append(trans)
                while pendA:
                    pendA.pop(0)()
                ph.dma("sp", kT_d[:, :, T0 + tg * 512:T0 + (tg + 1) * 512].rearrange("h d t -> d h t"), kT_st[:],
                       reads=[("kT_st", i) for i in range(4)], writes=["kT_d"])
                ph.dma("sp", qT_d[:, :, tg * 512:(tg + 1) * 512].rearrange("h d t -> d h t"), qT_st[:],
                       reads=[("qT_st", i) for i in range(4)], writes=["qT_d"])
                b0 = own0 + 2 * tg
                ph.op("dve", lambda e: e.tensor_reduce(out=kms[:], in_=kT_st[:].rearrange("p h (b t) -> p (h b) t", b=2),
                                                       axis=AX.X, op=ALU.add), reads=[("kT_st", i) for i in range(4)], writes=["kms"])
                ph.op("dve", lambda e, b0=b0: e.tensor_scalar(out=kmT[:, :, b0:b0 + 2], in0=kms[:].rearrange("p (h b) -> p h b", b=2),
                                                              scalar1=1.0 / 256, scalar2=None, op0=ALU.mult), reads=["kms"], writes=["kmT"])
                for t4 in range(4):
                    tt = tg * 4 + t4
                    ob = tt // 2
                    gp, gk = grot.next()
                    for h in range(H):
                        ph.op("pe", lambda e, gp=gp, h=h, t4=t4: e.matmul(
                            gp[:, h * 16:(h + 1) * 16], lhsT=qT_st[:, h, t4 * 128:(t4 + 1) * 128], rhs=kmT[:, h, :], start=True, stop=True),
                            reads=[("qT_st", t4), "kmT"], writes=[gk])
                    g3 = gp[:, 0:128].rearrange("p (h s) -> p h s", h=H)
                    bc = lambda t: t[:, ob, :].unsqueeze(1).to_broadcast([128, H, 16])
                    ph.op("dve", lambda e, g3=g3, ob=ob: e.tensor_tensor(out=gsb[:], in0=g3, in1=VB[:, ob, :].unsqueeze(1).to_broadcast([128, H, 16]),
                                                                         op=ALU.add), reads=[gk, "VB"], writes=["gsb"])
                    for h in range(H):
                        ph.op("dve", lambda e, h=h: e.max(out=top8[:, h, :], in_=gsb[:, h, :]), reads=["gsb"], writes=[("top8", h)])
                    ph.op("dve", lambda e: e.tensor_tensor(out=msk[:], in0=gsb[:], in1=top8[:, :, 2:3].to_broadcast([128, H, 16]), op=ALU.is_lt),
                          reads=["gsb"] + [("top8", h) for h in range(H)], writes=["msk"])
                    ph.op("dve", lambda e, ob=ob: e.tensor_tensor(out=msk[:], in0=msk[:], in1=NB[:, ob, :].unsqueeze(1).to_broadcast([128, H, 16]),
                                                                  op=ALU.mult), reads=["msk", "NB"], writes=["msk"])
                    ph.op("dve", lambda e, ob=ob: e.tensor_tensor(out=bias_sb[:], in0=msk[:], in1=VS[:, ob, :].unsqueeze(1).to_broadcast([128, H, 16]),
                                                                  op=ALU.add), reads=["msk", "VS"], writes=["bias_sb"])
                    tp, tk = trot.next()
                    tpb = tp[:].bitcast(BF16)
                    for h in range(H):
                        ph.op("pe", lambda e, tpb=tpb, h=h: e.transpose(tpb[0:16, h * 128:(h + 1) * 128], bias_sb[:, h, :], ident_b[:]),
                              reads=["bias_sb"], writes=[tk])
                    ph.op("act", lambda e, tpb=tpb, t4=t4: e.activation(
                        out=bT_st[:, :, t4 * 128:(t4 + 1) * 128], in_=tpb[0:16, 0:1024].rearrange("p (h t) -> p h t", h=H), func=AF.Copy),
                        reads=[tk], writes=[("bT_st", t4)])
                ph.dma("sp", bT_d[:, :, tg * 512:(tg + 1) * 512].rearrange("h s t -> s h t"), bT_st[:],
                       reads=[("bT_st", i) for i in range(4)], writes=["bT_d"])
            ph.emit()
        es_x.close()
        es_cv = contextlib.ExitStack()
        cvT = es_cv.enter_context(_sbt(nc, "cvT", [128, 4, HALF], BF16))

        with contextlib.ExitStack() as esb:
            sbb = lambda name, shape, dt=F32: esb.enter_context(_sbt(nc, name, shape, dt))
            hT = sbb("hT", [128, 4, 32 + HALF], BF16)
            wgl = sbb("wgl", [128, 2, 8, 512], BF16)
            dmat = sbb("dmat", [128, 4 * KC, 128], BF16)
            sg = [sbb(f"sg{i}", [128, 512], F32) for i in range(2)]
            cv = sbb("cv", [128, 4, 512], F32)
            cvb = sbb("cvb", [128, 4, 512], BF16)
            csq = sbb("csq", [128, 4, 512], BF16)
            st1 = sbb("st1", [128, 512], F32)
            st2 = sbb("st2", [128, 512], F32)
            st3 = sbb("st3", [128, 512], F32)
            mhalf = sbb("mhalf", [128, 512], F32)
            zt = [sbb(f"zt{i}", [128, 512], F32) for i in range(2)]
            th = [sbb(f"th{i}", [128, 512], F32) for i in range(2)]
            ph = Phase(nc, nm + "b")
            for i in range(2):
                wload(ph, wgl[:, i], win_d[l, 3 + i], ("wgl", i), 4)
            ph.op("pool", lambda e: e.memset(mhalf[:], -0.5), writes=["mhalf"])
            for j in range(4 * KC):
                ph.op("dve", lambda e, j=j: e.tensor_scalar(out=dmat[:, j, :], in0=ident_b[:], scalar1=dww[:, l, j:j + 1], scalar2=None,
                                                          op0=ALU.mult), writes=[("dmat", j)])
            if hf == 0:
                ph.op("dve", lambda e: e.memset(hT[:, :, 0:32], 0.0), writes=["hT_halo"])
            else:
                ph.op("dve", lambda e: e.tensor_copy(out=hT[:, :, 0:32], in_=halo[:]), writes=["hT_halo"])
            rot = PsumRot(pbank[0:4], "glu")
            for tg in range(4):
                tsl = slice(tg * 512, (tg + 1) * 512)
                for c in range(4):
                    pa, pak = rot.next()
                    pg, pgk = rot.next()
                    for which, pt, pk in ((0, pa, pak), (1, pg, pgk)):
                        for k in range(8):
                            ph.op("pe", lambda e, pt=pt, k=k, which=which, c=c, tsl=tsl: e.matmul(
                                pt[:], lhsT=wgl[:, which, k, c * 128:(c + 1) * 128], rhs=uT[:, k, tsl], start=(k == 0), stop=(k == 7)),
                                reads=[("wgl", which, k // 2)], writes=[pk])
                    s, sk = sg[c % 2], ("sg", c % 2)
                    ph.op("act", lambda e, s=s, pg=pg: e.activation(out=s[:], in_=pg[:], func=AF.Tanh, scale=0.5), reads=[pgk], writes=[sk])
                    ph.op("dve", lambda e, s=s, pa=pa, c=c, tg=tg: e.scalar_tensor_tensor(
                        out=hT[:, c, 32 + tg * 512:32 + (tg + 1) * 512], in0=s[:], scalar=1.0, in1=pa[:], op0=ALU.add, op1=ALU.mult),
                        reads=[sk, pak], writes=[("hT", tg)])
            if hf == 0:
                ph.op("dve", lambda e: e.tensor_copy(out=halo[:], in_=hT[:, :, HALF:HALF + 32]), reads=[("hT", 3)], writes=["halo"])
            crot = PsumRot(pbank[4:6], "conv")
            for tg in range(4):
                hk = ["hT_halo"] + [("hT", i) for i in range(max(0, tg - 1), tg + 1)]
                s1p, s2p = pbank[6], pbank[7]
                for c in range(4):
                    pt, pk = crot.next()
                    for k in range(KC):
                        c0 = 32 + tg * 512 - 30 + k
                        ph.op("pe", lambda e, pt=pt, c=c, k=k, c0=c0: e.matmul(
                            pt[:], lhsT=dmat[:, c * KC + k, :], rhs=hT[:, c, c0:c0 + 512], start=(k == 0), stop=(k == KC - 1)),
                            reads=hk + [("dmat", c * KC + k)], writes=[pk])
                    ph.op("act", lambda e, pt=pt, c=c: e.activation(out=cv[:, c, :], in_=pt[:], func=AF.Identity, bias=prm[:, l, 80 + c:81 + c]),
                          reads=[pk], writes=[("cv", c)])
                    ph.op("act", lambda e, c=c: e.activation(out=csq[:, c, :], in_=cv[:, c, :], func=AF.Square), reads=[("cv", c)], writes=[("csq", c)])
                    ph.op("act", lambda e, c=c: e.activation(out=cvb[:, c, :], in_=cv[:, c, :], func=AF.Copy), reads=[("cv", c)], writes=[("cvb", c)])
                for c in range(4):
                    ph.op("pe", lambda e, c=c: e.matmul(s1p[:], lhsT=ones_b[:], rhs=cvb[:, c, :], start=(c == 0), stop=(c == 3)),
                          reads=[("cvb", c)], writes=["s1p"])
                for c in range(4):
                    ph.op("pe", lambda e, c=c: e.matmul(s2p[:], lhsT=ones_b[:], rhs=csq[:, c, :], start=(c == 0), stop=(c == 3)),
                          reads=[("csq", c)], writes=["s2p"])
                ln_stats(ph, s1p, s2p, st1, st2, st3, mhalf, 1.0 / CW)
                for c in range(4):
                    z, zk = zt[c % 2], ("zt", c % 2)
                    t_, tk_ = th[c % 2], ("th", c % 2)
                    ph.op("dve", lambda e, c=c: e.tensor_tensor(out=cv[:, c, :], in0=cv[:, c, :], in1=st1[:], op=ALU.subtract),
                          reads=[("cv", c), "mean"], writes=[("cv", c)])
                    ph.op("dve", lambda e, c=c: e.tensor_tensor(out=cv[:, c, :], in0=cv[:, c, :], in1=st3[:], op=ALU.mult),
                          reads=[("cv", c), "rstd"], writes=[("cv", c)])
                    ph.op("dve", lambda e, c=c, z=z: e.tensor_scalar(out=z[:], in0=cv[:, c, :], scalar1=prm[:, l, 84 + c:85 + c],
                                                                     scalar2=prm[:, l, 88 + c:89 + c], op0=ALU.mult, op1=ALU.add),
                          reads=[("cv", c)], writes=[zk])
                    ph.op("act", lambda e, z=z, t_=t_: e.activation(out=t_[:], in_=z[:], func=AF.Tanh), reads=[zk], writes=[tk_])
                    ph.op("dve", lambda e, c=c, z=z, t_=t_, tg=tg: e.scalar_tensor_tensor(
                        out=cvT[:, c, tg * 512:(tg + 1) * 512], in0=t_[:], scalar=1.0, in1=z[:], op0=ALU.add, op1=ALU.mult),
                        reads=[zk, tk_], writes=[("cvT", tg)])
            ph.emit()
        es_at = contextlib.ExitStack()
        attnT = es_at.enter_context(_sbt(nc, "attnT", [64, H, HALF], BF16))

        nk = HALF * (hf + 1)
        nkt = nk // 128
        with contextlib.ExitStack() as esc:
            sbc = lambda name, shape, dt=F32: esc.enter_context(_sbt(nc, name, shape, dt))
            Qa = [sbc(f"Qa{i}", [80, HALF], BF16) for i in range(2)]
            Ka = [sbc(f"Ka{i}", [80, S], BF16) for i in range(2)]
            Va = [sbc(f"Va{i}", [128, 32, 128], BF16) for i in range(2)]
            pT = [sbc(f"pT{i}", [128, 512], BF16) for i in range(4)]
            rec = [sbc(f"rec{i}", [64, 512], F32) for i in range(2)]
            ph = Phase(nc, nm + "c")
            for i in range(2):
                ph.op("dve", lambda e, i=i: e.memset(Va[i][:, :, 64:128], 1.0), writes=[("Va1", i)])
                ph.dma("sp", Ka[i][64:80, 0:nk], indic[:, 0:nk] if hf == 1 else indic[:, 0:nk], writes=[("Kai", i)])
            if hf == 0:
                pass
            srot = PsumRot(pbank[0:4], "sT")
            orot = PsumRot(pbank[4:6], "oT")
            pi = 0
            pend = []
            LAG = 2

            def flush(n_keep):
                while len(pend) > n_keep:
                    pend.pop(0)()

            for h in range(H):
                b = h % 2
                ph.dma("sp", Qa[b][0:64, :], qT_d[h], writes=[("Qa", b)])
                ph.dma("sp", Qa[b][64:80, :], bT_d[h], writes=[("Qab", b)])
                ph.dma("sp", Ka[b][0:64, 0:nk], kT_d[h, :, 0:nk], writes=[("Ka", b)])
                ph.dma("sp", Va[b][:, 0:nkt, 0:64], v_d[0:nk, h * 64:(h + 1) * 64].rearrange("(t p) d -> p t d", p=128),
                       writes=[("Va", b)])
                rd = [("Qa", b), ("Qab", b), ("Ka", b), ("Kai", b)]
                for qg in range(4):
                    op_, ok_ = orot.next()
                    ktiles = list(range(hf * 16 + qg * 4 + 4))
                    for kt in ktiles:
                        lt = kt - hf * 16
                        r0 = max(0, lt - qg * 4)
                        q0 = qg * 512 + r0 * 128
                        n = 512 - r0 * 128
                        sp_, sk_ = srot.next()
                        diag = lt >= qg * 4
                        ph.op("pe", lambda e, sp_=sp_, b=b, kt=kt, q0=q0, n=n, diag=diag: e.matmul(
                            sp_[:, 0:n], lhsT=Ka[b][:, kt * 128:(kt + 1) * 128], rhs=Qa[b][:, q0:q0 + n], start=True, stop=not diag),
                            reads=rd, writes=[sk_])
                        if diag:
                            ph.op("pe", lambda e, sp_=sp_: e.matmul(sp_[:, 0:128], lhsT=ident_b[:], rhs=trimask[:], start=False, stop=True),
                                  reads=[], writes=[sk_])
                        p_, pk_ = pT[pi % 4], ("pT", pi % 4)
                        pi += 1
                        ph.op("act", lambda e, p_=p_, sp_=sp_, n=n: e.activation(out=p_[:, 0:n], in_=sp_[:, 0:n], func=AF.Exp, scale=HD ** -0.5),
                              reads=[sk_], writes=[pk_])
                        c0 = r0 * 128

                        def pv(op_=op_, ok_=ok_, p_=p_, pk_=pk_, b=b, kt=kt, c0=c0, n=n, first=(kt == 0), last=(kt == ktiles[-1])):
                            ph.op("pe", lambda e: e.matmul(op_[:, c0:c0 + n], lhsT=Va[b][:, kt, :], rhs=p_[:, 0:n], start=first, stop=last),
                                  reads=[pk_, ("Va", b), ("Va1", b)], writes=[ok_])
                        pend.append(pv)
                        flush(LAG)

                    def norm(op_=op_, ok_=ok_, h=h, qg=qg):
                        rc, rk = rec[qg % 2], ("rec", qg % 2)
                        ph.op("dve", lambda e: e.reciprocal(out=rc[:], in_=op_[64:128, :]), reads=[ok_], writes=[rk])
                        ph.op("dve", lambda e: e.tensor_tensor(out=attnT[:, h, qg * 512:(qg + 1) * 512], in0=op_[0:64, :], in1=rc[:], op=ALU.mult),
                              reads=[ok_, rk], writes=[("attnT", h)])
                    pend.append(norm)
            flush(0)
            ph.emit()

        with contextlib.ExitStack() as esd:
            sbd = lambda name, shape, dt=F32: esd.enter_context(_sbt(nc, name, shape, dt))
            wga = [sbd(f"wga{i}", [128, 8, 512], BF16) for i in range(2)]
            wgc = [sbd(f"wgc{i}", [128, 8, 512], BF16) for i in range(2)]
            wbr = [sbd(f"wbr{i}", [64, 8, 512], BF16) for i in range(2)]
            wpw = [sbd(f"wpw{i}", [128, 4, 512], BF16) for i in range(2)]
            sa = [sbd(f"sa{i}", [128, 512], F32) for i in range(2)]
            sc_ = [sbd(f"sc{i}", [128, 512], F32) for i in range(2)]
            m1 = [sbd(f"m1{i}", [128, 512], F32) for i in range(2)]
            ph = Phase(nc, nm + "d1")
            rot = PsumRot(pbank[0:8], "mrg")
            for cb in range(2):
                wload(ph, wbr[cb][:], wbr_d[l, cb], ("wbr", cb), 4 if cb == 0 else 1)
                wload(ph, wpw[cb][:], wpw_d[l, cb], ("wpw", cb), 2 if cb == 0 else 1, npieces=2)
                wload(ph, wga[cb][:], win_d[l, 5 + cb], ("wga", cb), 4 if cb == 0 else 1)
                wload(ph, wgc[cb][:], win_d[l, 7 + cb], ("wgc", cb), 4 if cb == 0 else 1)
            it = 0
            for m in range(8):
                cb, mc = m // 4, (m % 4) * 128
                for tg in range(4):
                    tsl = slice(tg * 512, (tg + 1) * 512)
                    pya, kya = rot.next()
                    pyc, kyc = rot.next()
                    pga, kga = rot.next()
                    pgc, kgc = rot.next()
                    for hh in range(H):
                        ph.op("pe", lambda e, pya=pya, hh=hh, cb=cb, mc=mc, tsl=tsl: e.matmul(
                            pya[:], lhsT=wbr[cb][:, hh, mc:mc + 128], rhs=attnT[:, hh, tsl], start=(hh == 0), stop=(hh == H - 1)),
                            reads=[("wbr", cb, hh // 2)], writes=[kya])
                    for k in range(4):
                        ph.op("pe", lambda e, pyc=pyc, k=k, cb=cb, mc=mc, tsl=tsl: e.matmul(
                            pyc[:], lhsT=wpw[cb][:, k, mc:mc + 128], rhs=cvT[:, k, tsl], start=(k == 0), stop=(k == 3)),
                            reads=[("wpw", cb, k // 2)], writes=[kyc])
                    for k in range(8):
                        ph.op("pe", lambda e, pga=pga, k=k, cb=cb, mc=mc, tsl=tsl: e.matmul(
                            pga[:], lhsT=wga[cb][:, k, mc:mc + 128], rhs=uT[:, k, tsl], start=(k == 0), stop=(k == 7)),
                            reads=[("wga", cb, k // 2)], writes=[kga])
                    for k in range(8):
                        ph.op("pe", lambda e, pgc=pgc, k=k, cb=cb, mc=mc, tsl=tsl: e.matmul(
                            pgc[:], lhsT=wgc[cb][:, k, mc:mc + 128], rhs=uT[:, k, tsl], start=(k == 0), stop=(k == 7)),
                            reads=[("wgc", cb, k // 2)], writes=[kgc])
                    i2 = it % 2
                    it += 1
                    ph.op("act", lambda e, i2=i2, pga=pga: e.activation(out=sa[i2][:], in_=pga[:], func=AF.Tanh, scale=0.5), reads=[kga], writes=[("sa", i2)])
                    ph.op("act", lambda e, i2=i2, pgc=pgc: e.activation(out=sc_[i2][:], in_=pgc[:], func=AF.Tanh, scale=0.5), reads=[kgc], writes=[("sc", i2)])
                    ph.op("dve", lambda e, i2=i2, pya=pya: e.scalar_tensor_tensor(out=m1[i2][:], in0=sa[i2][:], scalar=1.0, in1=pya[:], op0=ALU.add, op1=ALU.mult),
                          reads=[("sa", i2), kya], writes=[("m1", i2)])
                    ph.op("dve", lambda e, i2=i2, pyc=pyc: e.scalar_tensor_tensor(out=sc_[i2][:], in0=sc_[i2][:], scalar=1.0, in1=pyc[:], op0=ALU.add, op1=ALU.mult),
                          reads=[("sc", i2), kyc], writes=[("sc", i2)])
                    ph.op("dve", lambda e, i2=i2, m=m, tsl=tsl: e.tensor_tensor(out=mg[:, m, tsl], in0=m1[i2][:], in1=sc_[i2][:], op=ALU.add),
                          reads=[("m1", i2), ("sc", i2)], writes=[("mg", m)])
            ph.emit()
        es_at.close()
        es_cv.close()
        esh.close()
        with contextlib.ExitStack() as esd2:
            xT = esd2.enter_context(_sbt(nc, "xT", [128, 8, HALF], F32))
            wo = [esd2.enter_context(_sbt(nc, f"wo{i}", [128, 8, 512], BF16)) for i in range(2)]
            ph = Phase(nc, nm + "d2")
            for tg in range(4):
                ph.dma("sp", xT[:, :, tg * 512:(tg + 1) * 512], xs3[:, :, tg * 512:(tg + 1) * 512], writes=[("xld", tg)])
            for cb in range(2):
                wload(ph, wo[cb][:], wout_d[l, cb], ("wo", cb), 4 if cb == 0 else 1)
            def ymm(ph, pt, pk, mo, tg):
                cb, mc = mo // 4, (mo % 4) * 128
                tsl = slice(tg * 512, (tg + 1) * 512)
                for k in range(8):
                    ph.op("pe", lambda e, k=k: e.matmul(pt[:], lhsT=wo[cb][:, k, mc:mc + 128], rhs=mg[:, k, tsl], start=(k == 0), stop=(k == 7)),
                          reads=[("wo", cb, k // 2)], writes=[pk])
            deepnorm_ln(nc, ph, esd2, ymm, xT, prm, l, 2, 6, 7, pbank, ones_b, ntg=4, conc=2)
            for tg in range(4):
                ph.dma("sp", xs3[:, :, tg * 512:(tg + 1) * 512], xT[:, :, tg * 512:(tg + 1) * 512],
                       reads=[("x", mo, tg) for mo in range(8)], writes=[("xst", tg)])
            if dbg_d.get("x1") is not None and l == 0:
                ph.dma("sp", dbg_d["x1"][hf], xT[:].rearrange("p k t -> p (k t)"), reads=[("x", mo, tg) for mo in range(8) for tg in range(4)])
            ph.emit()
    es_mg.close()

    for grp in range(2):
        g0 = grp * 1024
        with contextlib.ExitStack() as ese:
            sbe = lambda name, shape, dt=F32: ese.enter_context(_sbt(nc, name, shape, dt))
            xg = sbe("xg", [128, 8, 1024], F32)
            hm = sbe("hm", [128, 32, 1024], BF16)
            with contextlib.ExitStack() as ese1:
                sb1 = lambda name, shape, dt=F32: ese1.enter_context(_sbt(nc, name, shape, dt))
                u2 = sb1("u2", [128, 8, 1024], BF16)
                wu = [sb1(f"wu{i}", [128, 8, 512], BF16) for i in range(2)]
                rl = [sb1(f"rl{i}", [128, 512], BF16) for i in range(3)]
                ph = Phase(nc, nm + f"e{grp}")
                for k in range(8):
                    ph.dma("sp", xg[:, k, :], xs3[:, k, g0:g0 + 1024], writes=[("xT", k)])
                for k in range(8):
                    ph.op("dve", lambda e, k=k: e.tensor_scalar(out=u2[:, k, :], in0=xg[:, k, :], scalar1=P(l, 3, k), scalar2=P(l, 4, k),
                                                                op0=ALU.mult, op1=ALU.add), reads=[("xT", k)], writes=[("u2", k)])
                rot = PsumRot(pbank[0:4], "up")
                it = 0
                for jb in range(8):
                    w, wk = wu[jb % 2], ("wu", jb % 2)
                    wload(ph, w[:], wup_d[l, jb], wk, 4 if jb == 0 else 1)
                    for jj in range(4):
                        j = jb * 4 + jj
                        for sub in range(2):
                            pt, pk = rot.next()
                            for k in range(8):
                                ph.op("pe", lambda e, pt=pt, k=k, w=w, jj=jj, sub=sub: e.matmul(
                                    pt[:], lhsT=w[:, k, jj * 128:(jj + 1) * 128], rhs=u2[:, k, sub * 512:(sub + 1) * 512], start=(k == 0), stop=(k == 7)),
                                    reads=[wk + ((k // 2),), ("u2", k)], writes=[pk])
                            r, rk = rl[it % 3], ("rl", it % 3)
                            if it % 4 == 3:
                                ph.op("dve", lambda e, r=r, pt=pt: e.tensor_scalar(out=r[:], in0=pt[:], scalar1=0.0, scalar2=None, op0=ALU.max),
                                      reads=[pk], writes=[rk])
                            else:
                                ph.op("act", lambda e, r=r, pt=pt: e.activation(out=r[:], in_=pt[:], func=AF.Relu), reads=[pk], writes=[rk])
                            ph.op("dve", lambda e, r=r, j=j, sub=sub: e.tensor_tensor(out=hm[:, j, sub * 512:(sub + 1) * 512], in0=r[:], in1=r[:], op=ALU.mult),
                                  reads=[rk], writes=[("hm", j)])
                            it += 1
                ph.emit()
            with contextlib.ExitStack() as ese2:
                wd = [ese2.enter_context(_sbt(nc, f"wd{i}", [128, 32, 128], BF16)) for i in range(2)]
                ph = Phase(nc, nm + f"f{grp}")
                def ymm2(ph, pt, pk, mo, tg):
                    w, wk = wd[mo % 2], ("wd", mo % 2)
                    tsl = slice(tg * 512, (tg + 1) * 512)
                    if tg == 0:
                        wload(ph, w[:], wdn_d[l, mo], wk, 4 if mo == 0 else 1)
                    for k in range(32):
                        ph.op("pe", lambda e, k=k: e.matmul(pt[:], lhsT=w[:, k, :], rhs=hm[:, k, tsl], start=(k == 0), stop=(k == 31)),
                              reads=[wk + ((k // 8),)], writes=[pk])
                deepnorm_ln(nc, ph, ese2, ymm2, xg, prm, l, 5, 8, 9, pbank, ones_b, ntg=2, conc=2)
                ph.dma("sp", xs3[:, :, g0:g0 + 1024], xg[:], reads=[("x", mo, tg) for mo in range(8) for tg in range(2)], writes=["xst"])
                ph.emit()


def ln_stats(ph, s1p, s2p, st1, st2, st3, mhalf, inv_n):
    ph.op("dve", lambda e: e.tensor_scalar(out=st1[:], in0=s1p[:], scalar1=inv_n, scalar2=None, op0=ALU.mult), reads=["s1p"], writes=["mean"])
    ph.op("dve", lambda e: e.tensor_tensor(out=st2[:], in0=st1[:], in1=st1[:], op=ALU.mult), reads=["mean"], writes=["msq"])
    ph.op("dve", lambda e: e.scalar_tensor_tensor(out=st2[:], in0=s2p[:], scalar=inv_n, in1=st2[:], op0=ALU.mult, op1=ALU.subtract),
          reads=["s2p", "msq"], writes=["var"])
    ph.op("dve", lambda e: e.tensor_scalar(out=st2[:], in0=st2[:], scalar1=0.0, scalar2=EPS, op0=ALU.max, op1=ALU.add), reads=["var"], writes=["var2"])
    ph.op("act", lambda e: e.activation(out=st2[:], in_=st2[:], func=AF.Sqrt), reads=["var2"], writes=["sd"])
    ph.op("dve", lambda e: e.reciprocal(out=st3[:], in_=st2[:]), reads=["sd"], writes=["rstd"])


def deepnorm_ln(nc, ph, es, ymm, xT, prm, l, i_gt, i_g, i_b, pbank, ones_b, ntg, conc):
    sbx = lambda name, shape, dt=F32: es.enter_context(_sbt(nc, name, shape, dt))
    tb = [sbx(f"tb{i}", [128, 512], BF16) for i in range(4)]
    tq = [sbx(f"tq{i}", [128, 512], BF16) for i in range(4)]
    pendS = []
    st1 = [sbx(f"lst1{i}", [128, 512], F32) for i in range(conc)]
    st2 = sbx("lst2", [128, 512], F32)
    st3 = [sbx(f"lst3{i}", [128, 512], F32) for i in range(conc)]
    mhalf = sbx("lmhalf", [128, 512], F32)
    ph.op("pool", lambda e: e.memset(mhalf[:], -0.5), writes=["mhalf"])
    nyb = 8 - 2 * conc
    rot = PsumRot(pbank[0:nyb], "y")
    it = 0
    for base in range(0, ntg, conc):
        for mo in range(8):
            for ci in range(conc):
                tg = base + ci
                xsl = slice(tg * 512, (tg + 1) * 512)
                s1p, s2p = pbank[nyb + 2 * ci], pbank[nyb + 2 * ci + 1]
                pt, pk = rot.next()
                ymm(ph, pt, pk, mo, tg)
                xk = ("x", mo, tg)
                ph.op("dve", lambda e, pt=pt, mo=mo, xsl=xsl: e.scalar_tensor_tensor(
                    out=xT[:, mo, xsl], in0=pt[:], scalar=prm[:, l, i_gt * 8 + mo:i_gt * 8 + mo + 1], in1=xT[:, mo, xsl], op0=ALU.mult, op1=ALU.add),
                    reads=[pk, ("xld", tg)], writes=[xk])
                i2 = it % 4
                it += 1
                ph.op("act", lambda e, i2=i2, mo=mo, xsl=xsl: e.activation(out=tb[i2][:], in_=xT[:, mo, xsl], func=AF.Copy), reads=[xk], writes=[("tb", i2)])
                ph.op("act", lambda e, i2=i2, mo=mo, xsl=xsl: e.activation(out=tq[i2][:], in_=xT[:, mo, xsl], func=AF.Square), reads=[xk], writes=[("tq", i2)])
                def stats(i2=i2, mo=mo, s1p=s1p, s2p=s2p, ci=ci):
                    ph.op("pe", lambda e: e.matmul(s1p[:], lhsT=ones_b[:], rhs=tb[i2][:], start=(mo == 0), stop=(mo == 7)),
                          reads=[("tb", i2)], writes=[("s1p", ci)])
                    ph.op("pe", lambda e: e.matmul(s2p[:], lhsT=ones_b[:], rhs=tq[i2][:], start=(mo == 0), stop=(mo == 7)),
                          reads=[("tq", i2)], writes=[("s2p", ci)])
                pendS.append(stats)
                while len(pendS) > 2:
                    pendS.pop(0)()
        while pendS:
            pendS.pop(0)()
        for ci in range(conc):
            tg = base + ci
            xsl = slice(tg * 512, (tg + 1) * 512)
            s1p, s2p = pbank[nyb + 2 * ci], pbank[nyb + 2 * ci + 1]
            a, c = st1[ci], st3[ci]
            ph.op("dve", lambda e, a=a, s1p=s1p: e.tensor_scalar(out=a[:], in0=s1p[:], scalar1=1.0 / D, scalar2=None, op0=ALU.mult),
                  reads=[("s1p", ci)], writes=[("mean", ci)])
            ph.op("dve", lambda e, a=a: e.tensor_tensor(out=st2[:], in0=a[:], in1=a[:], op=ALU.mult), reads=[("mean", ci)], writes=["msq"])
            ph.op("dve", lambda e, s2p=s2p: e.scalar_tensor_tensor(out=st2[:], in0=s2p[:], scalar=1.0 / D, in1=st2[:], op0=ALU.mult, op1=ALU.subtract),
                  reads=[("s2p", ci), "msq"], writes=["var"])
            ph.op("dve", lambda e: e.tensor_scalar(out=st2[:], in0=st2[:], scalar1=0.0, scalar2=EPS, op0=ALU.max, op1=ALU.add), reads=["var"], writes=["var2"])
            ph.op("act", lambda e: e.activation(out=st2[:], in_=st2[:], func=AF.Sqrt), reads=["var2"], writes=["sd"])
            ph.op("dve", lambda e, c=c: e.reciprocal(out=c[:], in_=st2[:]), reads=["sd"], writes=[("rstd", ci)])
            for mo in range(8):
                xk = ("x", mo, tg)
                ph.op("dve", lambda e, mo=mo, xsl=xsl, a=a: e.tensor_tensor(out=xT[:, mo, xsl], in0=xT[:, mo, xsl], in1=a[:], op=ALU.subtract),
                      reads=[xk, ("mean", ci)], writes=[xk])
                ph.op("dve", lambda e, mo=mo, xsl=xsl, c=c: e.tensor_tensor(out=xT[:, mo, xsl], in0=xT[:, mo, xsl], in1=c[:], op=ALU.mult),
                      reads=[xk, ("rstd", ci)], writes=[xk])
                ph.op("act", lambda e, mo=mo, xsl=xsl: e.activation(
                    out=xT[:, mo, xsl], in_=xT[:, mo, xsl], func=AF.Identity, scale=prm[:, l, i_g * 8 + mo:i_g * 8 + mo + 1],
                    bias=prm[:, l, i_b * 8 + mo:i_b * 8 + mo + 1]), reads=[xk], writes=[xk])


def _cols(v):
    n = v.shape[-1] // 128
    return np.swapaxes(v.reshape(v.shape[:-1] + (n, 128)), -1, -2)


def _blk(w, bw):
    Lw, K, N = w.shape
    return np.ascontiguousarray(w.reshape(Lw, K // 128, 128, N // bw, bw).transpose(0, 3, 2, 1, 4)).reshape(Lw, N // bw, 128, (K // 128) * bw)


def prep_shared(inp):
    f = lambda a: np.asarray(a, dtype=np.float32)
    pvec = np.concatenate([_cols(f(inp["ln_mix_g"])), _cols(f(inp["ln_mix_b"])), _cols(f(inp["ln_ffn_g"])), _cols(f(inp["ln_ffn_b"])),
                           _cols(f(inp["b_ada"])), _cols(f(inp["conv_dw_b"])), _cols(f(inp["conv_ln_g"])), _cols(f(inp["conv_ln_b"]))], axis=-1)
    assert pvec.shape == (L, 128, 92), pvec.shape
    dwt = f(inp["conv_dw_w"])
    dww = np.ascontiguousarray(dwt.reshape(L, KC, 4, 128).transpose(0, 3, 2, 1)).reshape(L, 128, 4 * KC)
    wbr = f(inp["w_attn_br"]).reshape(L, H, 64, 2, 512).transpose(0, 3, 2, 1, 4)
    wbr = np.ascontiguousarray(wbr).reshape(L, 2, 64, 8 * 512)
    return {
        "pvec": np.ascontiguousarray(pvec), "dww": dww,
        "wada": _blk(f(inp["w_ada"]), 512), "win": _blk(f(inp["w_in"]), 512), "wbr": wbr,
        "wpw": _blk(f(inp["w_conv_pw"]), 512), "wout": _blk(f(inp["w_out"]), 512),
        "wup": _blk(f(inp["w_up"]), 512), "wdn": _blk(f(inp["w_down"]), 128),
    }


NPV = 92
_NC_CACHE = {}


def kernel(**inputs):
    shared = prep_shared(inputs)
    x = np.asarray(inputs["x"], dtype=np.float32)
    c = np.asarray(inputs["c"], dtype=np.float32)
    pos = np.asarray(inputs["positions"], dtype=np.int32)
    if "nc" not in _NC_CACHE:
        _NC_CACHE["nc"] = build(L)
    nc = _NC_CACHE["nc"]
    work = {0: 0, 1: 1, 4: 2, 5: 3}
    zeros = {k: np.zeros_like(v) for k, v in shared.items()}
    zeros["x"] = np.zeros((S, D), np.float32)
    zeros["cT"] = np.zeros((128, 8), np.float32)
    zeros["pos"] = np.zeros((128, 32), np.int32)
    in_maps = []
    for core in range(8):
        if core not in work:
            in_maps.append(zeros)
            continue
        b = work[core]
        m = dict(shared)
        m["x"] = np.ascontiguousarray(x[b])
        m["cT"] = np.ascontiguousarray(c[b].reshape(8, 128).T)
        m["pos"] = np.ascontiguousarray(pos[b].reshape(32, 128).T)
        in_maps.append(m)
    res = run_bass_kernel_spmd(nc, in_maps, core_ids=list(range(8)))
    inv = {b: core for core, b in work.items()}
    return np.stack([res.results[inv[b]]["out"] for b in range(B)], axis=0).astype(np.float32)
```

```python
import contextlib
import math
import numpy as np
import concourse.bass as bass
import concourse.mybir as mybir
from concourse.bass_utils import run_bass_kernel_spmd

F32 = mybir.dt.float32
BF16 = mybir.dt.bfloat16
I32 = mybir.dt.int32
AF = mybir.ActivationFunctionType
ALU = mybir.AluOpType
AX = mybir.AxisListType

D = 1024
S = 4096
B = 4
L = 4
H = 8
HD = 64
CW = 512
KC = 31
DFF = 4096
HALF = 2048
ALPHA = (2 * L) ** 0.25
EPS = 1e-5
BIG = 30000.0
NPV = 28 + 48 + 12


class _Op:
    __slots__ = ("eng", "fn", "deps", "is_dma", "idx", "sig", "sem", "semval", "prev_slot_val")


class Phase:
    NDMA = 12

    def __init__(self, nc, name):
        self.nc, self.name = nc, name
        self.ops, self.last_w, self.readers = [], {}, {}

    def op(self, eng, fn, reads=(), writes=(), dma=False):
        o = _Op()
        o.eng, o.fn, o.is_dma = eng, fn, dma
        deps = set()
        for k in reads:
            w = self.last_w.get(k)
            if w is not None:
                deps.add(w)
        for k in writes:
            w = self.last_w.get(k)
            if w is not None:
                deps.add(w)
            deps.update(self.readers.get(k, ()))
        o.idx = len(self.ops)
        deps.discard(o.idx)
        o.deps = deps
        self.ops.append(o)
        for k in reads:
            self.readers.setdefault(k, []).append(o.idx)
        for k in writes:
            self.last_w[k] = o.idx
            self.readers[k] = []
        return o.idx

    def dma(self, eng, out, in_, reads=(), writes=(), **kw):
        return self.op(eng, lambda e: e.dma_start(out=out, in_=in_, **kw), reads, writes, dma=True)

    def emit(self):
        for suf, n in TRUNC.items():
            if self.name.endswith(suf):
                print("phase", self.name, "ops", len(self.ops), "-> trunc", n)
                self.ops = self.ops[:n]
        nc, ops = self.nc, self.ops
        engs = ["pe", "act", "dve", "pool", "sp"]
        streams = {e: [o for o in ops if o.eng == e] for e in engs}

        def skip(p, o):
            return p.eng == "pe" and o.eng == "pe" and not p.is_dma and not o.is_dma

        for o in ops:
            o.sig = o.is_dma
        for o in ops:
            for d in o.deps:
                if not skip(ops[d], o):
                    ops[d].sig = True
        if id(nc) not in _SEMS:
            esem = {e: nc.semaphore(f"trk_{e}").__enter__() for e in engs}
            dsem = {e: [nc.semaphore(f"trk_d{e}{i}").__enter__() for i in range(self.NDMA)] for e in ("sp", "pool")}
            _SEMS[id(nc)] = (esem, dsem, {e: [0] * self.NDMA for e in dsem}, {e: 0 for e in dsem})
        esem, dsem, dcount, dnext = _SEMS[id(nc)]
        with nc.Block() as b0:
            def clr(e):
                for s_ in list(esem.values()):
                    e.sem_clear(s_)
            b0.gpsimd(clr)
        if True:
            ecount = {e: 0 for e in engs}
            dstart = {e: list(dcount[e]) for e in dsem}
            for o in ops:
                if o.is_dma:
                    s = dnext[o.eng] % self.NDMA
                    dnext[o.eng] += 1
                    o.prev_slot_val = dcount[o.eng][s]
                    dcount[o.eng][s] += 16
                    o.sem, o.semval = dsem[o.eng][s], dcount[o.eng][s]
                elif o.sig:
                    ecount[o.eng] += 1
                    o.sem, o.semval = esem[o.eng], ecount[o.eng]
                else:
                    o.sem, o.semval = None, 0
            with nc.Block() as blk:
                def run(e_name):
                    def body(e):
                        waited = {}
                        for o in streams[e_name]:
                            need = {}
                            for d in o.deps:
                                p = ops[d]
                                if skip(p, o):
                                    continue
                                key = id(p.sem)
                                if key not in need or need[key][1] < p.semval:
                                    need[key] = (p.sem, p.semval)
                            if o.is_dma and o.prev_slot_val > 0:
                                key = id(o.sem)
                                if key not in need or need[key][1] < o.prev_slot_val:
                                    need[key] = (o.sem, o.prev_slot_val)
                            for key, (s, v) in need.items():
                                if waited.get(key, 0) >= v:
                                    continue
                                e.wait_ge(s, v)
                                waited[key] = v
                            ins = o.fn(e)
                            if o.is_dma:
                                ins.then_inc(o.sem, 16)
                            elif o.sig:
                                ins.then_inc(o.sem, 1)
                        if e_name in dsem:
                            for i, s in enumerate(dsem[e_name]):
                                v = dcount[e_name][i]
                                if v > dstart[e_name][i] and waited.get(id(s), 0) < v:
                                    e.wait_ge(s, v)
                    return body
                for e_name, reg in (("pe", blk.tensor), ("act", blk.scalar), ("dve", blk.vector),
                                    ("pool", blk.gpsimd), ("sp", blk.sync)):
                    if streams[e_name]:
                        reg(run(e_name))
        if STOP[0] is not None and self.name.endswith(STOP[0]):
            raise StopBuild()


_UID = [0]


def _sbt(nc, name, shape, dt):
    _UID[0] += 1
    return nc.sbuf_tensor(f"{name}_u{_UID[0]}", shape, dt)


_SEMS = {}


class StopBuild(Exception):
    pass


STOP = [None]
TRUNC = {}


def wload(ph, dst, src, key, nsplit, npieces=4):
    K, N = dst.shape[1], dst.shape[2]
    step = K // nsplit
    for pc in range(nsplit):
        keys = [key + (q,) for q in range(pc * npieces // nsplit, (pc + 1) * npieces // nsplit)]
        ph.dma("pool", dst[:, pc * step:(pc + 1) * step, :].rearrange("p k n -> p (k n)"),
               src[:, pc * step * N:(pc + 1) * step * N], writes=keys, max_dma_last_dim=8192)


class PsumRot:
    def __init__(self, tiles, tag):
        self.tiles, self.tag, self.i = tiles, tag, 0

    def next(self):
        t = self.tiles[self.i % len(self.tiles)]
        k = (self.tag, self.i % len(self.tiles))
        self.i += 1
        return t, k


def build(depth=L, dbg=None):
    nc = bass.Bass("TRN2", target_bir_lowering=False)
    dt_in = lambda name, shape, dt=F32: nc.dram_tensor(name, shape, dt, kind="ExternalInput").ap()
    x_d = dt_in("x", [S, D])
    cT_d = dt_in("cT", [128, 8])
    pos_d = dt_in("pos", [128, 32], I32)
    pv_d = dt_in("pvec", [L, 128, NPV])
    dw_d = dt_in("dww", [L, 128, 4 * KC])
    wada_d = dt_in("wada", [L, 12, 128, 8 * 512])
    win_d = dt_in("win", [L, 9, 128, 8 * 512])
    wbr_d = dt_in("wbr", [L, 2, 64, 8 * 512])
    wpw_d = dt_in("wpw", [L, 2, 128, 4 * 512])
    wout_d = dt_in("wout", [L, 2, 128, 8 * 512])
    wup_d = dt_in("wup", [L, 8, 128, 8 * 512])
    wdn_d = dt_in("wdn", [L, 8, 128, 32 * 128])
    out_d = nc.dram_tensor("out", [S, D], F32, kind="ExternalOutput").ap()
    dbg_d = {}
    if dbg:
        for name, shape, dt in dbg:
            dbg_d[name] = nc.dram_tensor("dbg_" + name, shape, dt, kind="ExternalOutput").ap()

    xs_d = nc.dram_tensor("xs", [2, 128, 8 * HALF], F32).ap()
    qT_d = nc.dram_tensor("qTd", [H, 64, HALF], BF16).ap()
    kT_d = nc.dram_tensor("kTd", [H, 64, S], BF16).ap()
    v_d = nc.dram_tensor("vd", [S, 512], BF16).ap()
    bT_d = nc.dram_tensor("bTd", [H, 16, HALF], BF16).ap()

    es = contextlib.ExitStack()
    sb = lambda name, shape, dt=F32: es.enter_context(_sbt(nc, name, shape, dt))
    ps = lambda name, shape, dt=F32: es.enter_context(nc.psum_tensor(name, shape, dt))

    try:
      with es:
        _build_body(nc, es, sb, ps, locals(), depth)
    except StopBuild:
        pass
    return nc


def _build_body(nc, es, sb, ps, G, depth):
    x_d, cT_d, pos_d, pv_d, dw_d, wada_d, out_d = G["x_d"], G["cT_d"], G["pos_d"], G["pv_d"], G["dw_d"], G["wada_d"], G["out_d"]
    win_d, wbr_d, wpw_d, wout_d, wup_d, wdn_d = G["win_d"], G["wbr_d"], G["wpw_d"], G["wout_d"], G["wup_d"], G["wdn_d"]
    xs_d, qT_d, kT_d, v_d, bT_d, dbg_d = G["xs_d"], G["qT_d"], G["kT_d"], G["v_d"], G["bT_d"], G["dbg_d"]
    if True:
        ident_b = sb("ident_b", [128, 128], BF16)
        ident_f = sb("ident_f", [128, 128], F32)
        ones_b = sb("ones_b", [128, 128], BF16)
        trimask = sb("trimask", [128, 128], BF16)
        indic = sb("indic", [16, S], BF16)
        cos_t = sb("cos_t", [128, 32, 8], F32)
        sin_t = sb("sin_t", [128, 32, 8], F32)
        pv = sb("pv", [128, L, NPV], F32)
        prm = sb("prm", [128, L, 96], F32)
        dww = sb("dww", [128, L, 4 * KC], F32)
        kmT = sb("kmT", [64, H, 16], BF16)
        halo = sb("halo", [128, 4, 32], BF16)
        pbank = [ps(f"pb{i}", [128, 512], F32) for i in range(8)]

        def P(l, i, k=None):
            if k is None:
                return prm[:, l, i * 8:(i + 1) * 8]
            return prm[:, l, i * 8 + k:i * 8 + k + 1]


        with contextlib.ExitStack() as es0:
            sb0 = lambda name, shape, dt=F32: es0.enter_context(_sbt(nc, name, shape, dt))
            posi = sb0("posi", [128, 32], I32)
            posf = sb0("posf", [128, 32], F32)
            ang = sb0("ang", [128, 32, 8], F32)
            kk = sb0("kk", [128, 32, 8], F32)
            kki = sb0("kki", [128, 32, 8], I32)
            tmpa = sb0("tmpa", [128, 32, 8], F32)
            iot = sb0("iot", [128, 128], F32)
            iop = sb0("iop", [128, 1], F32)
            iot_i = sb0("iot_i", [128, 128], I32)
            iop_i = sb0("iop_i", [128, 1], I32)
            indi = sb0("indi", [16, S], I32)
            cT = sb0("cTs", [128, 8], F32)
            cth = sb0("cth", [128, 8], F32)
            cTb = sb0("cTb", [128, 8], BF16)
            wa = [sb0(f"wa{i}", [128, 8, 512], BF16) for i in range(2)]
            adas = sb0("adas", [128, L, 48], F32)
            indf = sb0("indf", [16, S], F32)
            halfpi = sb0("halfpi", [128, 1], F32)
            ph = Phase(nc, "p0")
            ph.op("pool", lambda e: e.iota(iot_i[:], pattern=[[1, 128]], base=0, channel_multiplier=0), writes=["iot_i"])
            ph.op("pool", lambda e: e.iota(iop_i[:], pattern=[[0, 1]], base=0, channel_multiplier=1), writes=["iop_i"])
            ph.op("dve", lambda e: e.tensor_copy(out=iot[:], in_=iot_i[:]), reads=["iot_i"], writes=["iot"])
            ph.op("dve", lambda e: e.tensor_copy(out=iop[:], in_=iop_i[:]), reads=["iop_i"], writes=["iop"])
            ph.op("dve", lambda e: e.memset(kmT[:], 0.0), writes=["kmT"])
            ph.op("dve", lambda e: e.tensor_scalar(out=ident_f[:], in0=iot[:], scalar1=iop[:, 0:1], scalar2=None,
                                                   op0=ALU.is_equal), reads=["iot", "iop"], writes=["idf"])
            ph.op("dve", lambda e: e.tensor_copy(out=ident_b[:], in_=ident_f[:]), reads=["idf"], writes=["idb"])
            ph.op("dve", lambda e: e.memset(ones_b[:], 1.0), writes=["ones"])
            ph.op("dve", lambda e: e.tensor_scalar(out=trimask[:], in0=iot[:], scalar1=iop[:, 0:1], scalar2=-BIG,
                                                   op0=ALU.is_lt, op1=ALU.mult), reads=["iot", "iop"], writes=["tri"])
            tmpi = sb0("tmpi", [16, S], F32)
            ph.op("pool", lambda e: e.iota(indi[:], pattern=[[1, S]], base=0, channel_multiplier=-256), writes=["indi"])
            ph.op("dve", lambda e: e.tensor_copy(out=indf[:], in_=indi[:]), reads=["indi"], writes=["indf"])
            ph.op("dve", lambda e: e.tensor_scalar(out=tmpi[:], in0=indf[:], scalar1=255.5, scalar2=None, op0=ALU.is_lt),
                  reads=["indf"], writes=["tmpi1"])
            ph.op("dve", lambda e: e.tensor_scalar(out=indf[:], in0=indf[:], scalar1=-0.5, scalar2=None, op0=ALU.is_gt),
                  reads=["indf", "tmpi1"], writes=["indf1"])
            ph.op("dve", lambda e: e.tensor_tensor(out=indic[:], in0=indf[:], in1=tmpi[:], op=ALU.mult),
                  reads=["indf1", "tmpi1"], writes=["indic"])
            ph.dma("sp", posi[:], pos_d, writes=["posi"])
            ph.op("dve", lambda e: e.tensor_copy(out=posf[:], in_=posi[:]), reads=["posi"], writes=["posf"])
            for i in range(8):
                invf = float(np.float32(500000.0) ** np.float32(-(2.0 * i) / 16.0))
                ph.op("dve", lambda e, i=i, invf=invf: e.tensor_scalar(out=ang[:, :, i], in0=posf[:], scalar1=invf,
                                                                       scalar2=None, op0=ALU.mult),
                      reads=["posf"], writes=["ang"])
            ph.op("dve", lambda e: e.tensor_scalar(out=kk[:], in0=ang[:], scalar1=float(1.0 / (2 * math.pi)), scalar2=None,
                                                   op0=ALU.mult), reads=["ang"], writes=["kk"])
            ph.op("dve", lambda e: e.tensor_copy(out=kki[:], in_=kk[:]), reads=["kk"], writes=["kki"])
            ph.op("dve", lambda e: e.tensor_copy(out=kk[:], in_=kki[:]), reads=["kki"], writes=["kk2"])
            C1, C2 = 6.28125, float(2 * math.pi - 6.28125)
            ph.op("dve", lambda e: e.scalar_tensor_tensor(out=ang[:], in0=kk[:], scalar=-C1, in1=ang[:], op0=ALU.mult,
                                                          op1=ALU.add), reads=["kk2", "ang"], writes=["ang"])
            ph.op("dve", lambda e: e.scalar_tensor_tensor(out=ang[:], in0=kk[:], scalar=-C2, in1=ang[:], op0=ALU.mult,
                                                          op1=ALU.add), reads=["kk2", "ang"], writes=["ang"])
            ph.op("dve", lambda e: e.tensor_scalar(out=tmpa[:], in0=ang[:], scalar1=math.pi, scalar2=-2 * math.pi,
                                                   op0=ALU.is_gt, op1=ALU.mult), reads=["ang"], writes=["tmpa"])
            ph.op("dve", lambda e: e.tensor_tensor(out=ang[:], in0=ang[:], in1=tmpa[:], op=ALU.add),
                  reads=["ang", "tmpa"], writes=["ang"])
            ph.op("dve", lambda e: e.tensor_scalar(out=tmpa[:], in0=ang[:], scalar1=-math.pi, scalar2=2 * math.pi,
                                                   op0=ALU.is_lt, op1=ALU.mult), reads=["ang"], writes=["tmpa"])
            ph.op("dve", lambda e: e.tensor_tensor(out=ang[:], in0=ang[:], in1=tmpa[:], op=ALU.add),
                  reads=["ang", "tmpa"], writes=["ang"])
            ph.op("act", lambda e: e.activation(out=sin_t[:], in_=ang[:], func=AF.Sin), reads=["ang"], writes=["sin"])
            ph.op("dve", lambda e: e.tensor_scalar(out=tmpa[:], in0=ang[:], scalar1=-1.0, scalar2=None, op0=ALU.mult),
                  reads=["ang"], writes=["tmpa"])
            ph.op("dve", lambda e: e.tensor_tensor(out=tmpa[:], in0=tmpa[:], in1=ang[:], op=ALU.max),
                  reads=["ang", "tmpa"], writes=["tmpa"])
            ph.op("dve", lambda e: e.memset(halfpi[:], math.pi / 2), writes=["halfpi"])
            ph.op("act", lambda e: e.activation(out=cos_t[:], in_=tmpa[:], func=AF.Sin, scale=-1.0, bias=halfpi[:, 0:1]),
                  reads=["tmpa", "halfpi"], writes=["cos"])
            ph.dma("sp", pv[:], pv_d.rearrange("l p n -> p l n"), writes=["pv"])
            ph.dma("sp", dww[:], dw_d.rearrange("l p n -> p l n"), writes=["dww"])
            ph.dma("sp", cT[:], cT_d, writes=["cT"])
            ph.op("act", lambda e: e.activation(out=cth[:], in_=cT[:], func=AF.Tanh, scale=0.5), reads=["cT"], writes=["cth"])
            ph.op("dve", lambda e: e.scalar_tensor_tensor(out=cth[:], in0=cth[:], scalar=1.0, in1=cT[:], op0=ALU.add,
                                                          op1=ALU.mult), reads=["cth", "cT"], writes=["cth2"])
            ph.op("dve", lambda e: e.tensor_scalar(out=cTb[:], in0=cth[:], scalar1=0.5, scalar2=None, op0=ALU.mult),
                  reads=["cth2"], writes=["cTb"])
            xT = sb0("xT", [128, 8, HALF], F32)
            xin = [sb0(f"xin{i}", [128, D], F32) for i in range(2)]
            xrot = PsumRot(pbank[4:8], "xps")

            def px_steps():
                for gtt in range(32):
                    hf, tt = divmod(gtt, 16)
                    xi, xk = xin[gtt % 2], ("xin", gtt % 2)
                    t0 = gtt * 128
                    ph.dma("sp", xi[:], x_d[t0:t0 + 128, :], writes=[xk])
                    for kq in range(2):
                        pt, pk = xrot.next()
                        for j in range(4):
                            k = kq * 4 + j
                            ph.op("pe", lambda e, pt=pt, j=j, k=k, xi=xi: e.transpose(
                                pt[:, j * 128:(j + 1) * 128], xi[:, k * 128:(k + 1) * 128], ident_f[:]),
                                reads=[xk, "idf"], writes=[pk])
                        dst = xT[:, kq * 4:(kq + 1) * 4, tt * 128:(tt + 1) * 128]
                        src = pt[:].rearrange("p (j t) -> p j t", j=4)
                        if kq == 0:
                            ph.op("act", lambda e, dst=dst, src=src: e.activation(out=dst, in_=src, func=AF.Copy, scale=ALPHA),
                                  reads=[pk], writes=[("xT", tt, kq)])
                        else:
                            ph.op("dve", lambda e, dst=dst, src=src: e.tensor_scalar(out=dst, in0=src, scalar1=ALPHA, scalar2=None,
                                                                                     op0=ALU.mult), reads=[pk], writes=[("xT", tt, kq)])
                    if tt % 4 == 3:
                        tg = tt // 4
                        xs3 = xs_d[hf].rearrange("p (k t) -> p k t", k=8)
                        ph.dma("sp", xs3[:, :, tg * 512:(tg + 1) * 512], xT[:, :, tg * 512:(tg + 1) * 512],
                               reads=[("xT", t_, q_) for t_ in range(tg * 4, tg * 4 + 4) for q_ in range(2)],
                               writes=[("xT", t_, q_) for t_ in range(tg * 4, tg * 4 + 4) for q_ in range(2)] + ["xs"])
                    yield

            pxg = px_steps()
            rot = PsumRot(pbank[0:2], "adaps")
            it = 0
            for l in range(depth):
                for cb in range(12):
                    w = wa[it % 2]
                    wk = ("wa", it % 2)
                    it += 1
                    ph.dma("pool", w[:].rearrange("p k n -> p (k n)"), wada_d[l, cb], writes=[wk], max_dma_last_dim=8192)
                    pt, pk = rot.next()
                    for m in range(4):
                        for k in range(8):
                            ph.op("pe", lambda e, w=w, m=m, k=k, pt=pt: e.matmul(
                                pt[:, m:m + 1], lhsT=w[:, k, m * 128:(m + 1) * 128], rhs=cTb[:, k:k + 1],
                                start=(k == 0), stop=(k == 7)), reads=[wk, "cTb"], writes=[pk])
                    ph.op("dve", lambda e, l=l, cb=cb, pt=pt: e.tensor_tensor(
                        out=adas[:, l, cb * 4:(cb + 1) * 4], in0=pt[:, 0:4], in1=pv[:, l, 32 + cb * 4:32 + (cb + 1) * 4],
                        op=ALU.add), reads=[pk, "pv"], writes=["adas"])
                    next(pxg, None)
            for _ in pxg:
                pass
            for l in range(depth):
                A = lambda i: adas[:, l, i * 8:(i + 1) * 8]
                def ts(out, in0, s1, s2, op0, op1=ALU.bypass, l=l):
                    ph.op("dve", lambda e: e.tensor_scalar(out=out, in0=in0, scalar1=s1, scalar2=s2, op0=op0, op1=op1),
                          reads=["adas", "pv", "dww"], writes=["prm"])
                ts(P(l, 0), A(1), 1.0, 1.0 / ALPHA, ALU.add, ALU.mult)
                ts(P(l, 1), A(0), 1.0, None, ALU.mult)
                ts(P(l, 2), A(2), 1.0, 0.5, ALU.add, ALU.mult)
                ts(P(l, 3), A(4), 1.0, 1.0 / ALPHA, ALU.add, ALU.mult)
                ts(P(l, 4), A(3), 1.0, None, ALU.mult)
                ts(P(l, 5), A(5), 1.0, None, ALU.add)
                ts(P(l, 6), pv[:, l, 0:8], ALPHA, None, ALU.mult)
                ts(P(l, 7), pv[:, l, 8:16], ALPHA, None, ALU.mult)
                ts(P(l, 8), pv[:, l, 16:24], ALPHA, None, ALU.mult)
                ts(P(l, 9), pv[:, l, 24:32], ALPHA, None, ALU.mult)
                ts(prm[:, l, 80:84], pv[:, l, 80:84], 1.0, None, ALU.mult)
                ts(prm[:, l, 84:88], pv[:, l, 84:88], 0.5, None, ALU.mult)
                ts(prm[:, l, 88:92], pv[:, l, 88:92], 0.5, None, ALU.mult)
                ph.op("dve", lambda e, l=l: e.tensor_scalar(out=dww[:, l, :], in0=dww[:, l, :], scalar1=0.5, scalar2=None,
                                                            op0=ALU.mult), reads=["dww"], writes=["dww"])
            ph.emit()

        for l in range(depth):
            for hf in range(2):
                layer_half(nc, l, hf, locals())

        for hf in range(2):
            with contextlib.ExitStack() as es1:
                xT = es1.enter_context(_sbt(nc, "xT", [128, 8, HALF], F32))
                xo = [es1.enter_context(_sbt(nc, f"xo{i}", [128, D], F32)) for i in range(2)]
                ph = Phase(nc, f"pf{hf}")
                ph.dma("sp", xT[:].rearrange("p k t -> p (k t)"), xs_d[hf], writes=["xT"])
                rot = PsumRot(pbank[0:4], "ops")
                for tt in range(16):
                    xi, xk = xo[tt % 2], ("xo", tt % 2)
                    for kq in range(2):
                        pt, pk = rot.next()
                        for j in range(4):
                            k = kq * 4 + j
                            ph.op("pe", lambda e, pt=pt, j=j, k=k, tt=tt: e.transpose(
                                pt[:, j * 128:(j + 1) * 128], xT[:, k, tt * 128:(tt + 1) * 128], ident_f[:]),
                                reads=["xT"], writes=[pk])
                        dst = xi[:, kq * 512:(kq + 1) * 512]
                        if kq == 0:
                            ph.op("act", lambda e, dst=dst, pt=pt: e.activation(out=dst, in_=pt[:], func=AF.Copy, scale=1.0 / ALPHA),
                                  reads=[pk], writes=[(xk, kq)])
                        else:
                            ph.op("dve", lambda e, dst=dst, pt=pt: e.tensor_scalar(out=dst, in0=pt[:], scalar1=1.0 / ALPHA, scalar2=None,
                                                                                   op0=ALU.mult), reads=[pk], writes=[(xk, kq)])
                    t0 = hf * HALF + tt * 128
                    ph.dma("sp", out_d[t0:t0 + 128, :], xi[:], reads=[(xk, 0), (xk, 1)], writes=["out"])
                ph.emit()
    return nc


def layer_half(nc, l, hf, env):
    g = env
    prm, dww, pbank = g["prm"], g["dww"], g["pbank"]
    ident_b, ones_b, trimask, indic = g["ident_b"], g["ones_b"], g["trimask"], g["indic"]
    cos_t, sin_t, kmT, halo = g["cos_t"], g["sin_t"], g["kmT"], g["halo"]
    xs_d, qT_d, kT_d, v_d, bT_d = g["xs_d"], g["qT_d"], g["kT_d"], g["v_d"], g["bT_d"]
    win_d, wbr_d, wpw_d, wout_d, wup_d, wdn_d = g["win_d"], g["wbr_d"], g["wpw_d"], g["wout_d"], g["wup_d"], g["wdn_d"]
    dbg_d = g["dbg_d"]
    P = g["P"]
    T0 = hf * HALF
    nm = f"l{l}h{hf}"
    xs3 = xs_d[hf].rearrange("p (k t) -> p k t", k=8)
    es_mg = contextlib.ExitStack()
    mg = es_mg.enter_context(_sbt(nc, "mg", [128, 8, HALF], BF16))
    esh = contextlib.ExitStack()
    uT = esh.enter_context(_sbt(nc, "uT", [128, 8, HALF], BF16))
    es_x = contextlib.ExitStack()
    xT = es_x.enter_context(_sbt(nc, "xT", [128, 8, HALF], F32))
    if True:
        with contextlib.ExitStack() as esa:
            sba = lambda name, shape, dt=F32: esa.enter_context(_sbt(nc, name, shape, dt))
            wqkv = sba("wqkv", [128, 3, 8, 512], BF16)
            qk_sb = [sba(f"qk_sb{i}", [128, 512], BF16) for i in range(4)]
            v_sb = [sba(f"v_sb{i}", [128, 512], BF16) for i in range(2)]
            rtmp = [sba(f"rtmp{i}", [128, H, 8], F32) for i in range(8)]
            qT_st = sba("qT_st", [64, H, 512], BF16)
            kT_st = sba("kT_st", [64, H, 512], BF16)
            kms = sba("kms", [64, 16], F32)
            gsb = sba("gsb", [128, H, 16], F32)
            top8 = sba("top8", [128, H, 8], F32)
            msk = sba("msk", [128, H, 16], F32)
            bias_sb = sba("bias_sb", [128, H, 16], BF16)
            bT_st = sba("bT_st", [16, H, 512], BF16)
            VB = sba("VB", [128, 8, 16], F32)
            NB = sba("NB", [128, 8, 16], F32)
            VS = sba("VS", [128, 8, 16], F32)
            ph = Phase(nc, nm + "a")
            for k in range(8):
                ph.dma("sp", xT[:, k, :], xs3[:, k, :], writes=[("xT", k)])
            for i in range(3):
                wload(ph, wqkv[:, i], win_d[l, i], ("wqkv", i), 4)
            own0 = 8 * hf
            ph.op("pool", lambda e: e.memset(VB[:], -1e30), writes=["VB"])
            ph.op("pool", lambda e: e.memset(NB[:], -BIG), writes=["NB"])
            ph.op("pool", lambda e: e.memset(VS[:], -BIG), writes=["VS"])
            for ob in range(8):
                own = own0 + ob
                if own > 0:
                    ph.op("pool", lambda e, ob=ob, own=own: e.memset(VB[:, ob, 0:own], 0.0), reads=["VB"], writes=["VB"])
                ph.op("pool", lambda e, ob=ob, own=own: e.memset(VS[:, ob, 0:own + 1], 0.0), reads=["VS"], writes=["VS"])
                ph.op("pool", lambda e, ob=ob, own=own: e.memset(NB[:, ob, own:own + 1], 0.0), reads=["NB"], writes=["NB"])
            for k in range(8):
                ph.op("dve", lambda e, k=k: e.tensor_scalar(out=uT[:, k, :], in0=xT[:, k, :], scalar1=P(l, 0, k), scalar2=P(l, 1, k),
                                                            op0=ALU.mult, op1=ALU.add), reads=[("xT", k)], writes=[("uT", k)])
            rot = PsumRot(pbank[0:3], "qkv")
            trot = PsumRot(pbank[3:5], "tr")
            grot = PsumRot(pbank[5:7], "gate")
            pendA = []
            for tg in range(4):
                for t4 in range(4):
                    tt = tg * 4 + t4
                    gt = hf * 16 + tt
                    tsl = slice(tt * 128, (tt + 1) * 128)
                    for blk in range(3):
                        pt, pk = rot.next()
                        for k in range(8):
                            ph.op("pe", lambda e, pt=pt, k=k, blk=blk, tsl=tsl: e.matmul(
                                pt[:], lhsT=uT[:, k, tsl], rhs=wqkv[:, blk, k, :], start=(k == 0), stop=(k == 7)),
                                reads=[("wqkv", blk, k // 2), ("uT", k)], writes=[pk])
                        if blk == 0:
                            while pendA:
                                pendA.pop(0)()
                        if blk == 2:
                            vs, vk = v_sb[tt % 2], ("v_sb", tt % 2)
                            ph.op("act", lambda e, vs=vs, pt=pt: e.activation(out=vs[:], in_=pt[:], func=AF.Copy), reads=[pk], writes=[vk])
                            ph.dma("sp", v_d[T0 + tt * 128:T0 + (tt + 1) * 128, :], vs[:], reads=[vk], writes=["v_d"])
                            continue
                        qs, qk = qk_sb[blk * 2 + tt % 2], ("qk_sb", blk * 2 + tt % 2)
                        p3 = pt[:].rearrange("p (h d) -> p h d", h=H)
                        q3 = qs[:].rearrange("p (h d) -> p h d", h=H)
                        ph.op("dve", lambda e, q3=q3, p3=p3: e.tensor_copy(out=q3[:, :, 16:64], in_=p3[:, :, 16:64]),
                              reads=[pk], writes=[(qk, "nr")])
                        cosb = cos_t[:, gt, :].unsqueeze(1).to_broadcast([128, H, 8])
                        sinb = sin_t[:, gt, :].unsqueeze(1).to_broadcast([128, H, 8])
                        x1, x2 = p3[:, :, 0:8], p3[:, :, 8:16]
                        r = rtmp[blk * 4:blk * 4 + 4]
                        ro = blk * 4
                        def tt_(out, a, b, op, rk, wk, extra_r=()):
                            ph.op("dve", lambda e: e.tensor_tensor(out=out, in0=a, in1=b, op=op), reads=list(rk) + list(extra_r), writes=wk)
                        tt_(r[0][:], x1, cosb, ALU.mult, [pk], [("rt", ro + 0)])
                        tt_(r[1][:], x2, sinb, ALU.mult, [pk], [("rt", ro + 1)])
                        tt_(r[2][:], x2, cosb, ALU.mult, [pk], [("rt", ro + 2)])
                        tt_(r[3][:], x1, sinb, ALU.mult, [pk], [("rt", ro + 3)])
                        tt_(q3[:, :, 0:8], r[0][:], r[1][:], ALU.subtract, [("rt", ro + 0), ("rt", ro + 1)], [(qk, "r1")])
                        tt_(q3[:, :, 8:16], r[2][:], r[3][:], ALU.add, [("rt", ro + 2), ("rt", ro + 3)], [(qk, "r2")])
                        def trans(qs=qs, qk=qk, blk=blk, t4=t4):
                            tp, tk = trot.next()
                            tpb = tp[:].bitcast(BF16)
                            for h in range(H):
                                ph.op("pe", lambda e, tpb=tpb, h=h, qs=qs: e.transpose(
                                    tpb[0:64, h * 128:(h + 1) * 128], qs[:, h * 64:(h + 1) * 64], ident_b[:]),
                                    reads=[(qk, "nr"), (qk, "r1"), (qk, "r2")], writes=[tk])
                            st = qT_st if blk == 0 else kT_st
                            stk = ("qT_st" if blk == 0 else "kT_st")
                            ph.op("act", lambda e, st=st, tpb=tpb, t4=t4: e.activation(
                                out=st[:, :, t4 * 128:(t4 + 1) * 128], in_=tpb[0:64, 0:1024].rearrange("p (h t) -> p h t", h=H), func=AF.Copy),
                                reads=[tk], writes=[(stk, t4)])
                        pendA.append(trans)
                while pendA:
                    pendA.pop(0)()
                ph.dma("sp", kT_d[:, :, T0 + tg * 512:T0 + (tg + 1) * 512].rearrange("h d t -> d h t"), kT_st[:],
                       reads=[("kT_st", i) for i in range(4)], writes=["kT_d"])
                ph.dma("sp", qT_d[:, :, tg * 512:(tg + 1) * 512].rearrange("h d t -> d h t"), qT_st[:],
                       reads=[("qT_st", i) for i in range(4)], writes=["qT_d"])
                b0 = own0 + 2 * tg
                ph.op("dve", lambda e: e.tensor_reduce(out=kms[:], in_=kT_st[:].rearrange("p h (b t) -> p (h b) t", b=2),
                                                       axis=AX.X, op=ALU.add), reads=[("kT_st", i) for i in range(4)], writes=["kms"])
                ph.op("dve", lambda e, b0=b0: e.tensor_scalar(out=kmT[:, :, b0:b0 + 2], in0=kms[:].rearrange("p (h b) -> p h b", b=2),
                                                              scalar1=1.0 / 256, scalar2=None, op0=ALU.mult), reads=["kms"], writes=["kmT"])
                for t4 in range(4):
                    tt = tg * 4 + t4
                    ob = tt // 2
                    gp, gk = grot.next()
                    for h in range(H):
                        ph.op("pe", lambda e, gp=gp, h=h, t4=t4: e.matmul(
                            gp[:, h * 16:(h + 1) * 16], lhsT=qT_st[:, h, t4 * 128:(t4 + 1) * 128], rhs=kmT[:, h, :], start=True, stop=True),
                            reads=[("qT_st", t4), "kmT"], writes=[gk])
                    g3 = gp[:, 0:128].rearrange("p (h s) -> p h s", h=H)
                    bc = lambda t: t[:, ob, :].unsqueeze(1).to_broadcast([128, H, 16])
                    ph.op("dve", lambda e, g3=g3, ob=ob: e.tensor_tensor(out=gsb[:], in0=g3, in1=VB[:, ob, :].unsqueeze(1).to_broadcast([128, H, 16]),
                                                                         op=ALU.add), reads=[gk, "VB"], writes=["gsb"])
                    for h in range(H):
                        ph.op("dve", lambda e, h=h: e.max(out=top8[:, h, :], in_=gsb[:, h, :]), reads=["gsb"], writes=[("top8", h)])
                    ph.op("dve", lambda e: e.tensor_tensor(out=msk[:], in0=gsb[:], in1=top8[:, :, 2:3].to_broadcast([128, H, 16]), op=ALU.is_lt),
                          reads=["gsb"] + [("top8", h) for h in range(H)], writes=["msk"])
                    ph.op("dve", lambda e, ob=ob: e.tensor_tensor(out=msk[:], in0=msk[:], in1=NB[:, ob, :].unsqueeze(1).to_broadcast([128, H, 16]),
                                                                  op=ALU.mult), reads=["msk", "NB"], writes=["msk"])
                    ph.op("dve", lambda e, ob=ob: e.tensor_tensor(out=bias_sb[:], in0=msk[:], in1=VS[:, ob, :].unsqueeze(1).to_broadcast([128, H, 16]),
                                                                  op=ALU.add), reads=["msk", "VS"], writes=["bias_sb"])
                    tp, tk = trot.next()
                    tpb = tp[:].bitcast(BF16)
                    for h in range(H):
                        ph.op("pe", lambda e, tpb=tpb, h=h: e.transpose(tpb[0:16, h * 128:(h + 1) * 128], bias_sb[:, h, :], ident_b[:]),
                              reads=["bias_sb"], writes=[tk])
                    ph.op("act", lambda e, tpb=tpb, t4=t4: e.activation(
                        out=bT_st[:, :, t4 * 128:(t4 + 1) * 128], in_=tpb[0:16, 0:1024].rearrange("p (h t) -> p h t", h=H), func=AF.Copy),
                        reads=[tk], writes=[("bT_st", t4)])
                ph.dma("sp", bT_d[:, :, tg * 512:(tg + 1) * 512].rearrange("h s t -> s h t"), bT_st[:],
                       reads=[("bT_st", i) for i in range(4)], writes=["bT_d"])
            ph.emit()
        es_x.close()
        es_cv = contextlib.ExitStack()
        cvT = es_cv.enter_context(_sbt(nc, "cvT", [128, 4, HALF], BF16))

        with contextlib.ExitStack() as esb:
            sbb = lambda name, shape, dt=F32: esb.enter_context(_sbt(nc, name, shape, dt))
            hT = sbb("hT", [128, 4, 32 + HALF], BF16)
            wgl = sbb("wgl", [128, 2, 8, 512], BF16)
            dmat = sbb("dmat", [128, 4 * KC, 128], BF16)
            sg = [sbb(f"sg{i}", [128, 512], F32) for i in range(2)]
            cv = sbb("cv", [128, 4, 512], F32)
            cvb = sbb("cvb", [128, 4, 512], BF16)
            csq = sbb("csq", [128, 4, 512], BF16)
            st1 = sbb("st1", [128, 512], F32)
            st2 = sbb("st2", [128, 512], F32)
            st3 = sbb("st3", [128, 512], F32)
            mhalf = sbb("mhalf", [128, 512], F32)
            zt = [sbb(f"zt{i}", [128, 512], F32) for i in range(2)]
            th = [sbb(f"th{i}", [128, 512], F32) for i in range(2)]
            ph = Phase(nc, nm + "b")
            for i in range(2):
                wload(ph, wgl[:, i], win_d[l, 3 + i], ("wgl", i), 4)
            ph.op("pool", lambda e: e.memset(mhalf[:], -0.5), writes=["mhalf"])
            for j in range(4 * KC):
                ph.op("dve", lambda e, j=j: e.tensor_scalar(out=dmat[:, j, :], in0=ident_b[:], scalar1=dww[:, l, j:j + 1], scalar2=None,
                                                          op0=ALU.mult), writes=[("dmat", j)])
            if hf == 0:
                ph.op("dve", lambda e: e.memset(hT[:, :, 0:32], 0.0), writes=["hT_halo"])
            else:
                ph.op("dve", lambda e: e.tensor_copy(out=hT[:, :, 0:32], in_=halo[:]), writes=["hT_halo"])
            rot = PsumRot(pbank[0:4], "glu")
            for tg in range(4):
                tsl = slice(tg * 512, (tg + 1) * 512)
                for c in range(4):
                    pa, pak = rot.next()
                    pg, pgk = rot.next()
                    for which, pt, pk in ((0, pa, pak), (1, pg, pgk)):
                        for k in range(8):
                            ph.op("pe", lambda e, pt=pt, k=k, which=which, c=c, tsl=tsl: e.matmul(
                                pt[:], lhsT=wgl[:, which, k, c * 128:(c + 1) * 128], rhs=uT[:, k, tsl], start=(k == 0), stop=(k == 7)),
                                reads=[("wgl", which, k // 2)], writes=[pk])
                    s, sk = sg[c % 2], ("sg", c % 2)
                    ph.op("act", lambda e, s=s, pg=pg: e.activation(out=s[:], in_=pg[:], func=AF.Tanh, scale=0.5), reads=[pgk], writes=[sk])
                    ph.op("dve", lambda e, s=s, pa=pa, c=c, tg=tg: e.scalar_tensor_tensor(
                        out=hT[:, c, 32 + tg * 512:32 + (tg + 1) * 512], in0=s[:], scalar=1.0, in1=pa[:], op0=ALU.add, op1=ALU.mult),
                        reads=[sk, pak], writes=[("hT", tg)])
            if hf == 0:
                ph.op("dve", lambda e: e.tensor_copy(out=halo[:], in_=hT[:, :, HALF:HALF + 32]), reads=[("hT", 3)], writes=["halo"])
            crot = PsumRot(pbank[4:6], "conv")
            for tg in range(4):
                hk = ["hT_halo"] + [("hT", i) for i in range(max(0, tg - 1), tg + 1)]
                s1p, s2p = pbank[6], pbank[7]
                for c in range(4):
                    pt, pk = crot.next()
                    for k in range(KC):
                        c0 = 32 + tg * 512 - 30 + k
                        ph.op("pe", lambda e, pt=pt, c=c, k=k, c0=c0: e.matmul(
                            pt[:], lhsT=dmat[:, c * KC + k, :], rhs=hT[:, c, c0:c0 + 512], start=(k == 0), stop=(k == KC - 1)),
                            reads=hk + [("dmat", c * KC + k)], writes=[pk])
                    ph.op("act", lambda e, pt=pt, c=c: e.activation(out=cv[:, c, :], in_=pt[:], func=AF.Identity, bias=prm[:, l, 80 + c:81 + c]),
                          reads=[pk], writes=[("cv", c)])
                    ph.op("act", lambda e, c=c: e.activation(out=csq[:, c, :], in_=cv[:, c, :], func=AF.Square), reads=[("cv", c)], writes=[("csq", c)])
                    ph.op("act", lambda e, c=c: e.activation(out=cvb[:, c, :], in_=cv[:, c, :], func=AF.Copy), reads=[("cv", c)], writes=[("cvb", c)])
                for c in range(4):
                    ph.op("pe", lambda e, c=c: e.matmul(s1p[:], lhsT=ones_b[:], rhs=cvb[:, c, :], start=(c == 0), stop=(c == 3)),
                          reads=[("cvb", c)], writes=["s1p"])
                for c in range(4):
                    ph.op("pe", lambda e, c=c: e.matmul(s2p[:], lhsT=ones_b[:], rhs=csq[:, c, :], start=(c == 0), stop=(c == 3)),
                          reads=[("csq", c)], writes=["s2p"])
                ln_stats(ph, s1p, s2p, st1, st2, st3, mhalf, 1.0 / CW)
                for c in range(4):
                    z, zk = zt[c % 2], ("zt", c % 2)
                    t_, tk_ = th[c % 2], ("th", c % 2)
                    ph.op("dve", lambda e, c=c: e.tensor_tensor(out=cv[:, c, :], in0=cv[:, c, :], in1=st1[:], op=ALU.subtract),
                          reads=[("cv", c), "mean"], writes=[("cv", c)])
                    ph.op("dve", lambda e, c=c: e.tensor_tensor(out=cv[:, c, :], in0=cv[:, c, :], in1=st3[:], op=ALU.mult),
                          reads=[("cv", c), "rstd"], writes=[("cv", c)])
                    ph.op("dve", lambda e, c=c, z=z: e.tensor_scalar(out=z[:], in0=cv[:, c, :], scalar1=prm[:, l, 84 + c:85 + c],
                                                                     scalar2=prm[:, l, 88 + c:89 + c], op0=ALU.mult, op1=ALU.add),
                          reads=[("cv", c)], writes=[zk])
                    ph.op("act", lambda e, z=z, t_=t_: e.activation(out=t_[:], in_=z[:], func=AF.Tanh), reads=[zk], writes=[tk_])
                    ph.op("dve", lambda e, c=c, z=z, t_=t_, tg=tg: e.scalar_tensor_tensor(
                        out=cvT[:, c, tg * 512:(tg + 1) * 512], in0=t_[:], scalar=1.0, in1=z[:], op0=ALU.add, op1=ALU.mult),
                        reads=[zk, tk_], writes=[("cvT", tg)])
            ph.emit()
        es_at = contextlib.ExitStack()
        attnT = es_at.enter_context(_sbt(nc, "attnT", [64, H, HALF], BF16))

        nk = HALF * (hf + 1)
        nkt = nk // 128
        with contextlib.ExitStack() as esc:
            sbc = lambda name, shape, dt=F32: esc.enter_context(_sbt(nc, name, shape, dt))
            Qa = [sbc(f"Qa{i}", [80, HALF], BF16) for i in range(2)]
            Ka = [sbc(f"Ka{i}", [80, S], BF16) for i in range(2)]
            Va = [sbc(f"Va{i}", [128, 32, 128], BF16) for i in range(2)]
            pT = [sbc(f"pT{i}", [128, 512], BF16) for i in range(4)]
            rec = [sbc(f"rec{i}", [64, 512], F32) for i in range(2)]
            ph = Phase(nc, nm + "c")
            for i in range(2):
                ph.op("dve", lambda e, i=i: e.memset(Va[i][:, :, 64:128], 1.0), writes=[("Va1", i)])
                ph.dma("sp", Ka[i][64:80, 0:nk], indic[:, 0:nk] if hf == 1 else indic[:, 0:nk], writes=[("Kai", i)])
            if hf == 0:
                pass
            srot = PsumRot(pbank[0:4], "sT")
            orot = PsumRot(pbank[4:6], "oT")
            pi = 0
            pend = []
            LAG = 2

            def flush(n_keep):
                while len(pend) > n_keep:
                    pend.pop(0)()

            for h in range(H):
                b = h % 2
                ph.dma("sp", Qa[b][0:64, :], qT_d[h], writes=[("Qa", b)])
                ph.dma("sp", Qa[b][64:80, :], bT_d[h], writes=[("Qab", b)])
                ph.dma("sp", Ka[b][0:64, 0:nk], kT_d[h, :, 0:nk], writes=[("Ka", b)])
                ph.dma("sp", Va[b][:, 0:nkt, 0:64], v_d[0:nk, h * 64:(h + 1) * 64].rearrange("(t p) d -> p t d", p=128),
                       writes=[("Va", b)])
                rd = [("Qa", b), ("Qab", b), ("Ka", b), ("Kai", b)]
                for qg in range(4):
                    op_, ok_ = orot.next()
                    ktiles = list(range(hf * 16 + qg * 4 + 4))
                    for kt in ktiles:
                        lt = kt - hf * 16
                        r0 = max(0, lt - qg * 4)
                        q0 = qg * 512 + r0 * 128
                        n = 512 - r0 * 128
                        sp_, sk_ = srot.next()
                        diag = lt >= qg * 4
                        ph.op("pe", lambda e, sp_=sp_, b=b, kt=kt, q0=q0, n=n, diag=diag: e.matmul(
                            sp_[:, 0:n], lhsT=Ka[b][:, kt * 128:(kt + 1) * 128], rhs=Qa[b][:, q0:q0 + n], start=True, stop=not diag),
                            reads=rd, writes=[sk_])
                        if diag:
                            ph.op("pe", lambda e, sp_=sp_: e.matmul(sp_[:, 0:128], lhsT=ident_b[:], rhs=trimask[:], start=False, stop=True),
                                  reads=[], writes=[sk_])
                        p_, pk_ = pT[pi % 4], ("pT", pi % 4)
                        pi += 1
                        ph.op("act", lambda e, p_=p_, sp_=sp_, n=n: e.activation(out=p_[:, 0:n], in_=sp_[:, 0:n], func=AF.Exp, scale=HD ** -0.5),
                              reads=[sk_], writes=[pk_])
                        c0 = r0 * 128

                        def pv(op_=op_, ok_=ok_, p_=p_, pk_=pk_, b=b, kt=kt, c0=c0, n=n, first=(kt == 0), last=(kt == ktiles[-1])):
                            ph.op("pe", lambda e: e.matmul(op_[:, c0:c0 + n], lhsT=Va[b][:, kt, :], rhs=p_[:, 0:n], start=first, stop=last),
                                  reads=[pk_, ("Va", b), ("Va1", b)], writes=[ok_])
                        pend.append(pv)
                        flush(LAG)

                    def norm(op_=op_, ok_=ok_, h=h, qg=qg):
                        rc, rk = rec[qg % 2], ("rec", qg % 2)
                        ph.op("dve", lambda e: e.reciprocal(out=rc[:], in_=op_[64:128, :]), reads=[ok_], writes=[rk])
                        ph.op("dve", lambda e: e.tensor_tensor(out=attnT[:, h, qg * 512:(qg + 1) * 512], in0=op_[0:64, :], in1=rc[:], op=ALU.mult),
                              reads=[ok_, rk], writes=[("attnT", h)])
                    pend.append(norm)
            flush(0)
            ph.emit()

        with contextlib.ExitStack() as esd:
            sbd = lambda name, shape, dt=F32: esd.enter_context(_sbt(nc, name, shape, dt))
            wga = [sbd(f"wga{i}", [128, 8, 512], BF16) for i in range(2)]
            wgc = [sbd(f"wgc{i}", [128, 8, 512], BF16) for i in range(2)]
            wbr = [sbd(f"wbr{i}", [64, 8, 512], BF16) for i in range(2)]
            wpw = [sbd(f"wpw{i}", [128, 4, 512], BF16) for i in range(2)]
            sa = [sbd(f"sa{i}", [128, 512], F32) for i in range(2)]
            sc_ = [sbd(f"sc{i}", [128, 512], F32) for i in range(2)]
            m1 = [sbd(f"m1{i}", [128, 512], F32) for i in range(2)]
            ph = Phase(nc, nm + "d1")
            rot = PsumRot(pbank[0:8], "mrg")
            for cb in range(2):
                wload(ph, wbr[cb][:], wbr_d[l, cb], ("wbr", cb), 4 if cb == 0 else 1)
                wload(ph, wpw[cb][:], wpw_d[l, cb], ("wpw", cb), 2 if cb == 0 else 1, npieces=2)
                wload(ph, wga[cb][:], win_d[l, 5 + cb], ("wga", cb), 4 if cb == 0 else 1)
                wload(ph, wgc[cb][:], win_d[l, 7 + cb], ("wgc", cb), 4 if cb == 0 else 1)
            it = 0
            for m in range(8):
                cb, mc = m // 4, (m % 4) * 128
                for tg in range(4):
                    tsl = slice(tg * 512, (tg + 1) * 512)
                    pya, kya = rot.next()
                    pyc, kyc = rot.next()
                    pga, kga = rot.next()
                    pgc, kgc = rot.next()
                    for hh in range(H):
                        ph.op("pe", lambda e, pya=pya, hh=hh, cb=cb, mc=mc, tsl=tsl: e.matmul(
                            pya[:], lhsT=wbr[cb][:, hh, mc:mc + 128], rhs=attnT[:, hh, tsl], start=(hh == 0), stop=(hh == H - 1)),
                            reads=[("wbr", cb, hh // 2)], writes=[kya])
                    for k in range(4):
                        ph.op("pe", lambda e, pyc=pyc, k=k, cb=cb, mc=mc, tsl=tsl: e.matmul(
                            pyc[:], lhsT=wpw[cb][:, k, mc:mc + 128], rhs=cvT[:, k, tsl], start=(k == 0), stop=(k == 3)),
                            reads=[("wpw", cb, k // 2)], writes=[kyc])
                    for k in range(8):
                        ph.op("pe", lambda e, pga=pga, k=k, cb=cb, mc=mc, tsl=tsl: e.matmul(
                            pga[:], lhsT=wga[cb][:, k, mc:mc + 128], rhs=uT[:, k, tsl], start=(k == 0), stop=(k == 7)),
                            reads=[("wga", cb, k // 2)], writes=[kga])
                    for k in range(8):
                        ph.op("pe", lambda e, pgc=pgc, k=k, cb=cb, mc=mc, tsl=tsl: e.matmul(
                            pgc[:], lhsT=wgc[cb][:, k, mc:mc + 128], rhs=uT[:, k, tsl], start=(k == 0), stop=(k == 7)),
                            reads=[("wgc", cb, k // 2)], writes=[kgc])
                    i2 = it % 2
                    it += 1
                    ph.op("act", lambda e, i2=i2, pga=pga: e.activation(out=sa[i2][:], in_=pga[:], func=AF.Tanh, scale=0.5), reads=[kga], writes=[("sa", i2)])
                    ph.op("act", lambda e, i2=i2, pgc=pgc: e.activation(out=sc_[i2][:], in_=pgc[:], func=AF.Tanh, scale=0.5), reads=[kgc], writes=[("sc", i2)])
                    ph.op("dve", lambda e, i2=i2, pya=pya: e.scalar_tensor_tensor(out=m1[i2][:], in0=sa[i2][:], scalar=1.0, in1=pya[:], op0=ALU.add, op1=ALU.mult),
                          reads=[("sa", i2), kya], writes=[("m1", i2)])
                    ph.op("dve", lambda e, i2=i2, pyc=pyc: e.scalar_tensor_tensor(out=sc_[i2][:], in0=sc_[i2][:], scalar=1.0, in1=pyc[:], op0=ALU.add, op1=ALU.mult),
                          reads=[("sc", i2), kyc], writes=[("sc", i2)])
                    ph.op("dve", lambda e, i2=i2, m=m, tsl=tsl: e.tensor_tensor(out=mg[:, m, tsl], in0=m1[i2][:], in1=sc_[i2][:], op=ALU.add),
                          reads=[("m1", i2), ("sc", i2)], writes=[("mg", m)])
            ph.emit()
        es_at.close()
        es_cv.close()
        esh.close()
        with contextlib.ExitStack() as esd2:
            xT = esd2.enter_context(_sbt(nc, "xT", [128, 8, HALF], F32))
            wo = [esd2.enter_context(_sbt(nc, f"wo{i}", [128, 8, 512], BF16)) for i in range(2)]
            ph = Phase(nc, nm + "d2")
            for tg in range(4):
                ph.dma("sp", xT[:, :, tg * 512:(tg + 1) * 512], xs3[:, :, tg * 512:(tg + 1) * 512], writes=[("xld", tg)])
            for cb in range(2):
                wload(ph, wo[cb][:], wout_d[l, cb], ("wo", cb), 4 if cb == 0 else 1)
            def ymm(ph, pt, pk, mo, tg):
                cb, mc = mo // 4, (mo % 4) * 128
                tsl = slice(tg * 512, (tg + 1) * 512)
                for k in range(8):
                    ph.op("pe", lambda e, k=k: e.matmul(pt[:], lhsT=wo[cb][:, k, mc:mc + 128], rhs=mg[:, k, tsl], start=(k == 0), stop=(k == 7)),
                          reads=[("wo", cb, k // 2)], writes=[pk])
            deepnorm_ln(nc, ph, esd2, ymm, xT, prm, l, 2, 6, 7, pbank, ones_b, ntg=4, conc=2)
            for tg in range(4):
                ph.dma("sp", xs3[:, :, tg * 512:(tg + 1) * 512], xT[:, :, tg * 512:(tg + 1) * 512],
                       reads=[("x", mo, tg) for mo in range(8)], writes=[("xst", tg)])
            if dbg_d.get("x1") is not None and l == 0:
                ph.dma("sp", dbg_d["x1"][hf], xT[:].rearrange("p k t -> p (k t)"), reads=[("x", mo, tg) for mo in range(8) for tg in range(4)])
            ph.emit()
    es_mg.close()

    for grp in range(2):
        g0 = grp * 1024
        with contextlib.ExitStack() as ese:
            sbe = lambda name, shape, dt=F32: ese.enter_context(_sbt(nc, name, shape, dt))
            xg = sbe("xg", [128, 8, 1024], F32)
            hm = sbe("hm", [128, 32, 1024], BF16)
            with contextlib.ExitStack() as ese1:
                sb1 = lambda name, shape, dt=F32: ese1.enter_context(_sbt(nc, name, shape, dt))
                u2 = sb1("u2", [128, 8, 1024], BF16)
                wu = [sb1(f"wu{i}", [128, 8, 512], BF16) for i in range(2)]
                rl = [sb1(f"rl{i}", [128, 512], BF16) for i in range(3)]
                ph = Phase(nc, nm + f"e{grp}")
                for k in range(8):
                    ph.dma("sp", xg[:, k, :], xs3[:, k, g0:g0 + 1024], writes=[("xT", k)])
                for k in range(8):
                    ph.op("dve", lambda e, k=k: e.tensor_scalar(out=u2[:, k, :], in0=xg[:, k, :], scalar1=P(l, 3, k), scalar2=P(l, 4, k),
                                                                op0=ALU.mult, op1=ALU.add), reads=[("xT", k)], writes=[("u2", k)])
                rot = PsumRot(pbank[0:4], "up")
                it = 0
                for jb in range(8):
                    w, wk = wu[jb % 2], ("wu", jb % 2)
                    wload(ph, w[:], wup_d[l, jb], wk, 4 if jb == 0 else 1)
                    for jj in range(4):
                        j = jb * 4 + jj
                        for sub in range(2):
                            pt, pk = rot.next()
                            for k in range(8):
                                ph.op("pe", lambda e, pt=pt, k=k, w=w, jj=jj, sub=sub: e.matmul(
                                    pt[:], lhsT=w[:, k, jj * 128:(jj + 1) * 128], rhs=u2[:, k, sub * 512:(sub + 1) * 512], start=(k == 0), stop=(k == 7)),
                                    reads=[wk + ((k // 2),), ("u2", k)], writes=[pk])
                            r, rk = rl[it % 3], ("rl", it % 3)
                            if it % 4 == 3:
                                ph.op("dve", lambda e, r=r, pt=pt: e.tensor_scalar(out=r[:], in0=pt[:], scalar1=0.0, scalar2=None, op0=ALU.max),
                                      reads=[pk], writes=[rk])
                            else:
                                ph.op("act", lambda e, r=r, pt=pt: e.activation(out=r[:], in_=pt[:], func=AF.Relu), reads=[pk], writes=[rk])
                            ph.op("dve", lambda e, r=r, j=j, sub=sub: e.tensor_tensor(out=hm[:, j, sub * 512:(sub + 1) * 512], in0=r[:], in1=r[:], op=ALU.mult),
                                  reads=[rk], writes=[("hm", j)])
                            it += 1
                ph.emit()
            with contextlib.ExitStack() as ese2:
                wd = [ese2.enter_context(_sbt(nc, f"wd{i}", [128, 32, 128], BF16)) for i in range(2)]
                ph = Phase(nc, nm + f"f{grp}")
                def ymm2(ph, pt, pk, mo, tg):
                    w, wk = wd[mo % 2], ("wd", mo % 2)
                    tsl = slice(tg * 512, (tg + 1) * 512)
                    if tg == 0:
                        wload(ph, w[:], wdn_d[l, mo], wk, 4 if mo == 0 else 1)
                    for k in range(32):
                        ph.op("pe", lambda e, k=k: e.matmul(pt[:], lhsT=w[:, k, :], rhs=hm[:, k, tsl], start=(k == 0), stop=(k == 31)),
                              reads=[wk + ((k // 8),)], writes=[pk])
                deepnorm_ln(nc, ph, ese2, ymm2, xg, prm, l, 5, 8, 9, pbank, ones_b, ntg=2, conc=2)
                ph.dma("sp", xs3[:, :, g0:g0 + 1024], xg[:], reads=[("x", mo, tg) for mo in range(8) for tg in range(2)], writes=["xst"])
                ph.emit()


def ln_stats(ph, s1p, s2p, st1, st2, st3, mhalf, inv_n):
    ph.op("dve", lambda e: e.tensor_scalar(out=st1[:], in0=s1p[:], scalar1=inv_n, scalar2=None, op0=ALU.mult), reads=["s1p"], writes=["mean"])
    ph.op("dve", lambda e: e.tensor_tensor(out=st2[:], in0=st1[:], in1=st1[:], op=ALU.mult), reads=["mean"], writes=["msq"])
    ph.op("dve", lambda e: e.scalar_tensor_tensor(out=st2[:], in0=s2p[:], scalar=inv_n, in1=st2[:], op0=ALU.mult, op1=ALU.subtract),
          reads=["s2p", "msq"], writes=["var"])
    ph.op("dve", lambda e: e.tensor_scalar(out=st2[:], in0=st2[:], scalar1=0.0, scalar2=EPS, op0=ALU.max, op1=ALU.add), reads=["var"], writes=["var2"])
    ph.op("act", lambda e: e.activation(out=st2[:], in_=st2[:], func=AF.Sqrt), reads=["var2"], writes=["sd"])
    ph.op("dve", lambda e: e.reciprocal(out=st3[:], in_=st2[:]), reads=["sd"], writes=["rstd"])


def deepnorm_ln(nc, ph, es, ymm, xT, prm, l, i_gt, i_g, i_b, pbank, ones_b, ntg, conc):
    sbx = lambda name, shape, dt=F32: es.enter_context(_sbt(nc, name, shape, dt))
    tb = [sbx(f"tb{i}", [128, 512], BF16) for i in range(4)]
    tq = [sbx(f"tq{i}", [128, 512], BF16) for i in range(4)]
    pendS = []
    st1 = [sbx(f"lst1{i}", [128, 512], F32) for i in range(conc)]
    st2 = sbx("lst2", [128, 512], F32)
    st3 = [sbx(f"lst3{i}", [128, 512], F32) for i in range(conc)]
    mhalf = sbx("lmhalf", [128, 512], F32)
    ph.op("pool", lambda e: e.memset(mhalf[:], -0.5), writes=["mhalf"])
    nyb = 8 - 2 * conc
    rot = PsumRot(pbank[0:nyb], "y")
    it = 0
    for base in range(0, ntg, conc):
        for mo in range(8):
            for ci in range(conc):
                tg = base + ci
                xsl = slice(tg * 512, (tg + 1) * 512)
                s1p, s2p = pbank[nyb + 2 * ci], pbank[nyb + 2 * ci + 1]
                pt, pk = rot.next()
                ymm(ph, pt, pk, mo, tg)
                xk = ("x", mo, tg)
                ph.op("dve", lambda e, pt=pt, mo=mo, xsl=xsl: e.scalar_tensor_tensor(
                    out=xT[:, mo, xsl], in0=pt[:], scalar=prm[:, l, i_gt * 8 + mo:i_gt * 8 + mo + 1], in1=xT[:, mo, xsl], op0=ALU.mult, op1=ALU.add),
                    reads=[pk, ("xld", tg)], writes=[xk])
                i2 = it % 4
                it += 1
                ph.op("act", lambda e, i2=i2, mo=mo, xsl=xsl: e.activation(out=tb[i2][:], in_=xT[:, mo, xsl], func=AF.Copy), reads=[xk], writes=[("tb", i2)])
                ph.op("act", lambda e, i2=i2, mo=mo, xsl=xsl: e.activation(out=tq[i2][:], in_=xT[:, mo, xsl], func=AF.Square), reads=[xk], writes=[("tq", i2)])
                def stats(i2=i2, mo=mo, s1p=s1p, s2p=s2p, ci=ci):
                    ph.op("pe", lambda e: e.matmul(s1p[:], lhsT=ones_b[:], rhs=tb[i2][:], start=(mo == 0), stop=(mo == 7)),
                          reads=[("tb", i2)], writes=[("s1p", ci)])
                    ph.op("pe", lambda e: e.matmul(s2p[:], lhsT=ones_b[:], rhs=tq[i2][:], start=(mo == 0), stop=(mo == 7)),
                          reads=[("tq", i2)], writes=[("s2p", ci)])
                pendS.append(stats)
                while len(pendS) > 2:
                    pendS.pop(0)()
        while pendS:
            pendS.pop(0)()
        for ci in range(conc):
            tg = base + ci
            xsl = slice(tg * 512, (tg + 1) * 512)
            s1p, s2p = pbank[nyb + 2 * ci], pbank[nyb + 2 * ci + 1]
            a, c = st1[ci], st3[ci]
            ph.op("dve", lambda e, a=a, s1p=s1p: e.tensor_scalar(out=a[:], in0=s1p[:], scalar1=1.0 / D, scalar2=None, op0=ALU.mult),
                  reads=[("s1p", ci)], writes=[("mean", ci)])
            ph.op("dve", lambda e, a=a: e.tensor_tensor(out=st2[:], in0=a[:], in1=a[:], op=ALU.mult), reads=[("mean", ci)], writes=["msq"])
            ph.op("dve", lambda e, s2p=s2p: e.scalar_tensor_tensor(out=st2[:], in0=s2p[:], scalar=1.0 / D, in1=st2[:], op0=ALU.mult, op1=ALU.subtract),
                  reads=[("s2p", ci), "msq"], writes=["var"])
            ph.op("dve", lambda e: e.tensor_scalar(out=st2[:], in0=st2[:], scalar1=0.0, scalar2=EPS, op0=ALU.max, op1=ALU.add), reads=["var"], writes=["var2"])
            ph.op("act", lambda e: e.activation(out=st2[:], in_=st2[:], func=AF.Sqrt), reads=["var2"], writes=["sd"])
            ph.op("dve", lambda e, c=c: e.reciprocal(out=c[:], in_=st2[:]), reads=["sd"], writes=[("rstd", ci)])
            for mo in range(8):
                xk = ("x", mo, tg)
                ph.op("dve", lambda e, mo=mo, xsl=xsl, a=a: e.tensor_tensor(out=xT[:, mo, xsl], in0=xT[:, mo, xsl], in1=a[:], op=ALU.subtract),
                      reads=[xk, ("mean", ci)], writes=[xk])
                ph.op("dve", lambda e, mo=mo, xsl=xsl, c=c: e.tensor_tensor(out=xT[:, mo, xsl], in0=xT[:, mo, xsl], in1=c[:], op=ALU.mult),
                      reads=[xk, ("rstd", ci)], writes=[xk])
                ph.op("act", lambda e, mo=mo, xsl=xsl: e.activation(
                    out=xT[:, mo, xsl], in_=xT[:, mo, xsl], func=AF.Identity, scale=prm[:, l, i_g * 8 + mo:i_g * 8 + mo + 1],
                    bias=prm[:, l, i_b * 8 + mo:i_b * 8 + mo + 1]), reads=[xk], writes=[xk])


def _cols(v):
    n = v.shape[-1] // 128
    return np.swapaxes(v.reshape(v.shape[:-1] + (n, 128)), -1, -2)


def _blk(w, bw):
    Lw, K, N = w.shape
    return np.ascontiguousarray(w.reshape(Lw, K // 128, 128, N // bw, bw).transpose(0, 3, 2, 1, 4)).reshape(Lw, N // bw, 128, (K // 128) * bw)


def prep_shared(inp):
    f = lambda a: np.asarray(a, dtype=np.float32)
    pvec = np.concatenate([_cols(f(inp["ln_mix_g"])), _cols(f(inp["ln_mix_b"])), _cols(f(inp["ln_ffn_g"])), _cols(f(inp["ln_ffn_b"])),
                           _cols(f(inp["b_ada"])), _cols(f(inp["conv_dw_b"])), _cols(f(inp["conv_ln_g"])), _cols(f(inp["conv_ln_b"]))], axis=-1)
    assert pvec.shape == (L, 128, 92), pvec.shape
    dwt = f(inp["conv_dw_w"])
    dww = np.ascontiguousarray(dwt.reshape(L, KC, 4, 128).transpose(0, 3, 2, 1)).reshape(L, 128, 4 * KC)
    wbr = f(inp["w_attn_br"]).reshape(L, H, 64, 2, 512).transpose(0, 3, 2, 1, 4)
    wbr = np.ascontiguousarray(wbr).reshape(L, 2, 64, 8 * 512)
    return {
        "pvec": np.ascontiguousarray(pvec), "dww": dww,
        "wada": _blk(f(inp["w_ada"]), 512), "win": _blk(f(inp["w_in"]), 512), "wbr": wbr,
        "wpw": _blk(f(inp["w_conv_pw"]), 512), "wout": _blk(f(inp["w_out"]), 512),
        "wup": _blk(f(inp["w_up"]), 512), "wdn": _blk(f(inp["w_down"]), 128),
    }


NPV = 92
_NC_CACHE = {}


def kernel(**inputs):
    shared = prep_shared(inputs)
    x = np.asarray(inputs["x"], dtype=np.float32)
    c = np.asarray(inputs["c"], dtype=np.float32)
    pos = np.asarray(inputs["positions"], dtype=np.int32)
    if "nc" not in _NC_CACHE:
        _NC_CACHE["nc"] = build(L)
    nc = _NC_CACHE["nc"]
    work = {0: 0, 1: 1, 4: 2, 5: 3}
    zeros = {k: np.zeros_like(v) for k, v in shared.items()}
    zeros["x"] = np.zeros((S, D), np.float32)
    zeros["cT"] = np.zeros((128, 8), np.float32)
    zeros["pos"] = np.zeros((128, 32), np.int32)
    in_maps = []
    for core in range(8):
        if core not in work:
            in_maps.append(zeros)
            continue
        b = work[core]
        m = dict(shared)
        m["x"] = np.ascontiguousarray(x[b])
        m["cT"] = np.ascontiguousarray(c[b].reshape(8, 128).T)
        m["pos"] = np.ascontiguousarray(pos[b].reshape(32, 128).T)
        in_maps.append(m)
    res = run_bass_kernel_spmd(nc, in_maps, core_ids=list(range(8)))
    inv = {b: core for core, b in work.items()}
    return np.stack([res.results[inv[b]]["out"] for b in range(B)], axis=0).astype(np.float32)
```
